# Optimizing a Trainium2 kernel written in Bass

```python
import functools
import jax, jax.numpy as jnp
from jax import lax
import numpy as np

D_MODEL = 1024
BATCH = 4
SEQ = 4096
DEPTH = 1
DEC_BATCH = 32
DEC_SEQ = 1
PAST_LEN = 8192
PAGE_SIZE = 128

D_RNN = D_MODEL
N_BLK = 16
BLK_W = D_RNN // N_BLK
CONV_W = 4
RG_C = 8.0
N_HEADS = 16
HEAD_DIM = D_MODEL // N_HEADS
ATTN_W = N_HEADS * HEAD_DIM
Q_BLK = 128
D_FF = ((8 * D_MODEL + 3 * 256 - 1) // (3 * 256)) * 256
N_IN = 2 * D_RNN + 3 * ATTN_W + N_HEADS + 2 * D_MODEL
EPS = 1e-6

kernel_name = "hawk_fox_adaln_decode_step"


def rms_norm(x, g):
    xf = x.astype(jnp.float32)
    y = xf * lax.rsqrt(jnp.mean(xf * xf, axis=-1, keepdims=True) + EPS)
    return (y * g.astype(jnp.float32)).astype(x.dtype)


def ada_modulation(c, ada_w, ada_b):
    m = (jax.nn.silu(c) @ ada_w + ada_b)[:, None, :]
    return jnp.split(m, 6, axis=-1)


def causal_dwconv(x, buf, w, b):
    xp = jnp.concatenate([buf.astype(x.dtype), x], axis=1)
    T = x.shape[1]
    y = b
    for j in range(CONV_W):
        y = y + xp[:, j:j + T] * w[j]
    return y, xp[:, T:]


def rg_lru(x, h0, pos0, wr, br, wi, bi, lam):
    B, T, _ = x.shape
    f32 = jnp.float32
    xf = x.astype(f32)
    xb = xf.reshape(B, T, N_BLK, BLK_W)
    gate_r = jax.nn.sigmoid(jnp.einsum("btnc,ncd->btnd", xb, wr.astype(f32)).reshape(B, T, D_RNN) + br.astype(f32))
    gate_i = jax.nn.sigmoid(jnp.einsum("btnc,ncd->btnd", xb, wi.astype(f32)).reshape(B, T, D_RNN) + bi.astype(f32))
    log_a = -RG_C * gate_r * jax.nn.softplus(-lam.astype(f32))
    a = jnp.exp(log_a)
    mult = jnp.sqrt(-jnp.expm1(2.0 * log_a))
    pos = pos0 + jnp.arange(T)
    mult = jnp.where((pos == 0)[None, :, None], 1.0, mult)
    u = mult * gate_i * xf

    def combine(lhs, rhs):
        a1, u1 = lhs
        a2, u2 = rhs
        return a1 * a2, a2 * u1 + u2

    a_cum, u_cum = lax.associative_scan(combine, (a, u), axis=1)
    h = a_cum * h0.astype(f32)[:, None, :] + u_cum
    return h, h[:, -1]


def fox_prompt(q, k, v, logf):
    B, S = q.shape[:2]
    n_blk = S // Q_BLK
    scale = HEAD_DIM ** -0.5
    Fk = jnp.transpose(jnp.cumsum(logf, axis=1), (0, 2, 1))
    kpos = jnp.arange(S)
    qb = jnp.moveaxis(q.reshape(B, n_blk, Q_BLK, N_HEADS, HEAD_DIM), 1, 0)
    Fq = jnp.moveaxis(Fk.reshape(B, N_HEADS, n_blk, Q_BLK), 2, 0)

    def one_block(args):
        q_i, fq_i, blk = args
        s = jnp.einsum("bqhd,bkhd->bhqk", q_i, k, preferred_element_type=jnp.float32) * scale
        s = s + fq_i[..., None] - Fk[:, :, None, :]
        qpos = blk * Q_BLK + jnp.arange(Q_BLK)
        s = jnp.where(kpos[None, :] <= qpos[:, None], s, -jnp.inf)
        p = jax.nn.softmax(s, axis=-1)
        return jnp.einsum("bhqk,bkhd->bqhd", p.astype(v.dtype), v)

    o = lax.map(one_block, (qb, Fq, jnp.arange(n_blk)))
    return jnp.moveaxis(o, 0, 1).reshape(B, S, N_HEADS, HEAD_DIM)


def fox_sample(q, k, v, logf, cache_k, cache_v, cache_logf, page_table):
    DB, T = q.shape[:2]
    P = page_table.shape[1] * PAGE_SIZE
    scale = HEAD_DIM ** -0.5
    k_all = jnp.concatenate([cache_k[page_table].reshape(DB, P, N_HEADS, HEAD_DIM).astype(k.dtype), k], axis=1)
    v_all = jnp.concatenate([cache_v[page_table].reshape(DB, P, N_HEADS, HEAD_DIM).astype(v.dtype), v], axis=1)
    lf_all = jnp.concatenate([cache_logf[page_table].reshape(DB, P, N_HEADS).astype(jnp.float32), logf], axis=1)
    Fk = jnp.transpose(jnp.cumsum(lf_all, axis=1), (0, 2, 1))
    s = jnp.einsum("bqhd,bkhd->bhqk", q, k_all, preferred_element_type=jnp.float32) * scale
    s = s + Fk[:, :, P:, None] - Fk[:, :, None, :]
    kpos = jnp.arange(P + T)
    qpos = P + jnp.arange(T)
    s = jnp.where(kpos[None, :] <= qpos[:, None], s, -jnp.inf)
    p = jax.nn.softmax(s, axis=-1)
    return jnp.einsum("bhqk,bkhd->bqhd", p.astype(v_all.dtype), v_all)


def trunk_layer(x, c, conv_buf, h0, pos0, attend, ada_w, ada_b, norm1_g, norm2_g, w_in, b_f,
                conv_w, conv_b, rg_wr, rg_br, rg_wi, rg_bi, rg_lambda, w_proj_a, w_proj_b, w_o,
                w_ffn_in, w_ffn_out):
    B, T, _ = x.shape
    sh1, sc1, g1, sh2, sc2, g2 = ada_modulation(c, ada_w, ada_b)
    h = rms_norm(x, norm1_g) * (1.0 + sc1) + sh1
    z = h @ w_in
    cuts = list(np.cumsum([D_RNN, D_RNN, ATTN_W, ATTN_W, ATTN_W, N_HEADS, D_MODEL]))
    xr, gr, q, k, v, fl, ga, gb = jnp.split(z, cuts, axis=-1)
    xc, conv_new = causal_dwconv(xr, conv_buf, conv_w, conv_b)
    hr, h_last = rg_lru(xc, h0, pos0, rg_wr, rg_br, rg_wi, rg_bi, rg_lambda)
    ya = (hr.astype(x.dtype) * jax.nn.gelu(gr)) @ w_proj_a
    q = q.reshape(B, T, N_HEADS, HEAD_DIM)
    k = k.reshape(B, T, N_HEADS, HEAD_DIM)
    v = v.reshape(B, T, N_HEADS, HEAD_DIM)
    logf = jax.nn.log_sigmoid((fl + b_f).astype(jnp.float32))
    yb = attend(q, k, v, logf).reshape(B, T, ATTN_W) @ w_proj_b
    mixed = (jax.nn.sigmoid(ga) * ya + jax.nn.sigmoid(gb) * yb) @ w_o
    x = x + g1 * mixed
    h2 = rms_norm(x, norm2_g) * (1.0 + sc2) + sh2
    gu, up = jnp.split(h2 @ w_ffn_in, 2, axis=-1)
    x = x + g2 * ((jax.nn.silu(gu) * up) @ w_ffn_out)
    return x, k, v, logf, conv_new, h_last


def setup_inputs(seed: int = 0) -> dict:
    key = jax.random.key(seed)
    ks = jax.random.split(key, 32)
    f32 = jnp.float32
    n_pages = PAST_LEN // PAGE_SIZE
    n_pool = (DEC_BATCH * n_pages * 5) // 4

    def nrm(k, shape, scale):
        return jax.random.normal(k, shape, f32) * scale

    page_table = jax.random.permutation(ks[0], n_pool)[:DEC_BATCH * n_pages].reshape(DEC_BATCH, n_pages).astype(jnp.int32)
    u = jax.random.uniform(ks[1], (DEPTH, D_RNN), f32, 0.9, 0.999)
    s = u ** (1.0 / RG_C)
    rg_lambda = jnp.log(s) - jnp.log1p(-s)
    b_f = jnp.linspace(1.0, 4.0, N_HEADS, dtype=f32)[None, :] + nrm(ks[2], (DEPTH, N_HEADS), 0.1)
    return {
        "x_prompt": nrm(ks[3], (BATCH, SEQ, D_MODEL), 1.0),
        "x_sample": nrm(ks[4], (DEC_BATCH, DEC_SEQ, D_MODEL), 1.0),
        "c_prompt": nrm(ks[5], (BATCH, D_MODEL), 1.0),
        "c_sample": nrm(ks[6], (DEC_BATCH, D_MODEL), 1.0),
        "cache_k": nrm(ks[7], (DEPTH, n_pool, PAGE_SIZE, N_HEADS, HEAD_DIM), 1.0),
        "cache_v": nrm(ks[8], (DEPTH, n_pool, PAGE_SIZE, N_HEADS, HEAD_DIM), 1.0),
        "cache_logf": jax.nn.log_sigmoid(3.0 + nrm(ks[9], (DEPTH, n_pool, PAGE_SIZE, N_HEADS), 0.5)),
        "state_conv": nrm(ks[10], (DEPTH, DEC_BATCH, CONV_W - 1, D_RNN), 1.0),
        "state_rglru": nrm(ks[11], (DEPTH, DEC_BATCH, D_RNN), 0.5),
        "page_table": page_table,
        "ada_w": nrm(ks[12], (DEPTH, D_MODEL, 6 * D_MODEL), 0.5 * D_MODEL ** -0.5),
        "ada_b": nrm(ks[13], (DEPTH, 6 * D_MODEL), 0.02),
        "norm1_g": 1.0 + nrm(ks[14], (DEPTH, D_MODEL), 0.05),
        "norm2_g": 1.0 + nrm(ks[15], (DEPTH, D_MODEL), 0.05),
        "normf_g": 1.0 + nrm(ks[16], (D_MODEL,), 0.05),
        "w_in": nrm(ks[17], (DEPTH, D_MODEL, N_IN), D_MODEL ** -0.5),
        "b_f": b_f,
        "conv_w": nrm(ks[18], (DEPTH, CONV_W, D_RNN), CONV_W ** -0.5),
        "conv_b": nrm(ks[19], (DEPTH, D_RNN), 0.02),
        "rg_wr": nrm(ks[20], (DEPTH, N_BLK, BLK_W, BLK_W), BLK_W ** -0.5),
        "rg_br": nrm(ks[21], (DEPTH, D_RNN), 0.02),
        "rg_wi": nrm(ks[22], (DEPTH, N_BLK, BLK_W, BLK_W), BLK_W ** -0.5),
        "rg_bi": nrm(ks[23], (DEPTH, D_RNN), 0.02),
        "rg_lambda": rg_lambda,
        "w_proj_a": nrm(ks[24], (DEPTH, D_RNN, D_MODEL), D_RNN ** -0.5),
        "w_proj_b": nrm(ks[25], (DEPTH, ATTN_W, D_MODEL), ATTN_W ** -0.5),
        "w_o": nrm(ks[26], (DEPTH, D_MODEL, D_MODEL), D_MODEL ** -0.5),
        "w_ffn_in": nrm(ks[27], (DEPTH, D_MODEL, 2 * D_FF), D_MODEL ** -0.5),
        "w_ffn_out": nrm(ks[28], (DEPTH, D_FF, D_MODEL), D_FF ** -0.5),
    }


def reference(x_prompt, x_sample, c_prompt, c_sample, cache_k, cache_v, cache_logf, state_conv,
              state_rglru, page_table, ada_w, ada_b, norm1_g, norm2_g, normf_g, w_in, b_f, conv_w,
              conv_b, rg_wr, rg_br, rg_wi, rg_bi, rg_lambda, w_proj_a, w_proj_b, w_o, w_ffn_in,
              w_ffn_out):
    B = x_prompt.shape[0]
    past_len = page_table.shape[1] * PAGE_SIZE
    zero_buf = jnp.zeros((B, CONV_W - 1, D_RNN), x_prompt.dtype)
    zero_h = jnp.zeros((B, D_RNN), jnp.float32)
    y_p, y_s = x_prompt, x_sample
    k_p, v_p, lf_p, cv_p, h_p = [], [], [], [], []
    k_s, v_s, lf_s, cv_s, h_s = [], [], [], [], []
    for l in range(DEPTH):
        lw = (ada_w[l], ada_b[l], norm1_g[l], norm2_g[l], w_in[l], b_f[l], conv_w[l], conv_b[l],
              rg_wr[l], rg_br[l], rg_wi[l], rg_bi[l], rg_lambda[l], w_proj_a[l], w_proj_b[l],
              w_o[l], w_ffn_in[l], w_ffn_out[l])
        y_p, kk, vv, lf, cb, hl = trunk_layer(y_p, c_prompt, zero_buf, zero_h, 0, fox_prompt, *lw)
        k_p.append(kk); v_p.append(vv); lf_p.append(lf); cv_p.append(cb); h_p.append(hl)
        attend = functools.partial(fox_sample, cache_k=cache_k[l], cache_v=cache_v[l],
                                   cache_logf=cache_logf[l], page_table=page_table)
        y_s, kk, vv, lf, cb, hl = trunk_layer(y_s, c_sample, state_conv[l], state_rglru[l],
                                              past_len, attend, *lw)
        k_s.append(kk); v_s.append(vv); lf_s.append(lf); cv_s.append(cb); h_s.append(hl)
    y_prompt = rms_norm(y_p, normf_g)
    y_sample = rms_norm(y_s, normf_g)
    return (y_prompt, y_sample,
            jnp.stack(k_p), jnp.stack(v_p), jnp.stack(lf_p), jnp.stack(cv_p), jnp.stack(h_p),
            jnp.stack(k_s), jnp.stack(v_s), jnp.stack(lf_s), jnp.stack(cv_s), jnp.stack(h_s))
```

```python
import numpy as np
from contextlib import ExitStack
import concourse.bass as bass
import concourse.mybir as mybir
from concourse.bass_utils import run_bass_kernel_spmd

F32 = mybir.dt.float32
BF16 = mybir.dt.bfloat16
I32 = mybir.dt.int32
AF = mybir.ActivationFunctionType
ALU = mybir.AluOpType
AX = mybir.AxisListType

NEG = -30000.0
NRING = 5
SAME_SYNC = True


class Cfg:
    def __init__(self, s_own=2048, npg=64, npool=2560, debug=()):
        self.D = 1024
        self.KC = 8
        self.H = 16
        self.DH = 64
        self.DFF = 2816
        self.NIN = 7184
        self.S_OWN = s_own
        self.S_PRE = s_own
        self.S_ALL = 2 * s_own
        self.NS = 4
        self.NPG = npg
        self.NPOOL = npool
        self.NB_ALL = self.S_ALL // 128
        self.NB_PRE = self.S_PRE // 128
        self.NB_OWN = self.S_OWN // 128
        self.NT_ALL = self.S_ALL // 512
        self.NT_PRE = self.S_PRE // 512
        self.NT_OWN = self.S_OWN // 512
        self.debug = tuple(debug)
        self.stop = 99


XR, GR, QO, KO, VO, FLO, GA, GB = 0, 1024, 2048, 3072, 4096, 5120, 5136, 6160

C_ID = 0
C_U = 128
C_ONE = 256
C_BM = 384
C_SU = 384
C_SEL32 = C_SU + 128
C_ONESEL = C_SEL32 + 128
C_SELQ = C_ONESEL + 16
C_BMK = C_SELQ + 512
C_E = C_BMK + 1024
NCP = C_E + 16
C_BM_SRC = NCP
NCONST = NCP + 2048


def make_consts():
    c = np.zeros((128, NCONST), np.float32)
    c[:, C_ID:C_ID + 128] = np.eye(128)
    j = np.arange(128)[:, None]
    t = np.arange(128)[None, :]
    c[:, C_U:C_U + 128] = (j <= t)
    c[:, C_ONE:C_ONE + 128] = 1.0
    kt = np.arange(128)[:, None]
    qt = np.arange(512)[None, :]
    for jj in range(4):
        c[:, C_BM_SRC + 512 * jj:C_BM_SRC + 512 * (jj + 1)] = np.where(qt - 128 * jj - kt >= 0, 0.0, NEG)
    c[:, C_SU:C_SU + 128] = (j > t)
    c[32, C_SEL32:C_SEL32 + 128] = 1.0
    for b in range(4):
        c[:, C_ONESEL + 4 * b + b] = 1.0
        c[b, C_SELQ + 128 * b:C_SELQ + 128 * (b + 1)] = 1.0
        c[:16, C_E + 4 * b + b] = 1.0
    for h in range(16):
        c[h, C_BMK + 64 * h:C_BMK + 64 * (h + 1)] = 1.0
    return c


class Prog:
    def __init__(self, nc):
        self.nc = nc
        self.eng = {"pe": nc.tensor, "act": nc.scalar, "dve": nc.vector, "pool": nc.gpsimd, "sp": nc.sync}
        self.pool = [nc.alloc_semaphore("ks%d" % i) for i in range(96)]
        self.allsems = list(self.pool)
        for sm in self.pool:
            nc.gpsimd.sem_clear(sm)
        nc.all_engine_barrier()
        self.sem = {e: self.pool.pop() for e in ("pe", "act", "dve", "pool")}
        self.cnt = {e: 0 for e in ("pe", "act", "dve", "pool")}
        self.known = {e: {} for e in self.eng}
        self.last_w = {}
        self.readers = {}
        self.lanes = {}
        self.out_lanes = set()
        self.nops = 0
        self.fence_need = {}
        self.rec = None

    def _sem_of(self, key):
        if key in self.sem:
            return self.sem[key]
        return self.lanes[key][0]

    def fence(self):
        for k, v in self.cnt.items():
            if v:
                self.fence_need[k] = v
        for lk, (sm, v) in self.lanes.items():
            if v:
                self.fence_need[lk] = v

    def _deps(self, engine, reads, writes):
        need = {k: v for k, v in self.fence_need.items() if k != engine}

        def add(tok, kind):
            if tok is None:
                return
            sk, val = tok
            if sk == engine:
                if engine == "pe":
                    return
                if kind == "war" or not SAME_SYNC:
                    return
            if need.get(sk, 0) < val:
                need[sk] = val

        for r in reads:
            add(self.last_w.get(r), "raw")
        for w in writes:
            add(self.last_w.get(w), "waw")
            for tok in self.readers.get(w, ()):
                add(tok, "war")
        e = self.eng[engine]
        kn = self.known[engine]
        for sk, val in need.items():
            if kn.get(sk, 0) >= val:
                continue
            e.wait_ge(self._sem_of(sk), val)
            kn[sk] = val

    def _commit(self, tok, reads, writes):
        for w in writes:
            self.last_w[w] = tok
            self.readers[w] = []
        for r in reads:
            if r in writes:
                continue
            self.readers.setdefault(r, []).append(tok)

    def op(self, engine, fn, reads=(), writes=()):
        if self.rec is not None:
            self.rec.append(("op", (engine, fn, tuple(reads), tuple(writes)), {}))
            return
        self._deps(engine, reads, writes)
        ins = fn(self.eng[engine])
        self.cnt[engine] += 1
        ins.then_inc(self.sem[engine], 1)
        self._commit((engine, self.cnt[engine]), reads, writes)
        self.nops += 1

    def dma(self, q, out, in_, reads=(), writes=(), lane=None, final=False, indirect=None):
        if self.rec is not None:
            self.rec.append(("dma", (q, out, in_), dict(reads=tuple(reads), writes=tuple(writes), lane=lane, final=final, indirect=indirect)))
            return
        if lane is None:
            lane = writes[0] if writes else reads[0]
        lk = ("dma", lane)
        if lk not in self.lanes:
            self.lanes[lk] = [self.pool.pop(), 0]
        self._deps(q, reads, writes)
        e = self.eng[q]
        if indirect is not None:
            ins = e.indirect_dma_start(out=out, out_offset=None, in_=in_, in_offset=indirect)
        else:
            ins = e.dma_start(out=out, in_=in_)
        self.lanes[lk][1] += 16
        ins.then_inc(self.lanes[lk][0], 16)
        self._commit((lk, self.lanes[lk][1]), reads, writes)
        if final:
            self.out_lanes.add(lk)
        self.nops += 1

    def replay_interleaved(self, lists, lag=0):
        pos = [0] * len(lists)
        live = True
        step = 0
        while live:
            live = False
            step += 1
            for i, l in enumerate(lists):
                if i > 0 and step <= lag * i and pos[0] < len(lists[0]):
                    live = True
                    continue
                if pos[i] < len(l):
                    kind, a, kw = l[pos[i]]
                    pos[i] += 1
                    live = True
                    if kind == "op":
                        self.op(*a)
                    else:
                        self.dma(*a, **kw)

    def finish(self):
        e = self.eng["sp"]
        for lk in sorted(self.out_lanes, key=str):
            e.wait_ge(self.lanes[lk][0], self.lanes[lk][1])
        for k in ("pe", "act", "dve", "pool"):
            if self.cnt[k]:
                e.wait_ge(self.sem[k], self.cnt[k])
        self.nc.all_engine_barrier()
        for sm in self.allsems:
            self.nc.gpsimd.sem_clear(sm)
        self.nc.all_engine_barrier()


def build_program(cfg):
    nc = bass.Bass("TRN2", target_bir_lowering=False)
    P = Prog(nc)
    D, KC, H, DH = cfg.D, cfg.KC, cfg.H, cfg.DH
    S_ALL, S_OWN, S_PRE = cfg.S_ALL, cfg.S_OWN, cfg.S_PRE
    NB_ALL, NB_PRE, NB_OWN = cfg.NB_ALL, cfg.NB_PRE, cfg.NB_OWN
    NT_ALL, NT_PRE, NT_OWN = cfg.NT_ALL, cfg.NT_PRE, cfg.NT_OWN
    NS, NPG = cfg.NS, cfg.NPG
    NPGS = NS * NPG

    def din(name, shape, dt=F32):
        return nc.dram_tensor(name, list(shape), dt, kind="ExternalInput").ap()

    def dout(name, shape, dt=F32):
        return nc.dram_tensor(name, list(shape), dt, kind="ExternalOutput").ap()

    xa = din("xa", [S_ALL, D])
    flags = din("flags", [128, 2])
    crow = din("crow", [33, D])
    xs_d = din("xs", [NS, D])
    sconv_d = din("sconv", [NS, 3, D])
    srg_d = din("srg", [NS, D])
    pt_d = din("pt", [1, NPGS], I32)
    if cfg.stop >= 5:
        ck_d = din("ck", [cfg.NPOOL * 128, D])
        cv_d = din("cv", [cfg.NPOOL * 128, D])
        clf_d = din("clf", [cfg.NPOOL, 128 * H])
    consts_d = din("consts", [128, NCONST])
    ada_w = din("ada_w", [D, 6 * D])
    ada_b = din("ada_b", [1, 6 * D])
    vecs_d = din("vecs", [128, 128])
    normf_d = din("normf_g", [1, D])
    w_in = din("w_in", [D, cfg.NIN])
    b_f = din("b_f", [1, H])
    rg_wr = din("rg_wr", [16, 64, 64])
    rg_wi = din("rg_wi", [16, 64, 64])
    w_pa = din("w_proj_a", [D, D])
    w_pb = din("w_proj_b", [D, D])
    w_o = din("w_o", [D, D])
    w_fi = din("w_ffn_in", [D, 2 * cfg.DFF])
    w_fo = din("w_ffn_out", [cfg.DFF, D])

    y_o = dout("y", [S_OWN, D])
    k_o = dout("ko", [S_OWN, D])
    v_o = dout("vo", [S_OWN, D])
    lf_o = dout("lfo", [S_OWN, H])
    conv_o = dout("convo", [3, D])
    rg_o = dout("rgo", [1, D])
    ys_o = dout("yso", [NS, D])
    ks_o = dout("kso", [NS, D])
    vs_o = dout("vso", [NS, D])
    lfs_o = dout("lfso", [NS, H])
    convs_o = dout("convso", [NS, 3, D])
    rgs_o = dout("rgso", [NS, D])
    dbg_outs = {}

    es = ExitStack()

    def sb(name, shape, dt=F32, stack=None):
        return (stack or es).enter_context(nc.sbuf_tensor("s_" + name, list(shape), dt))

    ps = [es.enter_context(nc.psum_tensor("ps%d" % i, [128, 512], F32)) for i in range(8)]
    PK = [("ps", i) for i in range(8)]

    def wview(w, c0, ncols):
        return w[:, c0:c0 + ncols].rearrange("(k p) n -> p k n", p=128)

    def dbg(name, ap, key):
        if name not in cfg.debug:
            return
        shape = list(ap.shape)
        t = nc.dram_tensor("dbg_" + name, shape, ap.dtype, kind="ExternalOutput").ap()
        dbg_outs[name] = t
        P.dma("sp", t, ap, reads=list(key) if isinstance(key, list) else [key], lane=("dbg", name), final=True)

    consts = sb("consts", [128, NCP])
    P.dma("sp", consts[:], consts_d[:, 0:NCP], writes=["consts"])
    ident_f = consts[:, C_ID:C_ID + 128]
    ones_f = consts[:, C_ONE:C_ONE + 128]
    cbf = sb("cbf", [128, 384 + 2048], BF16)
    P.op("dve", lambda e: e.tensor_copy(out=cbf[:, 0:384], in_=consts[:, 0:384]), reads=["consts"], writes=["cbf"])
    ident_b = cbf[:, C_ID:C_ID + 128]
    flg = sb("flg", [128, 2])
    P.dma("sp", flg[:], flags, writes=["flg"])
    vf = flg[:, 0:1]
    pm = flg[:, 1:2]
    vecs_tm = sb("vecs_tm", [128, 128])
    P.dma("sp", vecs_tm[:], vecs_d, writes=["vecs_tm"])
    vfm = sb("vfm", [128, 128])
    P.op("pe", lambda e: e.transpose(ps[7][:, 0:128], vecs_tm[:], ident_f), reads=["vecs_tm", "consts"], writes=[PK[7]])
    P.op("dve", lambda e: e.tensor_copy(out=vfm[:], in_=ps[7][:, 0:128]), reads=[PK[7]], writes=["vfm"])
    V_N1, V_N2, V_CW, V_CB, V_BR, V_BI, V_LAM = 48, 56, 64, 96, 104, 112, 120
    normf_b = sb("normf_b", [128, D])
    P.dma("sp", normf_b[:], normf_d.partition_broadcast(128), writes=["normf_b"])
    bfb = sb("bfb", [128, H])
    P.dma("sp", bfb[:], b_f.partition_broadcast(128), writes=["bfb"])
    g1b = sb("g1b", [128, D])
    g2b = sb("g2b", [128, D])
    g1s = sb("g1s", [NS, D])
    g2s = sb("g2s", [NS, D])
    AB = sb("AB", [128, 4, KC])
    ABs = sb("ABs", [128, 4, KC, NS])
    cl = sb("cl", [128, 2, KC])
    hsT = sb("hsT", [128, KC, NS], BF16)

    with ExitStack() as s0:
        m_fm = sb("m_fm", [128, 48, 33], stack=s0)
        mtmp = sb("mtmp", [128, 2048], stack=s0)
        P.dma("sp", mtmp[:], consts_d[:, C_BM_SRC:C_BM_SRC + 2048], writes=["mtmp"])
        P.op("dve", lambda e: e.tensor_copy(out=cbf[:, C_BM:C_BM + 2048], in_=mtmp[:]), reads=["mtmp"], writes=["cbf"])
        c_tm = sb("c_tm", [33, D], stack=s0)
        P.dma("sp", c_tm[:], crow, writes=["c_tm"])
        P.op("act", lambda e: e.activation(out=c_tm[:], in_=c_tm[:], func=AF.Silu), reads=["c_tm"], writes=["c_tm"])
        silT = sb("silT", [128, KC, 33], BF16, stack=s0)
        for kc in range(KC):
            P.op("pe", lambda e, kc=kc: e.transpose(ps[7][:, kc * 33:(kc + 1) * 33], c_tm[:, kc * 128:(kc + 1) * 128], ident_f[0:33, 0:33]),
                 reads=["c_tm", "consts"], writes=[PK[7]])
        P.op("dve", lambda e: e.tensor_copy(out=silT[:].rearrange("p k n -> p (k n)"), in_=ps[7][:, 0:KC * 33]), reads=[PK[7]], writes=["silT"])
        adab = sb("adab", [1, 6 * D], stack=s0)
        P.dma("sp", adab[:], ada_b, writes=["adab"])
        m_tm = sb("m_tm", [33, 6 * D], stack=s0)
        awb = [sb("awb%d" % i, [128, KC, 512], BF16, stack=s0) for i in range(2)]
        for j in range(12):
            wk = ("awb", j % 2)
            P.dma("pool", awb[j % 2][:], wview(ada_w, j * 512, 512), writes=[wk])
            pk = PK[j % 2]

            def mm(e, j=j):
                for kc in range(KC):
                    e.matmul(ps[j % 2][0:33, :], lhsT=silT[:, kc, :], rhs=awb[j % 2][:, kc, :], start=(kc == 0), stop=False)
                return e.matmul(ps[j % 2][0:33, :], lhsT=ones_f[0:1, 0:33], rhs=adab[0:1, j * 512:(j + 1) * 512], start=False, stop=True)
            P.op("pe", mm, reads=["silT", wk, "adab", "consts"], writes=[pk])
            P.op("act", lambda e, j=j: e.activation(out=m_tm[:, j * 512:(j + 1) * 512], in_=ps[j % 2][0:33, :], func=AF.Copy),
                 reads=[pk], writes=["m_tm"])
        for g in range(4):
            for jj in range(12):
                j = g * 12 + jj
                P.op("pe", lambda e, j=j, jj=jj, g=g: e.transpose(ps[g][:, jj * 33:(jj + 1) * 33], m_tm[:, j * 128:(j + 1) * 128], ident_f[0:33, 0:33]),
                     reads=["m_tm", "consts"], writes=[PK[g]])
            P.op("dve", lambda e, g=g: e.tensor_copy(out=m_fm[:, g * 12:(g + 1) * 12, :].rearrange("p a b -> p (a b)"), in_=ps[g][:, 0:12 * 33]),
                 reads=[PK[g]], writes=["m_fm"])
        for gi, (gt, c0) in enumerate(((g1b, 2 * D), (g2b, 5 * D))):
            for hf in range(2):
                pk = PK[4 + hf]
                P.op("pe", lambda e, c0=c0, hf=hf: e.matmul(ps[4 + hf][:, :], lhsT=consts[0:33, C_SEL32:C_SEL32 + 128],
                                                            rhs=m_tm[:, c0 + hf * 512:c0 + (hf + 1) * 512], start=True, stop=True),
                     reads=["m_tm", "consts"], writes=[pk])
                P.op("dve", lambda e, gt=gt, hf=hf: e.tensor_copy(out=gt[:, hf * 512:(hf + 1) * 512], in_=ps[4 + hf][:, :]),
                     reads=[pk], writes=["g%db" % (gi + 1)])
        P.op("dve", lambda e: e.tensor_copy(out=g1s[:], in_=m_tm[0:NS, 2 * D:3 * D]), reads=["m_tm"], writes=["g1s"])
        P.op("dve", lambda e: e.tensor_copy(out=g2s[:], in_=m_tm[0:NS, 5 * D:6 * D]), reads=["m_tm"], writes=["g2s"])
        for i, (sc_c, sh_c, vn) in enumerate(((8, 0, V_N1), (32, 24, V_N2))):
            P.op("dve", lambda e, i=i, sc_c=sc_c, vn=vn: e.scalar_tensor_tensor(
                out=AB[:, 2 * i, :], in0=m_fm[:, sc_c:sc_c + KC, 32], scalar=1.0, in1=vfm[:, vn:vn + KC], op0=ALU.add, op1=ALU.mult),
                reads=["m_fm", "vfm"], writes=["AB"])
            P.op("dve", lambda e, i=i, sh_c=sh_c: e.tensor_copy(out=AB[:, 2 * i + 1, :], in_=m_fm[:, sh_c:sh_c + KC, 32]),
                 reads=["m_fm"], writes=["AB"])
            P.op("dve", lambda e, i=i, sc_c=sc_c, vn=vn: e.scalar_tensor_tensor(
                out=ABs[:, 2 * i, :, :], in0=m_fm[:, sc_c:sc_c + KC, 0:NS], scalar=1.0,
                in1=vfm[:, vn:vn + KC].unsqueeze(2).to_broadcast([128, KC, NS]), op0=ALU.add, op1=ALU.mult),
                reads=["m_fm", "vfm"], writes=["ABs"])
            P.op("dve", lambda e, i=i, sh_c=sh_c: e.tensor_copy(out=ABs[:, 2 * i + 1, :, :], in_=m_fm[:, sh_c:sh_c + KC, 0:NS]),
                 reads=["m_fm"], writes=["ABs"])
    P.fence()
    P.op("act", lambda e: e.activation(out=cl[:, 0, :], in_=vfm[:, V_LAM:V_LAM + KC], func=AF.Exp, scale=-1.0), reads=["vfm"], writes=["cl"])
    P.op("act", lambda e: e.activation(out=cl[:, 0, :], in_=cl[:, 0, :], func=AF.Ln, bias=1.0), reads=["cl"], writes=["cl"])
    P.op("dve", lambda e: e.tensor_scalar(out=cl[:, 1, :], in0=cl[:, 0, :], scalar1=-16.0, scalar2=None, op0=ALU.mult), reads=["cl"], writes=["cl"])
    P.op("dve", lambda e: e.tensor_scalar(out=cl[:, 0, :], in0=cl[:, 0, :], scalar1=-8.0, scalar2=None, op0=ALU.mult), reads=["cl"], writes=["cl"])

    dbg("AB", AB[:], "AB")
    dbg("cl", cl[:], "cl")
    dbg("g1b", g1b[:], "g1b")
    if cfg.stop == 0:
        P.finish()
        return nc, P, es, dbg_outs

    lfs_tm = sb("lfs_tm", [NS, H])
    hnT = sb("hnT", [128, KC, NS])
    xrsT = sb("xrsT", [128, KC, NS])
    omvf = sb("omvf", [128, 1])
    mergedsT = sb("mergedsT", [128, KC, NS], BF16)
    oT_d = nc.dram_tensor("oT_scr", [128, KC, S_OWN], BF16, kind="Internal").ap()
    ya_d = nc.dram_tensor("ya_scr", [128, KC, S_OWN], BF16, kind="Internal").ap()
    mg_d = nc.dram_tensor("mg_scr", [128, KC, S_OWN], BF16, kind="Internal").ap()
    sH = ExitStack()
    es.callback(sH.close)
    hT = sb("hT", [128, KC, S_ALL], BF16, stack=sH)
    osT = sb("osT", [128, KC, NS], BF16, stack=sH)
    yainsT = sb("yainsT", [128, KC, NS], BF16, stack=sH)
    qs_tm = sb("qs_tm", [NS, D], stack=sH)
    ks_tm = sb("ks_tm", [NS, D], stack=sH)
    vs_tm = sb("vs_tm", [NS, D], stack=sH)
    os_tm = sb("os_tm", [NS, D], stack=sH)
    EPS = 1e-6

    def norm_block(stk, x_tile, xkey, nt, stat, statkey, xn, xnkey, junk, idx):
        P.op("act", lambda e: e.activation(out=junk[0:nt, :], in_=x_tile, func=AF.Square, accum_out=stat[0:nt, 0:1]),
             reads=[xkey], writes=[statkey, ("junk", idx)])
        if idx == 0 and nt == 128:
            dbg("junk0", junk[:, :], ("junk", 0))
        P.op("act", lambda e: e.activation(out=stat[0:nt, 1:2], in_=stat[0:nt, 0:1], func=AF.Ln, scale=1.0 / D, bias=EPS),
             reads=[statkey], writes=[statkey])
        P.op("act", lambda e: e.activation(out=stat[0:nt, 1:2], in_=stat[0:nt, 1:2], func=AF.Exp, scale=-0.5),
             reads=[statkey], writes=[statkey])
        P.op("act", lambda e: e.activation(out=xn[0:nt, :], in_=x_tile, func=AF.Copy, scale=stat[0:nt, 1:2]),
             reads=[xkey, statkey], writes=[xnkey])

    def to_fm(xn, xnkey, nt, bank, ident):
        if xn.dtype == BF16:
            pv = ps[bank][:].bitcast(BF16)
        else:
            pv = ps[bank][:]

        def tr(e):
            ins = None
            for kc in range(KC):
                ins = e.transpose(pv[:, kc * nt:(kc + 1) * nt], xn[0:nt, kc * 128:(kc + 1) * 128], ident[0:nt, 0:nt])
            return ins
        P.op("pe", tr, reads=[xnkey, "consts", "cbf"], writes=[PK[bank]])
        return pv

    with ExitStack() as s1:
        xt = [sb("xt%d" % i, [128, D], stack=s1) for i in range(3)]
        xn = [sb("xn%d" % i, [128, D], BF16, stack=s1) for i in range(2)]
        junk = sb("junk", [128, D], F32, stack=s1)
        st1 = sb("st1", [128, 2 * (NB_ALL + 1)], stack=s1)
        for b in range(NB_ALL + 1):
            P.op("dve", lambda e, b=b: e.memset(st1[:, 2 * b:2 * b + 2], 0.0), writes=[("st1", b)])
        P.op("act", lambda e: e.activation(out=junk[:, 0:128], in_=consts[:, 0:128], func=AF.Square, accum_out=st1[:, 0:1]),
             reads=["consts"], writes=[("st1", 0)])
        P.op("dve", lambda e: e.memset(st1[:, 0:2], 0.0), writes=[("st1", 0)])
        for b in range(NB_ALL):
            xk = ("xt", b % 3)
            P.dma("sp", xt[b % 3][:], xa[b * 128:(b + 1) * 128, :], writes=[xk])
            nk = ("xn", b % 2)
            norm_block(s1, xt[b % 3][:], xk, 128, st1[:, 2 * b:2 * b + 2], ("st1", b), xn[b % 2], nk, junk, b)
            if b == 0:
                dbg("xt0", xt[0][:], xk)
                dbg("xn0", xn[0][:], nk)
                dbg("st0", st1[:, 0:2], ("st1", 0))
            bank = b % 2
            pv = to_fm(xn[b % 2], nk, 128, bank, ident_b)
            for kc in range(KC):
                P.op("dve", lambda e, kc=kc, pv=pv, b=b: e.tensor_scalar(
                    out=hT[:, kc, b * 128:(b + 1) * 128], in0=pv[:, kc * 128:(kc + 1) * 128],
                    scalar1=AB[:, 0, kc:kc + 1], scalar2=AB[:, 1, kc:kc + 1], op0=ALU.mult, op1=ALU.add),
                    reads=[PK[bank], "AB"], writes=[("hT", b)])
        xs_t = sb("xs_t", [NS, D], stack=s1)
        P.dma("sp", xs_t[:], xs_d, writes=["xs_t"])
        xns = sb("xns", [NS, D], BF16, stack=s1)
        norm_block(s1, xs_t[:], "xs_t", NS, st1[:, 2 * NB_ALL:2 * NB_ALL + 2], ("st1", NB_ALL), xns, "xns", junk, 999)
        pv = to_fm(xns, "xns", NS, 2, ident_b)
        tmp_s = sb("tmp_s", [128, KC, NS], stack=s1)
        P.op("dve", lambda e: e.tensor_tensor(out=tmp_s[:], in0=pv[:, 0:KC * NS].rearrange("p (k n) -> p k n", n=NS), in1=ABs[:, 0, :, :], op=ALU.mult),
             reads=[PK[2], "ABs"], writes=["tmp_s"])
        P.op("dve", lambda e: e.tensor_tensor(out=hsT[:], in0=tmp_s[:], in1=ABs[:, 1, :, :], op=ALU.add),
             reads=["tmp_s", "ABs"], writes=["hsT"])
    P.fence()
    dbg("hT", hT[:, :, :], [("hT", b) for b in range(NB_ALL)])
    dbg("hsT", hsT[:], "hsT")
    if cfg.stop == 1:
        P.finish()
        return nc, P, es, dbg_outs

    def wdma(tile_ap, w, c0, ncols, key):
        P.dma("pool", tile_ap, wview(w, c0, ncols), writes=[key])

    with ExitStack() as s2:
        NBH = NB_ALL * H
        GTk = sb("GTk", [48, S_ALL], BF16, stack=s2)
        GTq = sb("GTq", [48, S_OWN], BF16, stack=s2)
        s2f = ExitStack()
        wfl = sb("wfl", [128, KC, H], BF16, stack=s2f)
        wdma(wfl[:], w_in, FLO, H, "wfl")
        sp_t = sb("sp_t", [128, NB_ALL, H], stack=s2f)
        G = sb("G", [128, NB_ALL, H], stack=s2f)
        def flmm(e):
            ins = None
            for b in range(NB_ALL):
                for kc in range(KC):
                    ins = e.matmul(ps[7][:, b * H:(b + 1) * H], lhsT=hT[:, kc, b * 128:(b + 1) * 128], rhs=wfl[:, kc, :],
                                   start=(kc == 0), stop=(kc == KC - 1))
            return ins
        P.op("pe", flmm, reads=[("hT", b) for b in range(NB_ALL)] + ["wfl"], writes=[PK[7]])
        P.op("dve", lambda e: e.tensor_tensor(out=sp_t[:], in0=ps[7][:, 0:NBH].rearrange("p (b h) -> p b h", h=H),
                                              in1=bfb[:].unsqueeze(1).to_broadcast([128, NB_ALL, H]), op=ALU.add),
             reads=[PK[7], "bfb"], writes=["sp_t"])
        P.op("act", lambda e: e.activation(out=sp_t[:], in_=sp_t[:], func=AF.Exp, scale=-1.0), reads=["sp_t"], writes=["sp_t"])
        P.op("act", lambda e: e.activation(out=sp_t[:], in_=sp_t[:], func=AF.Ln, bias=1.0), reads=["sp_t"], writes=["sp_t"])
        def flmm_s(e):
            ins = None
            for kc in range(KC):
                ins = e.matmul(ps[6][0:NS, 0:H], lhsT=hsT[:, kc, :], rhs=wfl[:, kc, :], start=(kc == 0), stop=(kc == KC - 1))
            return ins
        P.op("pe", flmm_s, reads=["hsT", "wfl"], writes=[PK[6]])
        P.op("dve", lambda e: e.tensor_tensor(out=lfs_tm[:], in0=ps[6][0:NS, 0:H], in1=bfb[0:NS, :], op=ALU.add),
             reads=[PK[6], "bfb"], writes=["lfs_tm"])
        P.op("act", lambda e: e.activation(out=lfs_tm[:], in_=lfs_tm[:], func=AF.Exp, scale=-1.0), reads=["lfs_tm"], writes=["lfs_tm"])
        P.op("act", lambda e: e.activation(out=lfs_tm[:], in_=lfs_tm[:], func=AF.Ln, bias=1.0), reads=["lfs_tm"], writes=["lfs_tm"])
        P.op("dve", lambda e: e.tensor_scalar(out=lfs_tm[:], in0=lfs_tm[:], scalar1=-1.0, scalar2=None, op0=ALU.mult),
             reads=["lfs_tm"], writes=["lfs_tm"])
        P.dma("sp", lfs_o, lfs_tm[:], reads=["lfs_tm"], lane="lfs_out", final=True)
        lfst = sb("lfst", [128, NB_OWN, H], stack=s2f)
        P.op("dve", lambda e: e.tensor_scalar(out=lfst[:], in0=sp_t[:, NB_PRE:, :], scalar1=-1.0, scalar2=None, op0=ALU.mult),
             reads=["sp_t"], writes=["lfst"])
        P.dma("sp", lf_o.rearrange("(b p) h -> p b h", p=128), lfst[:], reads=["lfst"], lane="lf_out", final=True)
        spf = sp_t[:].rearrange("p b h -> p (b h)")
        P.op("pe", lambda e: e.matmul(ps[6][:, 0:NBH], lhsT=consts[:, C_U:C_U + 128], rhs=spf, start=True, stop=True),
             reads=["sp_t", "consts"], writes=[PK[6]])
        P.op("pe", lambda e: e.matmul(ps[7][:, 0:NBH], lhsT=ones_f, rhs=spf, start=True, stop=True),
             reads=["sp_t", "consts"], writes=[PK[7]])
        TA = sb("TA", [128, NB_ALL, H], stack=s2f)
        TB = sb("TB", [128, NB_ALL, H], stack=s2f)
        P.op("dve", lambda e: e.tensor_copy(out=TA[:].rearrange("p b h -> p (b h)"), in_=ps[7][:, 0:NBH]), reads=[PK[7]], writes=["TA"])
        cur, oth, ck_, ok_ = TA, TB, "TA", "TB"
        st = 1
        while st < NB_ALL:
            P.op("dve", lambda e, cur=cur, oth=oth, st=st: e.tensor_copy(out=oth[:, 0:st, :], in_=cur[:, 0:st, :]), reads=[ck_], writes=[ok_])
            P.op("dve", lambda e, cur=cur, oth=oth, st=st: e.tensor_tensor(out=oth[:, st:, :], in0=cur[:, st:, :], in1=cur[:, 0:NB_ALL - st, :], op=ALU.add),
                 reads=[ck_], writes=[ok_])
            cur, oth, ck_, ok_ = oth, cur, ok_, ck_
            st *= 2
        P.op("dve", lambda e, cur=cur: e.tensor_tensor(out=G[:].rearrange("p b h -> p (b h)"), in0=ps[6][:, 0:NBH],
                                                      in1=cur[:].rearrange("p b h -> p (b h)"), op=ALU.add),
             reads=[PK[6], ck_], writes=["G"])
        P.op("dve", lambda e: e.tensor_tensor(out=G[:].rearrange("p b h -> p (b h)"), in0=G[:].rearrange("p b h -> p (b h)"),
                                              in1=ps[7][:, 0:NBH], op=ALU.subtract),
             reads=[PK[7], "G"], writes=["G"])
        Gk = sb("Gk", [128, NB_ALL, H], stack=s2f)
        P.op("dve", lambda e: e.tensor_scalar(out=Gk[:, 0:NB_PRE, :], in0=G[:, 0:NB_PRE, :], scalar1=pm, scalar2=None, op0=ALU.add),
             reads=["G", "flg"], writes=["Gk"])
        P.op("dve", lambda e: e.tensor_copy(out=Gk[:, NB_PRE:, :], in_=G[:, NB_PRE:, :]), reads=["G"], writes=["Gk"])
        Gq = sb("Gq", [128, NB_OWN, H], stack=s2f)
        P.op("dve", lambda e: e.tensor_scalar(out=Gq[:], in0=G[:, NB_PRE:, :], scalar1=-1.0, scalar2=None, op0=ALU.mult),
             reads=["G"], writes=["Gq"])
        pk_ = sb("pk_", [128, NB_ALL, H, 3], BF16, stack=s2f)
        pq_ = sb("pq_", [128, NB_OWN, H, 3], BF16, stack=s2f)
        rr = sb("rr", [128, NB_ALL, H], stack=s2f)

        def split3(src, srckey, dst, dstkey, nb):
            P.op("dve", lambda e: e.tensor_copy(out=dst[:, :, :, 0], in_=src[:, 0:nb, :]), reads=[srckey], writes=[dstkey])
            P.op("dve", lambda e: e.tensor_tensor(out=rr[:, 0:nb, :], in0=src[:, 0:nb, :], in1=dst[:, :, :, 0], op=ALU.subtract),
                 reads=[srckey, dstkey], writes=["rr"])
            P.op("dve", lambda e: e.tensor_copy(out=dst[:, :, :, 1], in_=rr[:, 0:nb, :]), reads=["rr"], writes=[dstkey])
            P.op("dve", lambda e: e.tensor_tensor(out=rr[:, 0:nb, :], in0=rr[:, 0:nb, :], in1=dst[:, :, :, 1], op=ALU.subtract),
                 reads=["rr", dstkey], writes=["rr"])
            P.op("dve", lambda e: e.tensor_copy(out=dst[:, :, :, 2], in_=rr[:, 0:nb, :]), reads=["rr"], writes=[dstkey])
        split3(Gk, "Gk", pk_, "pk_", NB_ALL)
        split3(Gq, "Gq", pq_, "pq_", NB_OWN)
        for (src, srckey, dstT, dkey, nb) in ((pk_, "pk_", GTk, "GTk", NB_ALL), (pq_, "pq_", GTq, "GTq", NB_OWN)):
            gs = min(8, nb)
            for g0 in range(0, nb, gs):
                bank = 4 + (g0 // 8) % 2
                pvb = ps[bank][:].bitcast(BF16)

                def trg(e, src=src, g0=g0, pvb=pvb, gs=gs):
                    ins = None
                    for b in range(g0, g0 + gs):
                        ins = e.transpose(pvb[0:48, (b - g0) * 128:(b - g0 + 1) * 128],
                                          src[:, b, :, :].rearrange("p h c -> p (h c)"), ident_b)
                    return ins
                P.op("pe", trg, reads=[srckey, "cbf"], writes=[PK[bank]])
                P.op("dve", lambda e, dstT=dstT, g0=g0, pvb=pvb, gs=gs: e.tensor_copy(out=dstT[:, g0 * 128:(g0 + gs) * 128], in_=pvb[0:48, 0:gs * 128]),
                     reads=[PK[bank]], writes=[dkey])

        s2f.close()
        P.fence()
        qa = [sb("qa%d" % i, [128, S_OWN], BF16, stack=s2) for i in range(2)]
        ka = [sb("ka%d" % i, [128, S_ALL], BF16, stack=s2) for i in range(2)]
        for i in range(2):
            P.op("dve", lambda e, i=i: e.memset(qa[i][64:128, :], 1.0), writes=[("qa", i)])
            P.op("dve", lambda e, i=i: e.memset(ka[i][64:128, :], 1.0), writes=[("ka", i)])
        vaug = [sb("vaug%d" % i, [128, NB_ALL, 128], BF16, stack=s2) for i in range(2)]
        P.op("pool", lambda e: e.memset(vaug[0][:, :, 64:128], 1.0), writes=[("vaug", 0)])
        P.op("pool", lambda e: e.memset(vaug[1][:, :, 0:64], 1.0), writes=[("vaug", 1)])
        wqk = [sb("wqk%d" % i, [128, KC, 128], BF16, stack=s2) for i in range(2)]
        wkv = [sb("wkv%d" % i, [128, KC, 256], BF16, stack=s2) for i in range(2)]
        kvst = [sb("kvst%d" % i, [128, 2, 256], stack=s2) for i in range(2)]
        PT = [sb("PT%d" % i, [128, 512], BF16, stack=s2) for i in range(3)]
        rl = sb("rl", [128, 512], stack=s2)
        ostg = [sb("ostg%d" % i, [128, 512], BF16, stack=s2) for i in range(2)]
        kvs_st = sb("kvs_st", [NS, 256], stack=s2)
        bmb = [cbf[:, C_BM + 512 * j:C_BM + 512 * (j + 1)] for j in range(4)]
        kvcnt = [0]

        def pair_kv(pr):
            w = wkv[pr % 2]
            wk = ("wkv", pr % 2)
            wdma(w[:, :, 0:128], w_in, KO + pr * 128, 128, wk)
            wdma(w[:, :, 128:256], w_in, VO + pr * 128, 128, wk)
            import os
            for g0 in range(0, 0 if os.environ.get("ONLYOWN") else NB_PRE, 4):
                bank = 5 + (kvcnt[0] % 2)
                kvcnt[0] += 1

                def mm(e, g0=g0, bank=bank):
                    ins = None
                    for b in range(g0, g0 + 4):
                        for kc in range(KC):
                            ins = e.matmul(ps[bank][:, (b - g0) * 128:(b - g0 + 1) * 128], lhsT=hT[:, kc, b * 128:(b + 1) * 128],
                                           rhs=w[:, kc, 128:256], start=(kc == 0), stop=(kc == KC - 1))
                    return ins
                P.op("pe", mm, reads=[("hT", b) for b in range(g0, g0 + 4)] + [wk], writes=[PK[bank]])
                pv4 = ps[bank][:, :].rearrange("p (b c) -> p b c", c=128)
                P.op("dve", lambda e, g0=g0, pv4=pv4: e.tensor_copy(out=vaug[0][:, g0:g0 + 4, 0:64], in_=pv4[:, :, 0:64]),
                     reads=[PK[bank]], writes=[("vaug", 0)])
                P.op("dve", lambda e, g0=g0, pv4=pv4: e.tensor_copy(out=vaug[1][:, g0:g0 + 4, 64:128], in_=pv4[:, :, 64:128]),
                     reads=[PK[bank]], writes=[("vaug", 1)])
            import os
            BIS = int(os.environ.get("BIS", "99"))
            if BIS <= 1:
                return
            P.fence()
            for g0 in range(NB_PRE, NB_ALL, 2):
                bank = 5 + (kvcnt[0] % 2)
                if os.environ.get("BANK5"):
                    bank = 5
                sl = kvcnt[0] % 2
                kvcnt[0] += 1

                def mm2(e, g0=g0, bank=bank):
                    ins = None
                    for b in range(g0, g0 + 2):
                        for kc in range(KC):
                            if os.environ.get("VONLY"):
                                ins = e.matmul(ps[bank][:, (b - g0) * 256 + 128:(b - g0 + 1) * 256], lhsT=hT[:, kc, b * 128:(b + 1) * 128],
                                               rhs=w[:, kc, 128:256], start=(kc == 0), stop=(kc == KC - 1))
                            else:
                                ins = e.matmul(ps[bank][:, (b - g0) * 256:(b - g0 + 1) * 256], lhsT=hT[:, kc, b * 128:(b + 1) * 128],
                                               rhs=w[:, kc, :], start=(kc == 0), stop=(kc == KC - 1))
                    return ins
                P.op("pe", mm2, reads=[("hT", b) for b in range(g0, g0 + 2)] + [wk], writes=[PK[bank]])
                if os.environ.get("NOEVAC"):
                    continue
                P.op("act", lambda e, sl=sl, bank=bank: e.activation(out=kvst[sl][:].rearrange("p b c -> p (b c)"), in_=ps[bank][:, :], func=AF.Copy),
                     reads=[PK[bank]], writes=[("kvst", sl)])
                P.op("dve", lambda e, g0=g0, sl=sl: e.tensor_copy(out=vaug[0][:, g0:g0 + 2, 0:64], in_=kvst[sl][:, :, 128:192]),
                     reads=[("kvst", sl)], writes=[("vaug", 0)])
                P.op("dve", lambda e, g0=g0, sl=sl: e.tensor_copy(out=vaug[1][:, g0:g0 + 2, 64:128], in_=kvst[sl][:, :, 192:256]),
                     reads=[("kvst", sl)], writes=[("vaug", 1)])
                ob = g0 - NB_PRE
                if BIS <= 2:
                    continue
                P.dma("sp", k_o[ob * 128:(ob + 2) * 128, pr * 128:(pr + 1) * 128].rearrange("(b p) c -> p b c", p=128),
                      kvst[sl][:, :, 0:128], reads=[("kvst", sl)], lane=("kvst", sl), final=True)
                P.dma("sp", v_o[ob * 128:(ob + 2) * 128, pr * 128:(pr + 1) * 128].rearrange("(b p) c -> p b c", p=128),
                      kvst[sl][:, :, 128:256], reads=[("kvst", sl)], lane=("kvst", sl), final=True)
            if BIS <= 3:
                return
            def mms(e):
                ins = None
                for kc in range(KC):
                    ins = e.matmul(ps[7][0:NS, 0:256], lhsT=hsT[:, kc, :], rhs=w[:, kc, :], start=(kc == 0), stop=(kc == KC - 1))
                return ins
            P.op("pe", mms, reads=["hsT", wk], writes=[PK[7]])
            P.op("dve", lambda e: e.tensor_copy(out=ks_tm[:, pr * 128:(pr + 1) * 128], in_=ps[7][0:NS, 0:128]), reads=[PK[7]], writes=["ks_tm"])
            P.op("dve", lambda e: e.tensor_copy(out=vs_tm[:, pr * 128:(pr + 1) * 128], in_=ps[7][0:NS, 128:256]), reads=[PK[7]], writes=["vs_tm"])

        qkcnt = [0]

        def head_qk(h):
            import os
            BIS = int(os.environ.get("BIS", "99"))
            if BIS <= 4:
                return
            w = wqk[h % 2]
            wk = ("wqk", h % 2)
            wdma(w[:, :, 0:64], w_in, QO + h * 64, 64, wk)
            wdma(w[:, :, 64:128], w_in, KO + h * 64, 64, wk)
            qk_, kk_ = ("qa", h % 2), ("ka", h % 2)
            for t in range(NT_ALL):
                bank = 5 + (qkcnt[0] % 2)
                qkcnt[0] += 1

                def mm(e, t=t, bank=bank):
                    ins = None
                    for kc in range(KC):
                        ins = e.matmul(ps[bank][:, :], lhsT=w[:, kc, :], rhs=hT[:, kc, t * 512:(t + 1) * 512], start=(kc == 0), stop=(kc == KC - 1))
                    return ins
                P.op("pe", mm, reads=[("hT", b) for b in range(4 * t, 4 * t + 4)] + [wk], writes=[PK[bank]])
                P.op("dve", lambda e, t=t, bank=bank: e.tensor_copy(out=ka[h % 2][0:64, t * 512:(t + 1) * 512], in_=ps[bank][64:128, :]),
                     reads=[PK[bank]], writes=[kk_])
                if t >= NT_PRE:
                    P.op("dve", lambda e, t=t, bank=bank: e.tensor_scalar(out=qa[h % 2][0:64, (t - NT_PRE) * 512:(t - NT_PRE + 1) * 512], in0=ps[bank][0:64, :],
                                                                      scalar1=0.125, scalar2=None, op0=ALU.mult),
                         reads=[PK[bank]], writes=[qk_])
            import os
            if not os.environ.get("SKIP_AUG"):
                P.dma("sp", qa[h % 2][64:67, :], GTq[3 * h:3 * h + 3, :], reads=["GTq"], writes=[qk_], lane=qk_)
                P.dma("sp", ka[h % 2][67:70, :], GTk[3 * h:3 * h + 3, :], reads=["GTk"], writes=[kk_], lane=kk_)
            def mms(e):
                ins = None
                for kc in range(KC):
                    ins = e.matmul(ps[7][0:NS, 0:128], lhsT=hsT[:, kc, :], rhs=w[:, kc, :], start=(kc == 0), stop=(kc == KC - 1))
                return ins
            P.op("pe", mms, reads=["hsT", wk], writes=[PK[7]])
            P.op("dve", lambda e: e.tensor_scalar(out=qs_tm[:, h * 64:(h + 1) * 64], in0=ps[7][0:NS, 0:64], scalar1=0.125, scalar2=None, op0=ALU.mult),
                 reads=[PK[7]], writes=["qs_tm"])

        scnt = [0]

        def head_attn(h):
            qk_, kk_ = ("qa", h % 2), ("ka", h % 2)
            va = vaug[h % 2]
            vk = ("vaug", h % 2)
            for i in range(NT_OWN):
                nkb = NB_PRE + 4 * i + 4
                obank = 3 + (i % 2)
                ok = PK[obank]
                slots = []

                def S_step(kb):
                    sl = scnt[0] % 3
                    scnt[0] += 1
                    j = kb - (NB_PRE + 4 * i)

                    def mm(e):
                        ins = e.matmul(ps[sl][:, :], lhsT=ka[h % 2][0:70, kb * 128:(kb + 1) * 128], rhs=qa[h % 2][0:70, i * 512:(i + 1) * 512],
                                       start=True, stop=(j < 0))
                        if j >= 0:
                            ins = e.matmul(ps[sl][:, :], lhsT=ident_b, rhs=bmb[j], start=False, stop=True)
                        return ins
                    P.op("pe", mm, reads=[qk_, kk_, "cbf"], writes=[PK[sl]])
                    P.op("act", lambda e: e.activation(out=PT[sl][:, :], in_=ps[sl][:, :], func=AF.Exp), reads=[PK[sl]], writes=[("PT", sl)])
                    return sl

                def PV_step(kb, sl):
                    P.op("pe", lambda e: e.matmul(ps[obank][:, :], lhsT=va[:, kb, :], rhs=PT[sl][:, :], start=(kb == 0), stop=(kb == nkb - 1)),
                         reads=[("PT", sl), vk], writes=[ok])
                pend = []
                for kb in range(nkb):
                    pend.append((kb, S_step(kb)))
                    if len(pend) > 2:
                        PV_step(*pend.pop(0))
                while pend:
                    PV_step(*pend.pop(0))
                olo, llo = (0, 64) if h % 2 == 0 else (64, 0)
                P.op("act", lambda e: e.activation(out=rl[llo:llo + 64, :], in_=ps[obank][llo:llo + 64, :], func=AF.Ln), reads=[ok], writes=["rl"])
                P.op("act", lambda e: e.activation(out=rl[llo:llo + 64, :], in_=rl[llo:llo + 64, :], func=AF.Exp, scale=-1.0), reads=["rl"], writes=["rl"])
                osl = (h * NT_OWN + i) % 2
                P.op("dve", lambda e: e.tensor_tensor(out=ostg[osl][olo:olo + 64, :], in0=ps[obank][olo:olo + 64, :],
                                                      in1=rl[llo:llo + 64, :], op=ALU.mult),
                     reads=[ok, "rl"], writes=[("ostg", osl)])
                P.dma("sp", oT_d[olo:olo + 64, h // 2, i * 512:(i + 1) * 512], ostg[osl][olo:olo + 64, :], reads=[("ostg", osl)], writes=["oT_d"], lane=("ostg", osl))

        for h in range(H):
            if h % 2 == 0:
                pair_kv(h // 2)
            head_qk(h)
            if cfg.stop == 1.5:
                dbg("qa0", qa[0][0:70, :], ("qa", 0))
                dbg("ka0", ka[0][0:70, :], ("ka", 0))
                dbg("vaug0", vaug[0][:], ("vaug", 0))
                P.fence()
                P.finish()
                return nc, P, es, dbg_outs
            head_attn(h)
            if cfg.stop == 1.7:
                break
    P.fence()
    dbg("qs_tm", qs_tm[:], "qs_tm")
    if cfg.stop == 2:
        P.finish()
        return nc, P, es, dbg_outs
    P.op("dve", lambda e: e.tensor_scalar(out=omvf[:], in0=vf, scalar1=-1.0, scalar2=1.0, op0=ALU.mult, op1=ALU.add), reads=["flg"], writes=["omvf"])
    with ExitStack() as s3:
        sc_tm = sb("sc_tm", [NS, 3 * D], stack=s3)
        P.dma("sp", sc_tm[:], sconv_d.rearrange("b j d -> b (j d)"), writes=["sc_tm"])
        h0_tm = sb("h0_tm", [NS, D], stack=s3)
        P.dma("sp", h0_tm[:], srg_d, writes=["h0_tm"])
        scT = sb("scT", [128, 3, KC, NS], stack=s3)
        h0T = sb("h0T", [128, KC, NS], stack=s3)

        def trs(e):
            ins = None
            for j in range(3):
                for kc in range(KC):
                    ins = e.transpose(ps[7][:, (j * KC + kc) * NS:(j * KC + kc + 1) * NS], sc_tm[:, j * D + kc * 128:j * D + (kc + 1) * 128], ident_f[0:NS, 0:NS])
            for kc in range(KC):
                ins = e.transpose(ps[7][:, (3 * KC + kc) * NS:(3 * KC + kc + 1) * NS], h0_tm[:, kc * 128:(kc + 1) * 128], ident_f[0:NS, 0:NS])
            return ins
        P.op("pe", trs, reads=["sc_tm", "h0_tm", "consts"], writes=[PK[7]])
        P.op("dve", lambda e: e.tensor_copy(out=scT[:].rearrange("p j k n -> p (j k n)"), in_=ps[7][:, 0:3 * KC * NS]), reads=[PK[7]], writes=["scT"])
        P.op("dve", lambda e: e.tensor_copy(out=h0T[:].rearrange("p k n -> p (k n)"), in_=ps[7][:, 3 * KC * NS:4 * KC * NS]), reads=[PK[7]], writes=["h0T"])
        P.dma("sp", convs_o[:, 0:2, :], sconv_d[:, 1:3, :], lane="convs_a", final=True)

        wxg = [sb("wxg%d" % i, [128, KC, 256], BF16, stack=s3) for i in range(2)]
        wrb = [sb("wrb%d" % i, [128, 2, 128], BF16, stack=s3) for i in range(2)]
        for i in range(2):
            P.op("dve", lambda e, i=i: e.memset(wrb[i][:], 0.0), writes=[("wrb", i)])
        xrp = [sb("xrp%d" % i, [128, 515], stack=s3) for i in range(2)]
        xc = [sb("xc%d" % i, [128, 512], stack=s3) for i in range(2)]
        xcb = [sb("xcb%d" % i, [128, 512], BF16, stack=s3) for i in range(2)]
        rg = [sb("rg%d" % i, [128, 512], stack=s3) for i in range(2)]
        ig = [sb("ig%d" % i, [128, 512], stack=s3) for i in range(2)]
        at = [sb("at%d" % i, [128, 512], stack=s3) for i in range(2)]
        mt = [sb("mt%d" % i, [128, 512], stack=s3) for i in range(2)]
        ut = [sb("ut%d" % i, [128, 512], stack=s3) for i in range(2)]
        hh = [sb("hh%d" % i, [128, 512], stack=s3) for i in range(2)]
        gl = [sb("gl%d" % i, [128, 512], stack=s3) for i in range(2)]
        ystg = [sb("ystg%d" % i, [128, 512], BF16, stack=s3) for i in range(2)]
        cnt2 = [0]

        def rg_tile(c, w, wk, wb, wbk, src_rhs, srckeys, n, prev, is_prefix, first_own, sample):
            sl = cnt2[0] % 2
            cnt2[0] += 1
            K_ = lambda nm: (nm, sl)
            cw = lambda j: vfm[:, V_CW + j * 8 + c:V_CW + j * 8 + c + 1]
            cb = vfm[:, V_CB + c:V_CB + c + 1]
            def mm(e):
                ins = None
                for kc in range(KC):
                    ins = e.matmul(ps[0][:, 0:n], lhsT=w[:, kc, 0:128], rhs=src_rhs(kc), start=(kc == 0), stop=(kc == KC - 1))
                return ins
            P.op("pe", mm, reads=srckeys + [wk], writes=[PK[0]])
            if sample:
                P.op("dve", lambda e: e.tensor_copy(out=xrsT[:, c, :], in_=ps[0][:, 0:n]), reads=[PK[0]], writes=["xrsT"])
                P.op("dve", lambda e: e.tensor_scalar(out=xc[sl][:, 0:n], in0=ps[0][:, 0:n], scalar1=cw(3), scalar2=cb, op0=ALU.mult, op1=ALU.add),
                     reads=[PK[0], "vfm"], writes=[K_("xc")])
                for j in range(3):
                    P.op("dve", lambda e, j=j: e.scalar_tensor_tensor(out=xc[sl][:, 0:n], in0=scT[:, j, c, :], scalar=cw(j), in1=xc[sl][:, 0:n],
                                                                 op0=ALU.mult, op1=ALU.add),
                         reads=["scT", K_("xc"), "vfm"], writes=[K_("xc")])
            else:
                if is_prefix:
                    P.op("dve", lambda e: e.tensor_scalar(out=xrp[sl][:, 3:515], in0=ps[0][:, :], scalar1=vf, scalar2=None, op0=ALU.mult),
                         reads=[PK[0], "flg"], writes=[K_("xrp")])
                else:
                    P.op("dve", lambda e: e.tensor_copy(out=xrp[sl][:, 3:515], in_=ps[0][:, :]), reads=[PK[0]], writes=[K_("xrp")])
                if prev is None:
                    P.op("dve", lambda e: e.memset(xrp[sl][:, 0:3], 0.0), writes=[K_("xrp")])
                else:
                    P.op("dve", lambda e: e.tensor_copy(out=xrp[sl][:, 0:3], in_=xrp[prev][:, 512:515]), reads=[("xrp", prev)], writes=[K_("xrp")])
                P.op("dve", lambda e: e.tensor_scalar(out=xc[sl][:], in0=xrp[sl][:, 3:515], scalar1=cw(3), scalar2=cb, op0=ALU.mult, op1=ALU.add),
                     reads=[K_("xrp"), "vfm"], writes=[K_("xc")])
                for j in (2, 1, 0):
                    P.op("dve", lambda e, j=j: e.scalar_tensor_tensor(out=xc[sl][:], in0=xrp[sl][:, j:j + 512], scalar=cw(j), in1=xc[sl][:],
                                                                 op0=ALU.mult, op1=ALU.add),
                         reads=[K_("xrp"), K_("xc"), "vfm"], writes=[K_("xc")])
            P.op("pool", lambda e: e.tensor_copy(out=xcb[sl][:, 0:n], in_=xc[sl][:, 0:n]), reads=[K_("xc")], writes=[K_("xcb")])
            P.op("pe", lambda e: e.matmul(ps[1][:, 0:n], lhsT=wb[:, 0, :], rhs=xcb[sl][:, 0:n], start=True, stop=True), reads=[K_("xcb"), wbk], writes=[PK[1]])
            P.op("pe", lambda e: e.matmul(ps[2][:, 0:n], lhsT=wb[:, 1, :], rhs=xcb[sl][:, 0:n], start=True, stop=True), reads=[K_("xcb"), wbk], writes=[PK[2]])
            P.op("act", lambda e: e.activation(out=rg[sl][:, 0:n], in_=ps[1][:, 0:n], func=AF.Sigmoid, bias=vfm[:, V_BR + c:V_BR + c + 1]),
                 reads=[PK[1], "vfm"], writes=[K_("rg")])
            P.op("act", lambda e: e.activation(out=ig[sl][:, 0:n], in_=ps[2][:, 0:n], func=AF.Sigmoid, bias=vfm[:, V_BI + c:V_BI + c + 1]),
                 reads=[PK[2], "vfm"], writes=[K_("ig")])
            P.op("act", lambda e: e.activation(out=at[sl][:, 0:n], in_=rg[sl][:, 0:n], func=AF.Exp, scale=cl[:, 0, c:c + 1]), reads=[K_("rg"), "cl"], writes=[K_("at")])
            P.op("act", lambda e: e.activation(out=mt[sl][:, 0:n], in_=rg[sl][:, 0:n], func=AF.Exp, scale=cl[:, 1, c:c + 1]), reads=[K_("rg"), "cl"], writes=[K_("mt")])
            P.op("dve", lambda e: e.tensor_scalar(out=mt[sl][:, 0:n], in0=mt[sl][:, 0:n], scalar1=-1.0, scalar2=1.0, op0=ALU.mult, op1=ALU.add),
                 reads=[K_("mt")], writes=[K_("mt")])
            P.op("act", lambda e: e.activation(out=mt[sl][:, 0:n], in_=mt[sl][:, 0:n], func=AF.Sqrt), reads=[K_("mt")], writes=[K_("mt")])
            if (not sample) and is_prefix and prev is None:
                P.op("dve", lambda e: e.memset(mt[sl][:, 0:1], 1.0), reads=[K_("mt")], writes=[K_("mt")])
            if first_own:
                P.op("dve", lambda e: e.tensor_scalar(out=mt[sl][:, 0:1], in0=mt[sl][:, 0:1], scalar1=vf, scalar2=omvf[:, 0:1], op0=ALU.mult, op1=ALU.add),
                     reads=[K_("mt"), "flg", "omvf"], writes=[K_("mt")])
            P.op("dve", lambda e: e.tensor_tensor(out=ut[sl][:, 0:n], in0=mt[sl][:, 0:n], in1=ig[sl][:, 0:n], op=ALU.mult), reads=[K_("mt"), K_("ig")], writes=[K_("ut")])
            P.op("dve", lambda e: e.tensor_tensor(out=ut[sl][:, 0:n], in0=ut[sl][:, 0:n], in1=xc[sl][:, 0:n], op=ALU.mult), reads=[K_("ut"), K_("xc")], writes=[K_("ut")])
            if is_prefix:
                P.op("dve", lambda e: e.tensor_scalar(out=ut[sl][:, 0:n], in0=ut[sl][:, 0:n], scalar1=vf, scalar2=None, op0=ALU.mult),
                     reads=[K_("ut"), "flg"], writes=[K_("ut")])
            if sample:
                P.op("dve", lambda e: e.tensor_tensor(out=hh[sl][:, 0:n], in0=at[sl][:, 0:n], in1=h0T[:, c, :], op=ALU.mult), reads=[K_("at"), "h0T"], writes=[K_("hh")])
                P.op("dve", lambda e: e.tensor_tensor(out=hh[sl][:, 0:n], in0=hh[sl][:, 0:n], in1=ut[sl][:, 0:n], op=ALU.add), reads=[K_("hh"), K_("ut")], writes=[K_("hh")])
                P.op("dve", lambda e: e.tensor_copy(out=hnT[:, c, :], in_=hh[sl][:, 0:n]), reads=[K_("hh")], writes=["hnT"])
            else:
                init = 0.0 if prev is None else hh[prev][:, 511:512]
                P.op("dve", lambda e: e.tensor_tensor_scan(out=hh[sl][:], data0=at[sl][:], data1=ut[sl][:], initial=init, op0=ALU.mult, op1=ALU.add),
                     reads=[K_("at"), K_("ut")] + ([("hh", prev)] if prev is not None else []), writes=[K_("hh")])
            return sl

        def gate_tile(c, w, wk, src_rhs, srckeys, n, sl, dst, dstkey):
            def mm(e):
                ins = None
                for kc in range(KC):
                    ins = e.matmul(ps[3][:, 0:n], lhsT=w[:, kc, 128:256], rhs=src_rhs(kc), start=(kc == 0), stop=(kc == KC - 1))
                return ins
            P.op("pe", mm, reads=srckeys + [wk], writes=[PK[3]])
            P.op("act", lambda e: e.activation(out=gl[sl][:, 0:n], in_=ps[3][:, 0:n], func=AF.Gelu_apprx_tanh), reads=[PK[3]], writes=[("gl", sl)])
            P.op("dve", lambda e: e.tensor_tensor(out=dst, in0=hh[sl][:, 0:n], in1=gl[sl][:, 0:n], op=ALU.mult), reads=[("hh", sl), ("gl", sl)], writes=[dstkey])

        for c in range(KC):
            w = wxg[c % 2]
            wk = ("wxg", c % 2)
            wdma(w[:, :, 0:128], w_in, XR + c * 128, 128, wk)
            wdma(w[:, :, 128:256], w_in, GR + c * 128, 128, wk)
            wb = wrb[c % 2]
            wbk = ("wrb", c % 2)
            for gi, wsrc in enumerate((rg_wr, rg_wi)):
                for q2 in range(2):
                    P.dma("pool", wb[q2 * 64:(q2 + 1) * 64, gi, q2 * 64:(q2 + 1) * 64], wsrc[2 * c + q2], writes=[wbk])
            prev = None
            tl = []
            for t in range(NT_ALL):
                P.rec = []
                sl = rg_tile(c, w, wk, wb, wbk, lambda kc, t=t: hT[:, kc, t * 512:(t + 1) * 512], [("hT", b) for b in range(4 * t, 4 * t + 4)],
                             512, prev, t < NT_PRE, t == NT_PRE, False)
                if t >= NT_PRE:
                    o0 = (t - NT_PRE) * 512
                    ysl = t % 2
                    gate_tile(c, w, wk, lambda kc, t=t: hT[:, kc, t * 512:(t + 1) * 512], [("hT", b) for b in range(4 * t, 4 * t + 4)], 512, sl,
                              ystg[ysl][:, :], ("ystg", ysl))
                    P.dma("sp", ya_d[:, c, o0:o0 + 512], ystg[ysl][:, :], reads=[("ystg", ysl)], writes=["ya_d"], lane=("ystg", ysl))
                prev = sl
                tl.append(P.rec)
                P.rec = None
                if len(tl) == 2:
                    P.replay_interleaved(tl, lag=8)
                    tl = []
            if tl:
                P.replay_interleaved(tl, lag=8)
            P.dma("sp", rg_o[0:1, c * 128:(c + 1) * 128].rearrange("o p -> p o"), hh[prev][:, 511:512], reads=[("hh", prev)], lane=("hh", prev), final=True)
            for j in range(3):
                P.dma("sp", conv_o[j:j + 1, c * 128:(c + 1) * 128].rearrange("o p -> p o"), xrp[prev][:, 512 + j:513 + j], reads=[("xrp", prev)],
                      lane=("xrp", prev), final=True)
            sl = rg_tile(c, w, wk, wb, wbk, lambda kc: hsT[:, kc, :], ["hsT"], NS, None, False, False, True)
            gate_tile(c, w, wk, lambda kc: hsT[:, kc, :], ["hsT"], NS, sl, yainsT[:, c, :], "yainsT")
        for (srcT, skey, dst_ap, lane) in ((hnT, "hnT", rgs_o, "rgs_out"), (xrsT, "xrsT", convs_o[:, 2, :], "convs_b")):
            def trb(e, srcT=srcT):
                ins = None
                for kc in range(KC):
                    ins = e.transpose(ps[4 + kc // 4][0:NS, (kc % 4) * 128:(kc % 4 + 1) * 128], srcT[:, kc, :], ident_f)
                return ins
            P.op("pe", trb, reads=[skey, "consts"], writes=[PK[4], PK[5]])
            stt = sb("stt_" + lane, [NS, D], stack=s3)
            P.op("dve", lambda e, stt=stt: e.tensor_copy(out=stt[:, 0:512], in_=ps[4][0:NS, :]), reads=[PK[4]], writes=["stt_" + lane])
            P.op("dve", lambda e, stt=stt: e.tensor_copy(out=stt[:, 512:1024], in_=ps[5][0:NS, :]), reads=[PK[5]], writes=["stt_" + lane])
            P.dma("sp", dst_ap, stt[:], reads=["stt_" + lane], lane=lane, final=True)
    P.fence()
    dbg("yainsT", yainsT[:], "yainsT")
    if cfg.stop == 3:
        P.finish()
        return nc, P, es, dbg_outs
    with ExitStack() as s4:
        ptb = sb("ptb", [128, NPGS], I32, stack=s4)
        P.dma("sp", ptb[:], pt_d.partition_broadcast(128), writes=["ptb"])
        iot = sb("iot", [128, 1], I32, stack=s4)
        P.op("pool", lambda e: e.iota(iot[:], pattern=[[0, 1]], base=0, channel_multiplier=1), writes=["iot"])
        idx = sb("idx", [128, NPGS], I32, stack=s4)
        P.op("pool", lambda e: e.tensor_scalar(out=idx[:], in0=ptb[:], scalar1=128, scalar2=None, op0=ALU.mult), reads=["ptb"], writes=["idx"])
        P.op("pool", lambda e: e.tensor_tensor(out=idx[:], in0=idx[:], in1=iot[:].to_broadcast([128, NPGS]), op=ALU.add), reads=["idx", "iot"], writes=["idx"])
        ptc = sb("ptc", [NPG, NS], I32, stack=s4)
        for b in range(NS):
            P.dma("sp", ptc[:, b:b + 1], pt_d[0:1, b * NPG:(b + 1) * NPG].rearrange("o j -> j o"), writes=["ptc"])
        kp = [sb("kp%d" % i, [128, D], stack=s4) for i in range(NRING)]
        vp = [sb("vp%d" % i, [128, D], BF16, stack=s4) for i in range(NRING)]
        qb = sb("qb", [128, D], stack=s4)
        scj = [sb("scj%d" % i, [128, H], stack=s4) for i in range(NRING)]
        Pj = [sb("Pj%d" % i, [128, H], BF16, stack=s4) for i in range(NRING)]
        Dt = sb("Dt", [128, H, NPG], stack=s4)
        lfp = sb("lfp", [NPG, 128, H], stack=s4)
        inc = sb("inc", [NPG, 128, H], stack=s4)
        Dp = sb("Dp", [NPG, 128, H], stack=s4)
        totp = sb("totp", [NPG, H], stack=s4)
        onesP = sb("onesP", [NPG, 128], stack=s4)
        P.op("dve", lambda e: e.memset(onesP[:], 1.0), writes=["onesP"])
        onesb = cbf[:, C_ONE:C_ONE + 1]
        msk = consts[0:H, C_BMK:C_BMK + D]
        omask = sb("omask", [H, D], stack=s4)
        ldiag = sb("ldiag", [H, H], stack=s4)
        lcol = sb("lcol", [H, 1], stack=s4)
        pgc = [0]
        for b in range(NS):
            Eb = consts[0:H, C_E + 4 * b:C_E + 4 * b + NS]
            for hf in range(2):
                P.op("pe", lambda e, hf=hf, b=b: e.matmul(ps[hf][:, :], lhsT=consts[0:NS, C_SELQ + 128 * b:C_SELQ + 128 * (b + 1)],
                                                          rhs=qs_tm[:, hf * 512:(hf + 1) * 512], start=True, stop=True),
                     reads=["qs_tm", "consts"], writes=[PK[hf]])
                P.op("dve", lambda e, hf=hf: e.tensor_copy(out=qb[:, hf * 512:(hf + 1) * 512], in_=ps[hf][:, :]), reads=[PK[hf]], writes=["qb"])
            P.dma("pool", lfp[:].rearrange("j s h -> j (s h)"), clf_d, reads=["ptc"], writes=["lfp"],
                  indirect=bass.IndirectOffsetOnAxis(ap=ptc[:, b:b + 1], axis=0))
            for h in range(H):
                P.op("dve", lambda e, h=h: e.tensor_tensor_scan(out=inc[:, :, h], data0=onesP[:], data1=lfp[:, :, h], initial=0.0, op0=ALU.mult, op1=ALU.add),
                     reads=["lfp", "onesP"], writes=["inc"])
            P.op("dve", lambda e: e.tensor_copy(out=totp[:], in_=inc[:, 127, :]), reads=["inc"], writes=["totp"])

            def ymm(e, b=b):
                e.matmul(ps[2][0:NPG, 0:H], lhsT=consts[0:NPG, C_SU:C_SU + NPG], rhs=totp[:], start=True, stop=False)
                return e.matmul(ps[2][0:NPG, 0:H], lhsT=consts[0:NS, C_SELQ + 128 * b:C_SELQ + 128 * b + NPG], rhs=lfs_tm[:], start=False, stop=True)
            P.op("pe", ymm, reads=["totp", "lfs_tm", "consts"], writes=[PK[2]])
            P.op("dve", lambda e: e.tensor_tensor(out=Dp[:], in0=totp[:].unsqueeze(1).to_broadcast([NPG, 128, H]), in1=inc[:], op=ALU.subtract),
                 reads=["totp", "inc"], writes=["Dp"])
            P.op("dve", lambda e: e.tensor_copy(out=totp[:], in_=ps[2][0:NPG, 0:H]), reads=[PK[2]], writes=["totp"])
            P.op("dve", lambda e: e.tensor_tensor(out=Dp[:], in0=Dp[:], in1=totp[:].unsqueeze(1).to_broadcast([NPG, 128, H]), op=ALU.add),
                 reads=["Dp", "totp"], writes=["Dp"])
            nb_ = (H * NPG + 511) // 512
            hpb = 512 // NPG

            def dtr(e):
                ins = None
                for h in range(H):
                    ins = e.transpose(ps[3 + h // hpb][:, (h % hpb) * NPG:(h % hpb + 1) * NPG], Dp[:, :, h], ident_f[0:NPG, 0:NPG])
                return ins
            P.op("pe", dtr, reads=["Dp", "consts"], writes=[PK[3 + i] for i in range(nb_)])
            for i in range(nb_):
                w_ = min(512, H * NPG - i * 512)
                P.op("dve", lambda e, i=i, w_=w_: e.tensor_copy(out=Dt[:].rearrange("p h j -> p (h j)")[:, i * 512:i * 512 + w_], in_=ps[3 + i][:, 0:w_]),
                     reads=[PK[3 + i]], writes=["Dt"])
            for j in range(NPG):
                sl = pgc[0] % NRING
                pgc[0] += 1
                col = b * NPG + j
                P.dma("pool", kp[sl][:], ck_d, reads=["idx"], writes=[("kp", sl)], indirect=bass.IndirectOffsetOnAxis(ap=idx[:, col:col + 1], axis=0))
                P.dma("pool", vp[sl][:], cv_d, reads=["idx"], writes=[("vp", sl)], indirect=bass.IndirectOffsetOnAxis(ap=idx[:, col:col + 1], axis=0))
                P.op("dve", lambda e, sl=sl: e.tensor_tensor(out=kp[sl][:], in0=kp[sl][:], in1=qb[:], op=ALU.mult), reads=[("kp", sl), "qb"], writes=[("kp", sl)])
                P.op("dve", lambda e, sl=sl: e.tensor_reduce(out=scj[sl][:], in_=kp[sl][:].rearrange("p (h d) -> p h d", d=DH), axis=AX.X, op=ALU.add),
                     reads=[("kp", sl)], writes=[("scj", sl)])
                P.op("dve", lambda e, sl=sl, j=j: e.tensor_tensor(out=scj[sl][:], in0=scj[sl][:], in1=Dt[:, :, j], op=ALU.add), reads=[("scj", sl), "Dt"], writes=[("scj", sl)])
                P.op("act", lambda e, sl=sl: e.activation(out=Pj[sl][:], in_=scj[sl][:], func=AF.Exp), reads=[("scj", sl)], writes=[("Pj", sl)])

                def pvm(e, sl=sl, j=j):
                    e.matmul(ps[0][0:H, :], lhsT=Pj[sl][:], rhs=vp[sl][:, 0:512], start=(j == 0), stop=(j == NPG - 1))
                    e.matmul(ps[1][0:H, :], lhsT=Pj[sl][:], rhs=vp[sl][:, 512:1024], start=(j == 0), stop=(j == NPG - 1))
                    return e.matmul(ps[2][0:H, 0:1], lhsT=Pj[sl][:], rhs=onesb, start=(j == 0), stop=(j == NPG - 1))
                P.op("pe", pvm, reads=[("Pj", sl), ("vp", sl), "cbf"], writes=[PK[0], PK[1], PK[2]])
            P.op("dve", lambda e: e.tensor_tensor(out=omask[:, 0:512], in0=ps[0][0:H, :], in1=msk[:, 0:512], op=ALU.mult), reads=[PK[0], "consts"], writes=["omask"])
            P.op("dve", lambda e: e.tensor_tensor(out=omask[:, 512:1024], in0=ps[1][0:H, :], in1=msk[:, 512:1024], op=ALU.mult), reads=[PK[1], "consts"], writes=["omask"])
            P.op("dve", lambda e: e.tensor_copy(out=lcol[:], in_=ps[2][0:H, 0:1]), reads=[PK[2]], writes=["lcol"])
            P.op("dve", lambda e: e.tensor_scalar(out=ldiag[:], in0=ident_f[0:H, 0:H], scalar1=lcol[:, 0:1], scalar2=None, op0=ALU.mult),
                 reads=["lcol", "consts"], writes=["ldiag"])

            def colm(e, b=b, Eb=Eb):
                e.matmul(ps[5][0:NS, :], lhsT=Eb, rhs=omask[:, 0:512], start=(b == 0), stop=(b == NS - 1))
                e.matmul(ps[6][0:NS, :], lhsT=Eb, rhs=omask[:, 512:1024], start=(b == 0), stop=(b == NS - 1))
                return e.matmul(ps[7][0:NS, 0:H], lhsT=Eb, rhs=ldiag[:], start=(b == 0), stop=(b == NS - 1))
            P.op("pe", colm, reads=["omask", "ldiag", "consts"], writes=[PK[5], PK[6], PK[7]])
        tq = sb("tq", [NS, D], stack=s4)
        scur = sb("scur", [NS, H], stack=s4)
        lt = sb("lt", [NS, H], stack=s4)
        P.op("dve", lambda e: e.tensor_tensor(out=tq[:], in0=qs_tm[:], in1=ks_tm[:], op=ALU.mult), reads=["qs_tm", "ks_tm"], writes=["tq"])
        P.op("dve", lambda e: e.tensor_reduce(out=scur[:], in_=tq[:].rearrange("p (h d) -> p h d", d=DH), axis=AX.X, op=ALU.add), reads=["tq"], writes=["scur"])
        P.op("act", lambda e: e.activation(out=scur[:], in_=scur[:], func=AF.Exp), reads=["scur"], writes=["scur"])
        P.op("dve", lambda e: e.tensor_tensor(out=lt[:], in0=ps[7][0:NS, 0:H], in1=scur[:], op=ALU.add), reads=[PK[7], "scur"], writes=["lt"])
        P.op("dve", lambda e: e.reciprocal(out=lt[:], in_=lt[:]), reads=["lt"], writes=["lt"])
        P.op("dve", lambda e: e.tensor_tensor(out=tq[:].rearrange("p (h d) -> p h d", d=DH), in0=vs_tm[:].rearrange("p (h d) -> p h d", d=DH),
                                              in1=scur[:].unsqueeze(2).to_broadcast([NS, H, DH]), op=ALU.mult), reads=["vs_tm", "scur"], writes=["tq"])
        P.op("dve", lambda e: e.tensor_tensor(out=tq[:, 0:512], in0=tq[:, 0:512], in1=ps[5][0:NS, :], op=ALU.add), reads=["tq", PK[5]], writes=["tq"])
        P.op("dve", lambda e: e.tensor_tensor(out=tq[:, 512:1024], in0=tq[:, 512:1024], in1=ps[6][0:NS, :], op=ALU.add), reads=["tq", PK[6]], writes=["tq"])
        P.op("dve", lambda e: e.tensor_tensor(out=os_tm[:].rearrange("p (h d) -> p h d", d=DH), in0=tq[:].rearrange("p (h d) -> p h d", d=DH),
                                              in1=lt[:].unsqueeze(2).to_broadcast([NS, H, DH]), op=ALU.mult), reads=["tq", "lt"], writes=["os_tm"])
        P.dma("sp", ks_o, ks_tm[:], reads=["ks_tm"], lane="ks_out", final=True)
        P.dma("sp", vs_o, vs_tm[:], reads=["vs_tm"], lane="vs_out", final=True)
        osb = sb("osb", [NS, D], BF16, stack=s4)
        P.op("dve", lambda e: e.tensor_copy(out=osb[:], in_=os_tm[:]), reads=["os_tm"], writes=["osb"])
        pvs = to_fm(osb, "osb", NS, 3, ident_b)
        P.op("dve", lambda e: e.tensor_copy(out=osT[:].rearrange("p k n -> p (k n)"), in_=pvs[:, 0:KC * NS]), reads=[PK[3]], writes=["osT"])
    P.fence()
    dbg("os_tm", os_tm[:], "os_tm")
    if cfg.stop == 4:
        P.finish()
        return nc, P, es, dbg_outs

    with ExitStack() as s5:
        w3 = [sb("w3a%d" % i, [128, KC, 512], BF16, stack=s5) for i in range(2)]
        sga = [sb("sga%d" % i, [128, 512], stack=s5) for i in range(2)]
        sgb = [sb("sgb%d" % i, [128, 512], stack=s5) for i in range(2)]
        oTt = sb("oTt", [128, KC, 512], BF16, stack=s5)
        yaTt = sb("yaTt", [128, KC, 512], BF16, stack=s5)
        mstg = [sb("mstg%d" % i, [128, 512], BF16, stack=s5) for i in range(2)]
        c3 = [0]
        for c in range(KC):
            w = w3[c % 2]
            wk = ("w3a", c % 2)
            wdma(w[:, :, 0:128], w_in, GA + c * 128, 128, wk)
            wdma(w[:, :, 128:256], w_in, GB + c * 128, 128, wk)
            wdma(w[:, :, 256:384], w_pa, c * 128, 128, wk)
            wdma(w[:, :, 384:512], w_pb, c * 128, 128, wk)
            for t in range(NT_OWN + 1):
                sample = (t == NT_OWN)
                n = NS if sample else 512
                sl = c3[0] % 2
                c3[0] += 1
                if sample:
                    srcs = [(lambda kc: hsT[:, kc, :], ["hsT"]), (lambda kc: hsT[:, kc, :], ["hsT"]),
                            (lambda kc: yainsT[:, kc, :], ["yainsT"]), (lambda kc: osT[:, kc, :], ["osT"])]
                    dst, dkey = mergedsT[:, c, :], "mergedsT"
                else:
                    tb = NT_PRE + t
                    hk = [("hT", b) for b in range(4 * tb, 4 * tb + 4)]
                    P.dma("sp", oTt[:], oT_d[:, :, t * 512:(t + 1) * 512], reads=["oT_d"], writes=["oTt"])
                    P.dma("sp", yaTt[:], ya_d[:, :, t * 512:(t + 1) * 512], reads=["ya_d"], writes=["yaTt"])
                    srcs = [(lambda kc, tb=tb: hT[:, kc, tb * 512:(tb + 1) * 512], hk), (lambda kc, tb=tb: hT[:, kc, tb * 512:(tb + 1) * 512], hk),
                            (lambda kc: yaTt[:, kc, :], ["yaTt"]),
                            (lambda kc: oTt[:, kc, :], ["oTt"])]
                    dst, dkey = mstg[sl][:, :], ("mstg", sl)
                for gi, (src, skeys) in enumerate(srcs):
                    def mm(e, gi=gi, src=src):
                        ins = None
                        for kc in range(KC):
                            ins = e.matmul(ps[gi][:, 0:n], lhsT=w[:, kc, gi * 128:(gi + 1) * 128], rhs=src(kc), start=(kc == 0), stop=(kc == KC - 1))
                        return ins
                    P.op("pe", mm, reads=skeys + [wk], writes=[PK[gi]])
                P.op("act", lambda e: e.activation(out=sga[sl][:, 0:n], in_=ps[0][:, 0:n], func=AF.Sigmoid), reads=[PK[0]], writes=[("sga", sl)])
                P.op("act", lambda e: e.activation(out=sgb[sl][:, 0:n], in_=ps[1][:, 0:n], func=AF.Sigmoid), reads=[PK[1]], writes=[("sgb", sl)])
                P.op("dve", lambda e: e.tensor_tensor(out=sga[sl][:, 0:n], in0=ps[2][:, 0:n], in1=sga[sl][:, 0:n], op=ALU.mult), reads=[PK[2], ("sga", sl)], writes=[("sga", sl)])
                P.op("dve", lambda e: e.tensor_tensor(out=sgb[sl][:, 0:n], in0=ps[3][:, 0:n], in1=sgb[sl][:, 0:n], op=ALU.mult), reads=[PK[3], ("sgb", sl)], writes=[("sgb", sl)])
                P.op("dve", lambda e, dst=dst: e.tensor_tensor(out=dst, in0=sga[sl][:, 0:n], in1=sgb[sl][:, 0:n], op=ALU.add), reads=[("sga", sl), ("sgb", sl)], writes=[dkey])
                if not sample:
                    P.dma("sp", mg_d[:, c, t * 512:(t + 1) * 512], mstg[sl][:, :], reads=[("mstg", sl)], writes=["mg_d"], lane=("mstg", sl))
    sH.close()
    P.fence()
    if cfg.stop == 5:
        P.finish()
        return nc, P, es, dbg_outs

    NCH = cfg.DFF // 128
    with ExitStack() as s6:
        wo_t = sb("wo_t", [128, KC, D], BF16, stack=s6)
        for hf in range(2):
            wdma(wo_t[:, :, hf * 512:(hf + 1) * 512], w_o, hf * 512, 512, "wo_t")
        wfo_t = sb("wfo_t", [128, NCH, D], BF16, stack=s6)
        for c0 in range(0, NCH, 6):
            c1 = min(NCH, c0 + 6)
            P.dma("pool", wfo_t[:, c0:c1, :], w_fo[c0 * 128:c1 * 128, :].rearrange("(c p) n -> p c n", p=128), writes=["wfo_t"])
        wf = [sb("wf%d" % i, [128, KC, 256], BF16, stack=s6) for i in range(2)]
        x3 = sb("x3", [128, 4, D], stack=s6)
        h2T = sb("h2T", [128, KC, 512], BF16, stack=s6)
        actT = sb("actT", [128, NCH, 512], BF16, stack=s6)
        sg = [sb("sg%d" % i, [128, 512], stack=s6) for i in range(2)]
        tmp3 = [sb("tmp3%d" % i, [128, 512], stack=s6) for i in range(2)]
        xn3 = [sb("xn3%d" % i, [128, D], BF16, stack=s6) for i in range(2)]
        junk3 = sb("junk3", [128, D], BF16, stack=s6)
        st3 = sb("st3", [128, 4], stack=s6)
        yst = [sb("yst%d" % i, [128, D], stack=s6) for i in range(1)]
        tmp_s3 = sb("tmp_s3", [128, KC, NS], stack=s6)
        mTt = sb("mTt", [128, KC, 512], BF16, stack=s6)
        cc = [0]
        for t in range(NT_OWN + 1):
            sample = (t == NT_OWN)
            nblk = 1 if sample else 4
            nt = NS if sample else 128
            ntok = NS if sample else 512
            mT = mergedsT if sample else mTt
            mkeys = ["mergedsT"] if sample else ["mTt"]
            if not sample:
                P.dma("sp", mTt[:], mg_d[:, :, t * 512:(t + 1) * 512], reads=["mg_d"], writes=["mTt"])
            g1t, g1k = (g1s, "g1s") if sample else (g1b, "g1b")
            g2t, g2k = (g2s, "g2s") if sample else (g2b, "g2b")
            if sample:
                P.dma("sp", x3[0:NS, 0, :], xs_d, writes=["x3"])
            else:
                P.dma("sp", x3[:], xa[S_PRE + t * 512:S_PRE + (t + 1) * 512, :].rearrange("(b p) d -> p b d", p=128), writes=["x3"])

            def tok_cols(blk):
                return slice(0, NS) if sample else slice(blk * 128, (blk + 1) * 128)

            def resid(blk, hf, lhs_of, nk, rhs_of, rkeys, gt, gk):
                bank = cc[0] % 2
                cc[0] += 1

                def mm(e):
                    ins = None
                    for kc in range(nk):
                        ins = e.matmul(ps[bank][0:nt, :], lhsT=lhs_of(kc), rhs=rhs_of(kc, hf), start=(kc == 0), stop=(kc == nk - 1))
                    return ins
                P.op("pe", mm, reads=rkeys, writes=[PK[bank]])
                P.op("dve", lambda e: e.tensor_tensor(out=tmp3[bank][0:nt, :], in0=ps[bank][0:nt, :], in1=gt[0:nt, hf * 512:(hf + 1) * 512], op=ALU.mult),
                     reads=[PK[bank], gk], writes=[("tmp3", bank)])
                P.op("dve", lambda e: e.tensor_tensor(out=x3[0:nt, blk, hf * 512:(hf + 1) * 512], in0=x3[0:nt, blk, hf * 512:(hf + 1) * 512], in1=tmp3[bank][0:nt, :], op=ALU.add),
                     reads=[("tmp3", bank), "x3"], writes=["x3"])
            for blk in range(nblk):
                for hf in range(2):
                    resid(blk, hf, lambda kc, blk=blk: mT[:, kc, tok_cols(blk)], KC, lambda kc, hf: wo_t[:, kc, hf * 512:(hf + 1) * 512],
                          mkeys + ["wo_t"], g1t, g1k)
            for blk in range(nblk):
                sl = blk % 2
                P.op("dve", lambda e: e.memset(st3[:, 0:2], 0.0), writes=["st3"])
                norm_block(s6, x3[0:nt, blk, :], "x3", nt, st3[:, 0:2], "st3", xn3[sl], ("xn3", sl), junk3, 999)
                pvv = to_fm(xn3[sl], ("xn3", sl), nt, 2 + sl, ident_b)
                if sample:
                    P.op("dve", lambda e, pvv=pvv: e.tensor_tensor(out=tmp_s3[:], in0=pvv[:, 0:KC * NS].rearrange("p (k n) -> p k n", n=NS), in1=ABs[:, 2, :, :], op=ALU.mult),
                         reads=[PK[2 + sl], "ABs"], writes=["tmp_s3"])
                    P.op("dve", lambda e: e.tensor_tensor(out=h2T[:, :, 0:NS], in0=tmp_s3[:], in1=ABs[:, 3, :, :], op=ALU.add), reads=["tmp_s3", "ABs"], writes=["h2T"])
                else:
                    for kc in range(KC):
                        P.op("dve", lambda e, kc=kc, pvv=pvv, blk=blk: e.tensor_scalar(
                            out=h2T[:, kc, blk * 128:(blk + 1) * 128], in0=pvv[:, kc * 128:(kc + 1) * 128],
                            scalar1=AB[:, 2, kc:kc + 1], scalar2=AB[:, 3, kc:kc + 1], op0=ALU.mult, op1=ALU.add),
                            reads=[PK[2 + sl], "AB"], writes=["h2T"])
            for c in range(NCH):
                w = wf[c % 2]
                wk = ("wf", c % 2)
                wdma(w[:, :, 0:128], w_fi, c * 128, 128, wk)
                wdma(w[:, :, 128:256], w_fi, cfg.DFF + c * 128, 128, wk)
                pb = 4 + 2 * (c % 2)
                for gi in range(2):
                    def mm(e, gi=gi, pb=pb):
                        ins = None
                        for kc in range(KC):
                            ins = e.matmul(ps[pb + gi][:, 0:ntok], lhsT=w[:, kc, gi * 128:(gi + 1) * 128], rhs=h2T[:, kc, 0:ntok], start=(kc == 0), stop=(kc == KC - 1))
                        return ins
                    P.op("pe", mm, reads=["h2T", wk], writes=[PK[pb + gi]])
                P.op("act", lambda e, pb=pb, c=c: e.activation(out=sg[c % 2][:, 0:ntok], in_=ps[pb][:, 0:ntok], func=AF.Silu), reads=[PK[pb]], writes=[("sg", c % 2)])
                P.op("dve", lambda e, pb=pb, c=c: e.tensor_tensor(out=actT[:, c, 0:ntok], in0=ps[pb + 1][:, 0:ntok], in1=sg[c % 2][:, 0:ntok], op=ALU.mult),
                     reads=[PK[pb + 1], ("sg", c % 2)], writes=["actT"])
            for blk in range(nblk):
                for hf in range(2):
                    resid(blk, hf, lambda kc, blk=blk: actT[:, kc, (slice(0, NS) if sample else slice(blk * 128, (blk + 1) * 128))], NCH,
                          lambda kc, hf: wfo_t[:, kc, hf * 512:(hf + 1) * 512], ["actT", "wfo_t"], g2t, g2k)
            for blk in range(nblk):
                sl = 0
                P.op("dve", lambda e: e.memset(st3[:, 2:4], 0.0), writes=["st3b"])
                P.op("act", lambda e, blk=blk: e.activation(out=junk3[0:nt, :], in_=x3[0:nt, blk, :], func=AF.Square, accum_out=st3[0:nt, 2:3]), reads=["x3", "st3b"], writes=["st3b"])
                P.op("act", lambda e: e.activation(out=st3[0:nt, 3:4], in_=st3[0:nt, 2:3], func=AF.Ln, scale=1.0 / D, bias=EPS), reads=["st3b"], writes=["st3b"])
                P.op("act", lambda e: e.activation(out=st3[0:nt, 3:4], in_=st3[0:nt, 3:4], func=AF.Exp, scale=-0.5), reads=["st3b"], writes=["st3b"])
                P.op("act", lambda e, blk=blk, sl=sl: e.activation(out=yst[sl][0:nt, :], in_=x3[0:nt, blk, :], func=AF.Copy, scale=st3[0:nt, 3:4]),
                     reads=["x3", "st3b"], writes=[("yst", sl)])
                P.op("dve", lambda e, sl=sl: e.tensor_tensor(out=yst[sl][0:nt, :], in0=yst[sl][0:nt, :], in1=normf_b[0:nt, :], op=ALU.mult),
                     reads=[("yst", sl), "normf_b"], writes=[("yst", sl)])
                if sample:
                    P.dma("sp", ys_o, yst[sl][0:NS, :], reads=[("yst", sl)], lane=("yst", sl), final=True)
                else:
                    r0 = t * 512 + blk * 128
                    P.dma("sp", y_o[r0:r0 + 128, :], yst[sl][:, :], reads=[("yst", sl)], lane=("yst", sl), final=True)
    P.fence()
    P.finish()
    return nc, P, es, dbg_outs
    return nc, P, es, dbg_outs


def prep_inputs(cfg, inp):
    f = lambda a: np.ascontiguousarray(np.asarray(a))
    D = cfg.D
    x_prompt = f(inp["x_prompt"])
    B, SEQ, _ = x_prompt.shape
    assert SEQ == 2 * cfg.S_OWN
    n_cores = 2 * B
    consts = make_consts()
    vecs = np.concatenate([
        f(inp["ada_b"]).reshape(48, 128), f(inp["norm1_g"]).reshape(8, 128), f(inp["norm2_g"]).reshape(8, 128),
        f(inp["conv_w"]).reshape(32, 128), f(inp["conv_b"]).reshape(8, 128), f(inp["rg_br"]).reshape(8, 128),
        f(inp["rg_bi"]).reshape(8, 128), f(inp["rg_lambda"]).reshape(8, 128)], axis=0).astype(np.float32)
    shared = {
        "ck": f(inp["cache_k"]).reshape(cfg.NPOOL * 128, D),
        "cv": f(inp["cache_v"]).reshape(cfg.NPOOL * 128, D),
        "clf": f(inp["cache_logf"]).reshape(cfg.NPOOL, 128 * cfg.H),
        "consts": consts,
        "ada_w": f(inp["ada_w"]).reshape(D, 6 * D),
        "ada_b": f(inp["ada_b"]).reshape(1, 6 * D),
        "vecs": vecs,
        "normf_g": f(inp["normf_g"]).reshape(1, D),
        "w_in": f(inp["w_in"]).reshape(D, cfg.NIN),
        "b_f": f(inp["b_f"]).reshape(1, cfg.H),
        "rg_wr": f(inp["rg_wr"]).reshape(16, 64, 64),
        "rg_wi": f(inp["rg_wi"]).reshape(16, 64, 64),
        "w_proj_a": f(inp["w_proj_a"]).reshape(D, D),
        "w_proj_b": f(inp["w_proj_b"]).reshape(D, D),
        "w_o": f(inp["w_o"]).reshape(D, D),
        "w_ffn_in": f(inp["w_ffn_in"]).reshape(D, 2 * cfg.DFF),
        "w_ffn_out": f(inp["w_ffn_out"]).reshape(cfg.DFF, D),
    }
    c_prompt = f(inp["c_prompt"]); c_sample = f(inp["c_sample"])
    x_sample = f(inp["x_sample"]).reshape(-1, D)
    sconv = f(inp["state_conv"])[0]; srg = f(inp["state_rglru"])[0]
    pt = f(inp["page_table"]).astype(np.int32)
    maps = []
    for c in range(n_cores):
        b, half = c // 2, c % 2
        xa = np.zeros((cfg.S_ALL, D), np.float32)
        if half == 0:
            xa[cfg.S_PRE:] = x_prompt[b, :cfg.S_OWN]
        else:
            xa[:] = x_prompt[b]
        flags = np.zeros((128, 2), np.float32)
        flags[:, 0] = 1.0 if half == 1 else 0.0
        flags[:, 1] = 0.0 if half == 1 else NEG
        crow = np.zeros((33, D), np.float32)
        crow[0:4] = c_sample[4 * c:4 * c + 4]
        crow[32] = c_prompt[b]
        m = dict(shared)
        m.update({
            "xa": xa, "flags": flags, "crow": crow,
            "xs": np.ascontiguousarray(x_sample[4 * c:4 * c + 4]),
            "sconv": np.ascontiguousarray(sconv[4 * c:4 * c + 4]),
            "srg": np.ascontiguousarray(srg[4 * c:4 * c + 4]),
            "pt": np.ascontiguousarray(pt[4 * c:4 * c + 4].reshape(1, -1)),
        })
        maps.append(m)
    return maps


def run(cfg, inp):
    nc, P, es, dbg_outs = build_program(cfg)
    maps = prep_inputs(cfg, inp)
    used = set()
    for alloc in nc.allocations:
        if isinstance(alloc, mybir.MemoryLocationSet) and alloc.kind == "ExternalInput":
            used.add(alloc.memorylocations[0].name)
    maps = [{k: v for k, v in m.items() if k in used} for m in maps]
    res = run_bass_kernel_spmd(nc, maps, core_ids=list(range(len(maps))))
    es.close()
    return res.results


def assemble(cfg, res):
    n_cores = len(res)
    B = n_cores // 2
    S = cfg.S_OWN
    D, H, DH = cfg.D, cfg.H, cfg.DH
    cat = lambda name: np.concatenate([res[c][name] for c in range(n_cores)], axis=0)
    y = cat("y").reshape(B, 2 * S, D)
    k = cat("ko").reshape(1, B, 2 * S, H, DH)
    v = cat("vo").reshape(1, B, 2 * S, H, DH)
    lf = cat("lfo").reshape(1, B, 2 * S, H)
    conv = np.stack([res[2 * b + 1]["convo"] for b in range(B)])[None]
    rg = np.stack([res[2 * b + 1]["rgo"][0] for b in range(B)])[None]
    ys = cat("yso").reshape(-1, 1, D)
    ks = cat("kso").reshape(1, -1, 1, H, DH)
    vs = cat("vso").reshape(1, -1, 1, H, DH)
    lfs = cat("lfso").reshape(1, -1, 1, H)
    convs = cat("convso").reshape(1, -1, 3, D)
    rgs = cat("rgso").reshape(1, -1, D)
    return (y, ys, k, v, lf, conv, rg, ks, vs, lfs, convs, rgs)


def kernel(**inputs):
    cfg = Cfg()
    res = run(cfg, inputs)
    return tuple(np.ascontiguousarray(a, dtype=np.float32) for a in assemble(cfg, res))
```

```python
import numpy as np
from contextlib import ExitStack
import concourse.bass as bass
import concourse.mybir as mybir
from concourse.bass_utils import run_bass_kernel_spmd

F32 = mybir.dt.float32
BF16 = mybir.dt.bfloat16
I32 = mybir.dt.int32
AF = mybir.ActivationFunctionType
ALU = mybir.AluOpType
AX = mybir.AxisListType

NEG = -30000.0
NRING = 5
SAME_SYNC = True


class Cfg:
    def __init__(self, s_own=2048, npg=64, npool=2560, debug=()):
        self.D = 1024
        self.KC = 8
        self.H = 16
        self.DH = 64
        self.DFF = 2816
        self.NIN = 7184
        self.S_OWN = s_own
        self.S_PRE = s_own
        self.S_ALL = 2 * s_own
        self.NS = 4
        self.NPG = npg
        self.NPOOL = npool
        self.NB_ALL = self.S_ALL // 128
        self.NB_PRE = self.S_PRE // 128
        self.NB_OWN = self.S_OWN // 128
        self.NT_ALL = self.S_ALL // 512
        self.NT_PRE = self.S_PRE // 512
        self.NT_OWN = self.S_OWN // 512
        self.debug = tuple(debug)
        self.stop = 99


XR, GR, QO, KO, VO, FLO, GA, GB = 0, 1024, 2048, 3072, 4096, 5120, 5136, 6160

C_ID = 0
C_U = 128
C_ONE = 256
C_BM = 384
C_SU = 384
C_SEL32 = C_SU + 128
C_ONESEL = C_SEL32 + 128
C_SELQ = C_ONESEL + 16
C_BMK = C_SELQ + 512
C_E = C_BMK + 1024
NCP = C_E + 16
C_BM_SRC = NCP
NCONST = NCP + 2048


def make_consts():
    c = np.zeros((128, NCONST), np.float32)
    c[:, C_ID:C_ID + 128] = np.eye(128)
    j = np.arange(128)[:, None]
    t = np.arange(128)[None, :]
    c[:, C_U:C_U + 128] = (j <= t)
    c[:, C_ONE:C_ONE + 128] = 1.0
    kt = np.arange(128)[:, None]
    qt = np.arange(512)[None, :]
    for jj in range(4):
        c[:, C_BM_SRC + 512 * jj:C_BM_SRC + 512 * (jj + 1)] = np.where(qt - 128 * jj - kt >= 0, 0.0, NEG)
    c[:, C_SU:C_SU + 128] = (j > t)
    c[32, C_SEL32:C_SEL32 + 128] = 1.0
    for b in range(4):
        c[:, C_ONESEL + 4 * b + b] = 1.0
        c[b, C_SELQ + 128 * b:C_SELQ + 128 * (b + 1)] = 1.0
        c[:16, C_E + 4 * b + b] = 1.0
    for h in range(16):
        c[h, C_BMK + 64 * h:C_BMK + 64 * (h + 1)] = 1.0
    return c


class Prog:
    def __init__(self, nc):
        self.nc = nc
        self.eng = {"pe": nc.tensor, "act": nc.scalar, "dve": nc.vector, "pool": nc.gpsimd, "sp": nc.sync}
        self.pool = [nc.alloc_semaphore("ks%d" % i) for i in range(96)]
        self.allsems = list(self.pool)
        for sm in self.pool:
            nc.gpsimd.sem_clear(sm)
        nc.all_engine_barrier()
        self.sem = {e: self.pool.pop() for e in ("pe", "act", "dve", "pool")}
        self.cnt = {e: 0 for e in ("pe", "act", "dve", "pool")}
        self.known = {e: {} for e in self.eng}
        self.last_w = {}
        self.readers = {}
        self.lanes = {}
        self.out_lanes = set()
        self.nops = 0
        self.fence_need = {}
        self.rec = None

    def _sem_of(self, key):
        if key in self.sem:
            return self.sem[key]
        return self.lanes[key][0]

    def fence(self):
        for k, v in self.cnt.items():
            if v:
                self.fence_need[k] = v
        for lk, (sm, v) in self.lanes.items():
            if v:
                self.fence_need[lk] = v

    def _deps(self, engine, reads, writes):
        need = {k: v for k, v in self.fence_need.items() if k != engine}

        def add(tok, kind):
            if tok is None:
                return
            sk, val = tok
            if sk == engine:
                if engine == "pe":
                    return
                if kind == "war" or not SAME_SYNC:
                    return
            if need.get(sk, 0) < val:
                need[sk] = val

        for r in reads:
            add(self.last_w.get(r), "raw")
        for w in writes:
            add(self.last_w.get(w), "waw")
            for tok in self.readers.get(w, ()):
                add(tok, "war")
        e = self.eng[engine]
        kn = self.known[engine]
        for sk, val in need.items():
            if kn.get(sk, 0) >= val:
                continue
            e.wait_ge(self._sem_of(sk), val)
            kn[sk] = val

    def _commit(self, tok, reads, writes):
        for w in writes:
            self.last_w[w] = tok
            self.readers[w] = []
        for r in reads:
            if r in writes:
                continue
            self.readers.setdefault(r, []).append(tok)

    def op(self, engine, fn, reads=(), writes=()):
        if self.rec is not None:
            self.rec.append(("op", (engine, fn, tuple(reads), tuple(writes)), {}))
            return
        self._deps(engine, reads, writes)
        ins = fn(self.eng[engine])
        self.cnt[engine] += 1
        ins.then_inc(self.sem[engine], 1)
        self._commit((engine, self.cnt[engine]), reads, writes)
        self.nops += 1

    def dma(self, q, out, in_, reads=(), writes=(), lane=None, final=False, indirect=None):
        if self.rec is not None:
            self.rec.append(("dma", (q, out, in_), dict(reads=tuple(reads), writes=tuple(writes), lane=lane, final=final, indirect=indirect)))
            return
        if lane is None:
            lane = writes[0] if writes else reads[0]
        lk = ("dma", lane)
        if lk not in self.lanes:
            self.lanes[lk] = [self.pool.pop(), 0]
        self._deps(q, reads, writes)
        e = self.eng[q]
        if indirect is not None:
            ins = e.indirect_dma_start(out=out, out_offset=None, in_=in_, in_offset=indirect)
        else:
            ins = e.dma_start(out=out, in_=in_)
        self.lanes[lk][1] += 16
        ins.then_inc(self.lanes[lk][0], 16)
        self._commit((lk, self.lanes[lk][1]), reads, writes)
        if final:
            self.out_lanes.add(lk)
        self.nops += 1

    def replay_interleaved(self, lists, lag=0):
        pos = [0] * len(lists)
        live = True
        step = 0
        while live:
            live = False
            step += 1
            for i, l in enumerate(lists):
                if i > 0 and step <= lag * i and pos[0] < len(lists[0]):
                    live = True
                    continue
                if pos[i] < len(l):
                    kind, a, kw = l[pos[i]]
                    pos[i] += 1
                    live = True
                    if kind == "op":
                        self.op(*a)
                    else:
                        self.dma(*a, **kw)

    def finish(self):
        e = self.eng["sp"]
        for lk in sorted(self.out_lanes, key=str):
            e.wait_ge(self.lanes[lk][0], self.lanes[lk][1])
        for k in ("pe", "act", "dve", "pool"):
            if self.cnt[k]:
                e.wait_ge(self.sem[k], self.cnt[k])
        self.nc.all_engine_barrier()
        for sm in self.allsems:
            self.nc.gpsimd.sem_clear(sm)
        self.nc.all_engine_barrier()


def build_program(cfg):
    nc = bass.Bass("TRN2", target_bir_lowering=False)
    P = Prog(nc)
    D, KC, H, DH = cfg.D, cfg.KC, cfg.H, cfg.DH
    S_ALL, S_OWN, S_PRE = cfg.S_ALL, cfg.S_OWN, cfg.S_PRE
    NB_ALL, NB_PRE, NB_OWN = cfg.NB_ALL, cfg.NB_PRE, cfg.NB_OWN
    NT_ALL, NT_PRE, NT_OWN = cfg.NT_ALL, cfg.NT_PRE, cfg.NT_OWN
    NS, NPG = cfg.NS, cfg.NPG
    NPGS = NS * NPG

    def din(name, shape, dt=F32):
        return nc.dram_tensor(name, list(shape), dt, kind="ExternalInput").ap()

    def dout(name, shape, dt=F32):
        return nc.dram_tensor(name, list(shape), dt, kind="ExternalOutput").ap()

    xa = din("xa", [S_ALL, D])
    flags = din("flags", [128, 2])
    crow = din("crow", [33, D])
    xs_d = din("xs", [NS, D])
    sconv_d = din("sconv", [NS, 3, D])
    srg_d = din("srg", [NS, D])
    pt_d = din("pt", [1, NPGS], I32)
    if cfg.stop >= 5:
        ck_d = din("ck", [cfg.NPOOL * 128, D])
        cv_d = din("cv", [cfg.NPOOL * 128, D])
        clf_d = din("clf", [cfg.NPOOL, 128 * H])
    consts_d = din("consts", [128, NCONST])
    ada_w = din("ada_w", [D, 6 * D])
    ada_b = din("ada_b", [1, 6 * D])
    vecs_d = din("vecs", [128, 128])
    normf_d = din("normf_g", [1, D])
    w_in = din("w_in", [D, cfg.NIN])
    b_f = din("b_f", [1, H])
    rg_wr = din("rg_wr", [16, 64, 64])
    rg_wi = din("rg_wi", [16, 64, 64])
    w_pa = din("w_proj_a", [D, D])
    w_pb = din("w_proj_b", [D, D])
    w_o = din("w_o", [D, D])
    w_fi = din("w_ffn_in", [D, 2 * cfg.DFF])
    w_fo = din("w_ffn_out", [cfg.DFF, D])

    y_o = dout("y", [S_OWN, D])
    k_o = dout("ko", [S_OWN, D])
    v_o = dout("vo", [S_OWN, D])
    lf_o = dout("lfo", [S_OWN, H])
    conv_o = dout("convo", [3, D])
    rg_o = dout("rgo", [1, D])
    ys_o = dout("yso", [NS, D])
    ks_o = dout("kso", [NS, D])
    vs_o = dout("vso", [NS, D])
    lfs_o = dout("lfso", [NS, H])
    convs_o = dout("convso", [NS, 3, D])
    rgs_o = dout("rgso", [NS, D])
    dbg_outs = {}

    es = ExitStack()

    def sb(name, shape, dt=F32, stack=None):
        return (stack or es).enter_context(nc.sbuf_tensor("s_" + name, list(shape), dt))

    ps = [es.enter_context(nc.psum_tensor("ps%d" % i, [128, 512], F32)) for i in range(8)]
    PK = [("ps", i) for i in range(8)]

    def wview(w, c0, ncols):
        return w[:, c0:c0 + ncols].rearrange("(k p) n -> p k n", p=128)

    def dbg(name, ap, key):
        if name not in cfg.debug:
            return
        shape = list(ap.shape)
        t = nc.dram_tensor("dbg_" + name, shape, ap.dtype, kind="ExternalOutput").ap()
        dbg_outs[name] = t
        P.dma("sp", t, ap, reads=list(key) if isinstance(key, list) else [key], lane=("dbg", name), final=True)

    consts = sb("consts", [128, NCP])
    P.dma("sp", consts[:], consts_d[:, 0:NCP], writes=["consts"])
    ident_f = consts[:, C_ID:C_ID + 128]
    ones_f = consts[:, C_ONE:C_ONE + 128]
    cbf = sb("cbf", [128, 384 + 2048], BF16)
    P.op("dve", lambda e: e.tensor_copy(out=cbf[:, 0:384], in_=consts[:, 0:384]), reads=["consts"], writes=["cbf"])
    ident_b = cbf[:, C_ID:C_ID + 128]
    flg = sb("flg", [128, 2])
    P.dma("sp", flg[:], flags, writes=["flg"])
    vf = flg[:, 0:1]
    pm = flg[:, 1:2]
    vecs_tm = sb("vecs_tm", [128, 128])
    P.dma("sp", vecs_tm[:], vecs_d, writes=["vecs_tm"])
    vfm = sb("vfm", [128, 128])
    P.op("pe", lambda e: e.transpose(ps[7][:, 0:128], vecs_tm[:], ident_f), reads=["vecs_tm", "consts"], writes=[PK[7]])
    P.op("dve", lambda e: e.tensor_copy(out=vfm[:], in_=ps[7][:, 0:128]), reads=[PK[7]], writes=["vfm"])
    V_N1, V_N2, V_CW, V_CB, V_BR, V_BI, V_LAM = 48, 56, 64, 96, 104, 112, 120
    normf_b = sb("normf_b", [128, D])
    P.dma("sp", normf_b[:], normf_d.partition_broadcast(128), writes=["normf_b"])
    bfb = sb("bfb", [128, H])
    P.dma("sp", bfb[:], b_f.partition_broadcast(128), writes=["bfb"])
    g1b = sb("g1b", [128, D])
    g2b = sb("g2b", [128, D])
    g1s = sb("g1s", [NS, D])
    g2s = sb("g2s", [NS, D])
    AB = sb("AB", [128, 4, KC])
    ABs = sb("ABs", [128, 4, KC, NS])
    cl = sb("cl", [128, 2, KC])
    hsT = sb("hsT", [128, KC, NS], BF16)

    with ExitStack() as s0:
        m_fm = sb("m_fm", [128, 48, 33], stack=s0)
        mtmp = sb("mtmp", [128, 2048], stack=s0)
        P.dma("sp", mtmp[:], consts_d[:, C_BM_SRC:C_BM_SRC + 2048], writes=["mtmp"])
        P.op("dve", lambda e: e.tensor_copy(out=cbf[:, C_BM:C_BM + 2048], in_=mtmp[:]), reads=["mtmp"], writes=["cbf"])
        c_tm = sb("c_tm", [33, D], stack=s0)
        P.dma("sp", c_tm[:], crow, writes=["c_tm"])
        P.op("act", lambda e: e.activation(out=c_tm[:], in_=c_tm[:], func=AF.Silu), reads=["c_tm"], writes=["c_tm"])
        silT = sb("silT", [128, KC, 33], BF16, stack=s0)
        for kc in range(KC):
            P.op("pe", lambda e, kc=kc: e.transpose(ps[7][:, kc * 33:(kc + 1) * 33], c_tm[:, kc * 128:(kc + 1) * 128], ident_f[0:33, 0:33]),
                 reads=["c_tm", "consts"], writes=[PK[7]])
        P.op("dve", lambda e: e.tensor_copy(out=silT[:].rearrange("p k n -> p (k n)"), in_=ps[7][:, 0:KC * 33]), reads=[PK[7]], writes=["silT"])
        adab = sb("adab", [1, 6 * D], stack=s0)
        P.dma("sp", adab[:], ada_b, writes=["adab"])
        m_tm = sb("m_tm", [33, 6 * D], stack=s0)
        awb = [sb("awb%d" % i, [128, KC, 512], BF16, stack=s0) for i in range(2)]
        for j in range(12):
            wk = ("awb", j % 2)
            P.dma("pool", awb[j % 2][:], wview(ada_w, j * 512, 512), writes=[wk])
            pk = PK[j % 2]

            def mm(e, j=j):
                for kc in range(KC):
                    e.matmul(ps[j % 2][0:33, :], lhsT=silT[:, kc, :], rhs=awb[j % 2][:, kc, :], start=(kc == 0), stop=False)
                return e.matmul(ps[j % 2][0:33, :], lhsT=ones_f[0:1, 0:33], rhs=adab[0:1, j * 512:(j + 1) * 512], start=False, stop=True)
            P.op("pe", mm, reads=["silT", wk, "adab", "consts"], writes=[pk])
            P.op("act", lambda e, j=j: e.activation(out=m_tm[:, j * 512:(j + 1) * 512], in_=ps[j % 2][0:33, :], func=AF.Copy),
                 reads=[pk], writes=["m_tm"])
        for g in range(4):
            for jj in range(12):
                j = g * 12 + jj
                P.op("pe", lambda e, j=j, jj=jj, g=g: e.transpose(ps[g][:, jj * 33:(jj + 1) * 33], m_tm[:, j * 128:(j + 1) * 128], ident_f[0:33, 0:33]),
                     reads=["m_tm", "consts"], writes=[PK[g]])
            P.op("dve", lambda e, g=g: e.tensor_copy(out=m_fm[:, g * 12:(g + 1) * 12, :].rearrange("p a b -> p (a b)"), in_=ps[g][:, 0:12 * 33]),
                 reads=[PK[g]], writes=["m_fm"])
        for gi, (gt, c0) in enumerate(((g1b, 2 * D), (g2b, 5 * D))):
            for hf in range(2):
                pk = PK[4 + hf]
                P.op("pe", lambda e, c0=c0, hf=hf: e.matmul(ps[4 + hf][:, :], lhsT=consts[0:33, C_SEL32:C_SEL32 + 128],
                                                            rhs=m_tm[:, c0 + hf * 512:c0 + (hf + 1) * 512], start=True, stop=True),
                     reads=["m_tm", "consts"], writes=[pk])
                P.op("dve", lambda e, gt=gt, hf=hf: e.tensor_copy(out=gt[:, hf * 512:(hf + 1) * 512], in_=ps[4 + hf][:, :]),
                     reads=[pk], writes=["g%db" % (gi + 1)])
        P.op("dve", lambda e: e.tensor_copy(out=g1s[:], in_=m_tm[0:NS, 2 * D:3 * D]), reads=["m_tm"], writes=["g1s"])
        P.op("dve", lambda e: e.tensor_copy(out=g2s[:], in_=m_tm[0:NS, 5 * D:6 * D]), reads=["m_tm"], writes=["g2s"])
        for i, (sc_c, sh_c, vn) in enumerate(((8, 0, V_N1), (32, 24, V_N2))):
            P.op("dve", lambda e, i=i, sc_c=sc_c, vn=vn: e.scalar_tensor_tensor(
                out=AB[:, 2 * i, :], in0=m_fm[:, sc_c:sc_c + KC, 32], scalar=1.0, in1=vfm[:, vn:vn + KC], op0=ALU.add, op1=ALU.mult),
                reads=["m_fm", "vfm"], writes=["AB"])
            P.op("dve", lambda e, i=i, sh_c=sh_c: e.tensor_copy(out=AB[:, 2 * i + 1, :], in_=m_fm[:, sh_c:sh_c + KC, 32]),
                 reads=["m_fm"], writes=["AB"])
            P.op("dve", lambda e, i=i, sc_c=sc_c, vn=vn: e.scalar_tensor_tensor(
                out=ABs[:, 2 * i, :, :], in0=m_fm[:, sc_c:sc_c + KC, 0:NS], scalar=1.0,
                in1=vfm[:, vn:vn + KC].unsqueeze(2).to_broadcast([128, KC, NS]), op0=ALU.add, op1=ALU.mult),
                reads=["m_fm", "vfm"], writes=["ABs"])
            P.op("dve", lambda e, i=i, sh_c=sh_c: e.tensor_copy(out=ABs[:, 2 * i + 1, :, :], in_=m_fm[:, sh_c:sh_c + KC, 0:NS]),
                 reads=["m_fm"], writes=["ABs"])
    P.fence()
    P.op("act", lambda e: e.activation(out=cl[:, 0, :], in_=vfm[:, V_LAM:V_LAM + KC], func=AF.Exp, scale=-1.0), reads=["vfm"], writes=["cl"])
    P.op("act", lambda e: e.activation(out=cl[:, 0, :], in_=cl[:, 0, :], func=AF.Ln, bias=1.0), reads=["cl"], writes=["cl"])
    P.op("dve", lambda e: e.tensor_scalar(out=cl[:, 1, :], in0=cl[:, 0, :], scalar1=-16.0, scalar2=None, op0=ALU.mult), reads=["cl"], writes=["cl"])
    P.op("dve", lambda e: e.tensor_scalar(out=cl[:, 0, :], in0=cl[:, 0, :], scalar1=-8.0, scalar2=None, op0=ALU.mult), reads=["cl"], writes=["cl"])

    dbg("AB", AB[:], "AB")
    dbg("cl", cl[:], "cl")
    dbg("g1b", g1b[:], "g1b")
    if cfg.stop == 0:
        P.finish()
        return nc, P, es, dbg_outs

    lfs_tm = sb("lfs_tm", [NS, H])
    hnT = sb("hnT", [128, KC, NS])
    xrsT = sb("xrsT", [128, KC, NS])
    omvf = sb("omvf", [128, 1])
    mergedsT = sb("mergedsT", [128, KC, NS], BF16)
    oT_d = nc.dram_tensor("oT_scr", [128, KC, S_OWN], BF16, kind="Internal").ap()
    ya_d = nc.dram_tensor("ya_scr", [128, KC, S_OWN], BF16, kind="Internal").ap()
    mg_d = nc.dram_tensor("mg_scr", [128, KC, S_OWN], BF16, kind="Internal").ap()
    sH = ExitStack()
    es.callback(sH.close)
    hT = sb("hT", [128, KC, S_ALL], BF16, stack=sH)
    osT = sb("osT", [128, KC, NS], BF16, stack=sH)
    yainsT = sb("yainsT", [128, KC, NS], BF16, stack=sH)
    qs_tm = sb("qs_tm", [NS, D], stack=sH)
    ks_tm = sb("ks_tm", [NS, D], stack=sH)
    vs_tm = sb("vs_tm", [NS, D], stack=sH)
    os_tm = sb("os_tm", [NS, D], stack=sH)
    EPS = 1e-6

    def norm_block(stk, x_tile, xkey, nt, stat, statkey, xn, xnkey, junk, idx):
        P.op("act", lambda e: e.activation(out=junk[0:nt, :], in_=x_tile, func=AF.Square, accum_out=stat[0:nt, 0:1]),
             reads=[xkey], writes=[statkey, ("junk", idx)])
        if idx == 0 and nt == 128:
            dbg("junk0", junk[:, :], ("junk", 0))
        P.op("act", lambda e: e.activation(out=stat[0:nt, 1:2], in_=stat[0:nt, 0:1], func=AF.Ln, scale=1.0 / D, bias=EPS),
             reads=[statkey], writes=[statkey])
        P.op("act", lambda e: e.activation(out=stat[0:nt, 1:2], in_=stat[0:nt, 1:2], func=AF.Exp, scale=-0.5),
             reads=[statkey], writes=[statkey])
        P.op("act", lambda e: e.activation(out=xn[0:nt, :], in_=x_tile, func=AF.Copy, scale=stat[0:nt, 1:2]),
             reads=[xkey, statkey], writes=[xnkey])

    def to_fm(xn, xnkey, nt, bank, ident):
        if xn.dtype == BF16:
            pv = ps[bank][:].bitcast(BF16)
        else:
            pv = ps[bank][:]

        def tr(e):
            ins = None
            for kc in range(KC):
                ins = e.transpose(pv[:, kc * nt:(kc + 1) * nt], xn[0:nt, kc * 128:(kc + 1) * 128], ident[0:nt, 0:nt])
            return ins
        P.op("pe", tr, reads=[xnkey, "consts", "cbf"], writes=[PK[bank]])
        return pv

    with ExitStack() as s1:
        xt = [sb("xt%d" % i, [128, D], stack=s1) for i in range(3)]
        xn = [sb("xn%d" % i, [128, D], BF16, stack=s1) for i in range(2)]
        junk = sb("junk", [128, D], F32, stack=s1)
        st1 = sb("st1", [128, 2 * (NB_ALL + 1)], stack=s1)
        for b in range(NB_ALL + 1):
            P.op("dve", lambda e, b=b: e.memset(st1[:, 2 * b:2 * b + 2], 0.0), writes=[("st1", b)])
        P.op("act", lambda e: e.activation(out=junk[:, 0:128], in_=consts[:, 0:128], func=AF.Square, accum_out=st1[:, 0:1]),
             reads=["consts"], writes=[("st1", 0)])
        P.op("dve", lambda e: e.memset(st1[:, 0:2], 0.0), writes=[("st1", 0)])
        for b in range(NB_ALL):
            xk = ("xt", b % 3)
            P.dma("sp", xt[b % 3][:], xa[b * 128:(b + 1) * 128, :], writes=[xk])
            nk = ("xn", b % 2)
            norm_block(s1, xt[b % 3][:], xk, 128, st1[:, 2 * b:2 * b + 2], ("st1", b), xn[b % 2], nk, junk, b)
            if b == 0:
                dbg("xt0", xt[0][:], xk)
                dbg("xn0", xn[0][:], nk)
                dbg("st0", st1[:, 0:2], ("st1", 0))
            bank = b % 2
            pv = to_fm(xn[b % 2], nk, 128, bank, ident_b)
            for kc in range(KC):
                P.op("dve", lambda e, kc=kc, pv=pv, b=b: e.tensor_scalar(
                    out=hT[:, kc, b * 128:(b + 1) * 128], in0=pv[:, kc * 128:(kc + 1) * 128],
                    scalar1=AB[:, 0, kc:kc + 1], scalar2=AB[:, 1, kc:kc + 1], op0=ALU.mult, op1=ALU.add),
                    reads=[PK[bank], "AB"], writes=[("hT", b)])
        xs_t = sb("xs_t", [NS, D], stack=s1)
        P.dma("sp", xs_t[:], xs_d, writes=["xs_t"])
        xns = sb("xns", [NS, D], BF16, stack=s1)
        norm_block(s1, xs_t[:], "xs_t", NS, st1[:, 2 * NB_ALL:2 * NB_ALL + 2], ("st1", NB_ALL), xns, "xns", junk, 999)
        pv = to_fm(xns, "xns", NS, 2, ident_b)
        tmp_s = sb("tmp_s", [128, KC, NS], stack=s1)
        P.op("dve", lambda e: e.tensor_tensor(out=tmp_s[:], in0=pv[:, 0:KC * NS].rearrange("p (k n) -> p k n", n=NS), in1=ABs[:, 0, :, :], op=ALU.mult),
             reads=[PK[2], "ABs"], writes=["tmp_s"])
        P.op("dve", lambda e: e.tensor_tensor(out=hsT[:], in0=tmp_s[:], in1=ABs[:, 1, :, :], op=ALU.add),
             reads=["tmp_s", "ABs"], writes=["hsT"])
    P.fence()
    dbg("hT", hT[:, :, :], [("hT", b) for b in range(NB_ALL)])
    dbg("hsT", hsT[:], "hsT")
    if cfg.stop == 1:
        P.finish()
        return nc, P, es, dbg_outs

    def wdma(tile_ap, w, c0, ncols, key):
        P.dma("pool", tile_ap, wview(w, c0, ncols), writes=[key])

    with ExitStack() as s2:
        NBH = NB_ALL * H
        GTk = sb("GTk", [48, S_ALL], BF16, stack=s2)
        GTq = sb("GTq", [48, S_OWN], BF16, stack=s2)
        s2f = ExitStack()
        wfl = sb("wfl", [128, KC, H], BF16, stack=s2f)
        wdma(wfl[:], w_in, FLO, H, "wfl")
        sp_t = sb("sp_t", [128, NB_ALL, H], stack=s2f)
        G = sb("G", [128, NB_ALL, H], stack=s2f)
        def flmm(e):
            ins = None
            for b in range(NB_ALL):
                for kc in range(KC):
                    ins = e.matmul(ps[7][:, b * H:(b + 1) * H], lhsT=hT[:, kc, b * 128:(b + 1) * 128], rhs=wfl[:, kc, :],
                                   start=(kc == 0), stop=(kc == KC - 1))
            return ins
        P.op("pe", flmm, reads=[("hT", b) for b in range(NB_ALL)] + ["wfl"], writes=[PK[7]])
        P.op("dve", lambda e: e.tensor_tensor(out=sp_t[:], in0=ps[7][:, 0:NBH].rearrange("p (b h) -> p b h", h=H),
                                              in1=bfb[:].unsqueeze(1).to_broadcast([128, NB_ALL, H]), op=ALU.add),
             reads=[PK[7], "bfb"], writes=["sp_t"])
        P.op("act", lambda e: e.activation(out=sp_t[:], in_=sp_t[:], func=AF.Exp, scale=-1.0), reads=["sp_t"], writes=["sp_t"])
        P.op("act", lambda e: e.activation(out=sp_t[:], in_=sp_t[:], func=AF.Ln, bias=1.0), reads=["sp_t"], writes=["sp_t"])
        def flmm_s(e):
            ins = None
            for kc in range(KC):
                ins = e.matmul(ps[6][0:NS, 0:H], lhsT=hsT[:, kc, :], rhs=wfl[:, kc, :], start=(kc == 0), stop=(kc == KC - 1))
            return ins
        P.op("pe", flmm_s, reads=["hsT", "wfl"], writes=[PK[6]])
        P.op("dve", lambda e: e.tensor_tensor(out=lfs_tm[:], in0=ps[6][0:NS, 0:H], in1=bfb[0:NS, :], op=ALU.add),
             reads=[PK[6], "bfb"], writes=["lfs_tm"])
        P.op("act", lambda e: e.activation(out=lfs_tm[:], in_=lfs_tm[:], func=AF.Exp, scale=-1.0), reads=["lfs_tm"], writes=["lfs_tm"])
        P.op("act", lambda e: e.activation(out=lfs_tm[:], in_=lfs_tm[:], func=AF.Ln, bias=1.0), reads=["lfs_tm"], writes=["lfs_tm"])
        P.op("dve", lambda e: e.tensor_scalar(out=lfs_tm[:], in0=lfs_tm[:], scalar1=-1.0, scalar2=None, op0=ALU.mult),
             reads=["lfs_tm"], writes=["lfs_tm"])
        P.dma("sp", lfs_o, lfs_tm[:], reads=["lfs_tm"], lane="lfs_out", final=True)
        lfst = sb("lfst", [128, NB_OWN, H], stack=s2f)
        P.op("dve", lambda e: e.tensor_scalar(out=lfst[:], in0=sp_t[:, NB_PRE:, :], scalar1=-1.0, scalar2=None, op0=ALU.mult),
             reads=["sp_t"], writes=["lfst"])
        P.dma("sp", lf_o.rearrange("(b p) h -> p b h", p=128), lfst[:], reads=["lfst"], lane="lf_out", final=True)
        spf = sp_t[:].rearrange("p b h -> p (b h)")
        P.op("pe", lambda e: e.matmul(ps[6][:, 0:NBH], lhsT=consts[:, C_U:C_U + 128], rhs=spf, start=True, stop=True),
             reads=["sp_t", "consts"], writes=[PK[6]])
        P.op("pe", lambda e: e.matmul(ps[7][:, 0:NBH], lhsT=ones_f, rhs=spf, start=True, stop=True),
             reads=["sp_t", "consts"], writes=[PK[7]])
        TA = sb("TA", [128, NB_ALL, H], stack=s2f)
        TB = sb("TB", [128, NB_ALL, H], stack=s2f)
        P.op("dve", lambda e: e.tensor_copy(out=TA[:].rearrange("p b h -> p (b h)"), in_=ps[7][:, 0:NBH]), reads=[PK[7]], writes=["TA"])
        cur, oth, ck_, ok_ = TA, TB, "TA", "TB"
        st = 1
        while st < NB_ALL:
            P.op("dve", lambda e, cur=cur, oth=oth, st=st: e.tensor_copy(out=oth[:, 0:st, :], in_=cur[:, 0:st, :]), reads=[ck_], writes=[ok_])
            P.op("dve", lambda e, cur=cur, oth=oth, st=st: e.tensor_tensor(out=oth[:, st:, :], in0=cur[:, st:, :], in1=cur[:, 0:NB_ALL - st, :], op=ALU.add),
                 reads=[ck_], writes=[ok_])
            cur, oth, ck_, ok_ = oth, cur, ok_, ck_
            st *= 2
        P.op("dve", lambda e, cur=cur: e.tensor_tensor(out=G[:].rearrange("p b h -> p (b h)"), in0=ps[6][:, 0:NBH],
                                                      in1=cur[:].rearrange("p b h -> p (b h)"), op=ALU.add),
             reads=[PK[6], ck_], writes=["G"])
        P.op("dve", lambda e: e.tensor_tensor(out=G[:].rearrange("p b h -> p (b h)"), in0=G[:].rearrange("p b h -> p (b h)"),
                                              in1=ps[7][:, 0:NBH], op=ALU.subtract),
             reads=[PK[7], "G"], writes=["G"])
        Gk = sb("Gk", [128, NB_ALL, H], stack=s2f)
        P.op("dve", lambda e: e.tensor_scalar(out=Gk[:, 0:NB_PRE, :], in0=G[:, 0:NB_PRE, :], scalar1=pm, scalar2=None, op0=ALU.add),
             reads=["G", "flg"], writes=["Gk"])
        P.op("dve", lambda e: e.tensor_copy(out=Gk[:, NB_PRE:, :], in_=G[:, NB_PRE:, :]), reads=["G"], writes=["Gk"])
        Gq = sb("Gq", [128, NB_OWN, H], stack=s2f)
        P.op("dve", lambda e: e.tensor_scalar(out=Gq[:], in0=G[:, NB_PRE:, :], scalar1=-1.0, scalar2=None, op0=ALU.mult),
             reads=["G"], writes=["Gq"])
        pk_ = sb("pk_", [128, NB_ALL, H, 3], BF16, stack=s2f)
        pq_ = sb("pq_", [128, NB_OWN, H, 3], BF16, stack=s2f)
        rr = sb("rr", [128, NB_ALL, H], stack=s2f)

        def split3(src, srckey, dst, dstkey, nb):
            P.op("dve", lambda e: e.tensor_copy(out=dst[:, :, :, 0], in_=src[:, 0:nb, :]), reads=[srckey], writes=[dstkey])
            P.op("dve", lambda e: e.tensor_tensor(out=rr[:, 0:nb, :], in0=src[:, 0:nb, :], in1=dst[:, :, :, 0], op=ALU.subtract),
                 reads=[srckey, dstkey], writes=["rr"])
            P.op("dve", lambda e: e.tensor_copy(out=dst[:, :, :, 1], in_=rr[:, 0:nb, :]), reads=["rr"], writes=[dstkey])
            P.op("dve", lambda e: e.tensor_tensor(out=rr[:, 0:nb, :], in0=rr[:, 0:nb, :], in1=dst[:, :, :, 1], op=ALU.subtract),
                 reads=["rr", dstkey], writes=["rr"])
            P.op("dve", lambda e: e.tensor_copy(out=dst[:, :, :, 2], in_=rr[:, 0:nb, :]), reads=["rr"], writes=[dstkey])
        split3(Gk, "Gk", pk_, "pk_", NB_ALL)
        split3(Gq, "Gq", pq_, "pq_", NB_OWN)
        for (src, srckey, dstT, dkey, nb) in ((pk_, "pk_", GTk, "GTk", NB_ALL), (pq_, "pq_", GTq, "GTq", NB_OWN)):
            gs = min(8, nb)
            for g0 in range(0, nb, gs):
                bank = 4 + (g0 // 8) % 2
                pvb = ps[bank][:].bitcast(BF16)

                def trg(e, src=src, g0=g0, pvb=pvb, gs=gs):
                    ins = None
                    for b in range(g0, g0 + gs):
                        ins = e.transpose(pvb[0:48, (b - g0) * 128:(b - g0 + 1) * 128],
                                          src[:, b, :, :].rearrange("p h c -> p (h c)"), ident_b)
                    return ins
                P.op("pe", trg, reads=[srckey, "cbf"], writes=[PK[bank]])
                P.op("dve", lambda e, dstT=dstT, g0=g0, pvb=pvb, gs=gs: e.tensor_copy(out=dstT[:, g0 * 128:(g0 + gs) * 128], in_=pvb[0:48, 0:gs * 128]),
                     reads=[PK[bank]], writes=[dkey])

        s2f.close()
        P.fence()
        qa = [sb("qa%d" % i, [128, S_OWN], BF16, stack=s2) for i in range(2)]
        ka = [sb("ka%d" % i, [128, S_ALL], BF16, stack=s2) for i in range(2)]
        for i in range(2):
            P.op("dve", lambda e, i=i: e.memset(qa[i][64:128, :], 1.0), writes=[("qa", i)])
            P.op("dve", lambda e, i=i: e.memset(ka[i][64:128, :], 1.0), writes=[("ka", i)])
        vaug = [sb("vaug%d" % i, [128, NB_ALL, 128], BF16, stack=s2) for i in range(2)]
        P.op("pool", lambda e: e.memset(vaug[0][:, :, 64:128], 1.0), writes=[("vaug", 0)])
        P.op("pool", lambda e: e.memset(vaug[1][:, :, 0:64], 1.0), writes=[("vaug", 1)])
        wqk = [sb("wqk%d" % i, [128, KC, 128], BF16, stack=s2) for i in range(2)]
        wkv = [sb("wkv%d" % i, [128, KC, 256], BF16, stack=s2) for i in range(2)]
        kvst = [sb("kvst%d" % i, [128, 2, 256], stack=s2) for i in range(2)]
        PT = [sb("PT%d" % i, [128, 512], BF16, stack=s2) for i in range(3)]
        rl = sb("rl", [128, 512], stack=s2)
        ostg = [sb("ostg%d" % i, [128, 512], BF16, stack=s2) for i in range(2)]
        kvs_st = sb("kvs_st", [NS, 256], stack=s2)
        bmb = [cbf[:, C_BM + 512 * j:C_BM + 512 * (j + 1)] for j in range(4)]
        kvcnt = [0]

        def pair_kv(pr):
            w = wkv[pr % 2]
            wk = ("wkv", pr % 2)
            wdma(w[:, :, 0:128], w_in, KO + pr * 128, 128, wk)
            wdma(w[:, :, 128:256], w_in, VO + pr * 128, 128, wk)
            import os
            for g0 in range(0, 0 if os.environ.get("ONLYOWN") else NB_PRE, 4):
                bank = 5 + (kvcnt[0] % 2)
                kvcnt[0] += 1

                def mm(e, g0=g0, bank=bank):
                    ins = None
                    for b in range(g0, g0 + 4):
                        for kc in range(KC):
                            ins = e.matmul(ps[bank][:, (b - g0) * 128:(b - g0 + 1) * 128], lhsT=hT[:, kc, b * 128:(b + 1) * 128],
                                           rhs=w[:, kc, 128:256], start=(kc == 0), stop=(kc == KC - 1))
                    return ins
                P.op("pe", mm, reads=[("hT", b) for b in range(g0, g0 + 4)] + [wk], writes=[PK[bank]])
                pv4 = ps[bank][:, :].rearrange("p (b c) -> p b c", c=128)
                P.op("dve", lambda e, g0=g0, pv4=pv4: e.tensor_copy(out=vaug[0][:, g0:g0 + 4, 0:64], in_=pv4[:, :, 0:64]),
                     reads=[PK[bank]], writes=[("vaug", 0)])
                P.op("dve", lambda e, g0=g0, pv4=pv4: e.tensor_copy(out=vaug[1][:, g0:g0 + 4, 64:128], in_=pv4[:, :, 64:128]),
                     reads=[PK[bank]], writes=[("vaug", 1)])
            import os
            BIS = int(os.environ.get("BIS", "99"))
            if BIS <= 1:
                return
            P.fence()
            for g0 in range(NB_PRE, NB_ALL, 2):
                bank = 5 + (kvcnt[0] % 2)
                if os.environ.get("BANK5"):
                    bank = 5
                sl = kvcnt[0] % 2
                kvcnt[0] += 1

                def mm2(e, g0=g0, bank=bank):
                    ins = None
                    for b in range(g0, g0 + 2):
                        for kc in range(KC):
                            if os.environ.get("VONLY"):
                                ins = e.matmul(ps[bank][:, (b - g0) * 256 + 128:(b - g0 + 1) * 256], lhsT=hT[:, kc, b * 128:(b + 1) * 128],
                                               rhs=w[:, kc, 128:256], start=(kc == 0), stop=(kc == KC - 1))
                            else:
                                ins = e.matmul(ps[bank][:, (b - g0) * 256:(b - g0 + 1) * 256], lhsT=hT[:, kc, b * 128:(b + 1) * 128],
                                               rhs=w[:, kc, :], start=(kc == 0), stop=(kc == KC - 1))
                    return ins
                P.op("pe", mm2, reads=[("hT", b) for b in range(g0, g0 + 2)] + [wk], writes=[PK[bank]])
                if os.environ.get("NOEVAC"):
                    continue
                P.op("act", lambda e, sl=sl, bank=bank: e.activation(out=kvst[sl][:].rearrange("p b c -> p (b c)"), in_=ps[bank][:, :], func=AF.Copy),
                     reads=[PK[bank]], writes=[("kvst", sl)])
                P.op("dve", lambda e, g0=g0, sl=sl: e.tensor_copy(out=vaug[0][:, g0:g0 + 2, 0:64], in_=kvst[sl][:, :, 128:192]),
                     reads=[("kvst", sl)], writes=[("vaug", 0)])
                P.op("dve", lambda e, g0=g0, sl=sl: e.tensor_copy(out=vaug[1][:, g0:g0 + 2, 64:128], in_=kvst[sl][:, :, 192:256]),
                     reads=[("kvst", sl)], writes=[("vaug", 1)])
                ob = g0 - NB_PRE
                if BIS <= 2:
                    continue
                P.dma("sp", k_o[ob * 128:(ob + 2) * 128, pr * 128:(pr + 1) * 128].rearrange("(b p) c -> p b c", p=128),
                      kvst[sl][:, :, 0:128], reads=[("kvst", sl)], lane=("kvst", sl), final=True)
                P.dma("sp", v_o[ob * 128:(ob + 2) * 128, pr * 128:(pr + 1) * 128].rearrange("(b p) c -> p b c", p=128),
                      kvst[sl][:, :, 128:256], reads=[("kvst", sl)], lane=("kvst", sl), final=True)
            if BIS <= 3:
                return
            def mms(e):
                ins = None
                for kc in range(KC):
                    ins = e.matmul(ps[7][0:NS, 0:256], lhsT=hsT[:, kc, :], rhs=w[:, kc, :], start=(kc == 0), stop=(kc == KC - 1))
                return ins
            P.op("pe", mms, reads=["hsT", wk], writes=[PK[7]])
            P.op("dve", lambda e: e.tensor_copy(out=ks_tm[:, pr * 128:(pr + 1) * 128], in_=ps[7][0:NS, 0:128]), reads=[PK[7]], writes=["ks_tm"])
            P.op("dve", lambda e: e.tensor_copy(out=vs_tm[:, pr * 128:(pr + 1) * 128], in_=ps[7][0:NS, 128:256]), reads=[PK[7]], writes=["vs_tm"])

        qkcnt = [0]

        def head_qk(h):
            import os
            BIS = int(os.environ.get("BIS", "99"))
            if BIS <= 4:
                return
            w = wqk[h % 2]
            wk = ("wqk", h % 2)
            wdma(w[:, :, 0:64], w_in, QO + h * 64, 64, wk)
            wdma(w[:, :, 64:128], w_in, KO + h * 64, 64, wk)
            qk_, kk_ = ("qa", h % 2), ("ka", h % 2)
            for t in range(NT_ALL):
                bank = 5 + (qkcnt[0] % 2)
                qkcnt[0] += 1

                def mm(e, t=t, bank=bank):
                    ins = None
                    for kc in range(KC):
                        ins = e.matmul(ps[bank][:, :], lhsT=w[:, kc, :], rhs=hT[:, kc, t * 512:(t + 1) * 512], start=(kc == 0), stop=(kc == KC - 1))
                    return ins
                P.op("pe", mm, reads=[("hT", b) for b in range(4 * t, 4 * t + 4)] + [wk], writes=[PK[bank]])
                P.op("dve", lambda e, t=t, bank=bank: e.tensor_copy(out=ka[h % 2][0:64, t * 512:(t + 1) * 512], in_=ps[bank][64:128, :]),
                     reads=[PK[bank]], writes=[kk_])
                if t >= NT_PRE:
                    P.op("dve", lambda e, t=t, bank=bank: e.tensor_scalar(out=qa[h % 2][0:64, (t - NT_PRE) * 512:(t - NT_PRE + 1) * 512], in0=ps[bank][0:64, :],
                                                                      scalar1=0.125, scalar2=None, op0=ALU.mult),
                         reads=[PK[bank]], writes=[qk_])
            import os
            if not os.environ.get("SKIP_AUG"):
                P.dma("sp", qa[h % 2][64:67, :], GTq[3 * h:3 * h + 3, :], reads=["GTq"], writes=[qk_], lane=qk_)
                P.dma("sp", ka[h % 2][67:70, :], GTk[3 * h:3 * h + 3, :], reads=["GTk"], writes=[kk_], lane=kk_)
            def mms(e):
                ins = None
                for kc in range(KC):
                    ins = e.matmul(ps[7][0:NS, 0:128], lhsT=hsT[:, kc, :], rhs=w[:, kc, :], start=(kc == 0), stop=(kc == KC - 1))
                return ins
            P.op("pe", mms, reads=["hsT", wk], writes=[PK[7]])
            P.op("dve", lambda e: e.tensor_scalar(out=qs_tm[:, h * 64:(h + 1) * 64], in0=ps[7][0:NS, 0:64], scalar1=0.125, scalar2=None, op0=ALU.mult),
                 reads=[PK[7]], writes=["qs_tm"])

        scnt = [0]

        def head_attn(h):
            qk_, kk_ = ("qa", h % 2), ("ka", h % 2)
            va = vaug[h % 2]
            vk = ("vaug", h % 2)
            for i in range(NT_OWN):
                nkb = NB_PRE + 4 * i + 4
                obank = 3 + (i % 2)
                ok = PK[obank]
                slots = []

                def S_step(kb):
                    sl = scnt[0] % 3
                    scnt[0] += 1
                    j = kb - (NB_PRE + 4 * i)

                    def mm(e):
                        ins = e.matmul(ps[sl][:, :], lhsT=ka[h % 2][0:70, kb * 128:(kb + 1) * 128], rhs=qa[h % 2][0:70, i * 512:(i + 1) * 512],
                                       start=True, stop=(j < 0))
                        if j >= 0:
                            ins = e.matmul(ps[sl][:, :], lhsT=ident_b, rhs=bmb[j], start=False, stop=True)
                        return ins
                    P.op("pe", mm, reads=[qk_, kk_, "cbf"], writes=[PK[sl]])
                    P.op("act", lambda e: e.activation(out=PT[sl][:, :], in_=ps[sl][:, :], func=AF.Exp), reads=[PK[sl]], writes=[("PT", sl)])
                    return sl

                def PV_step(kb, sl):
                    P.op("pe", lambda e: e.matmul(ps[obank][:, :], lhsT=va[:, kb, :], rhs=PT[sl][:, :], start=(kb == 0), stop=(kb == nkb - 1)),
                         reads=[("PT", sl), vk], writes=[ok])
                pend = []
                for kb in range(nkb):
                    pend.append((kb, S_step(kb)))
                    if len(pend) > 2:
                        PV_step(*pend.pop(0))
                while pend:
                    PV_step(*pend.pop(0))
                olo, llo = (0, 64) if h % 2 == 0 else (64, 0)
                P.op("act", lambda e: e.activation(out=rl[llo:llo + 64, :], in_=ps[obank][llo:llo + 64, :], func=AF.Ln), reads=[ok], writes=["rl"])
                P.op("act", lambda e: e.activation(out=rl[llo:llo + 64, :], in_=rl[llo:llo + 64, :], func=AF.Exp, scale=-1.0), reads=["rl"], writes=["rl"])
                osl = (h * NT_OWN + i) % 2
                P.op("dve", lambda e: e.tensor_tensor(out=ostg[osl][olo:olo + 64, :], in0=ps[obank][olo:olo + 64, :],
                                                      in1=rl[llo:llo + 64, :], op=ALU.mult),
                     reads=[ok, "rl"], writes=[("ostg", osl)])
                P.dma("sp", oT_d[olo:olo + 64, h // 2, i * 512:(i + 1) * 512], ostg[osl][olo:olo + 64, :], reads=[("ostg", osl)], writes=["oT_d"], lane=("ostg", osl))

        for h in range(H):
            if h % 2 == 0:
                pair_kv(h // 2)
            head_qk(h)
            if cfg.stop == 1.5:
                dbg("qa0", qa[0][0:70, :], ("qa", 0))
                dbg("ka0", ka[0][0:70, :], ("ka", 0))
                dbg("vaug0", vaug[0][:], ("vaug", 0))
                P.fence()
                P.finish()
                return nc, P, es, dbg_outs
            head_attn(h)
            if cfg.stop == 1.7:
                break
    P.fence()
    dbg("qs_tm", qs_tm[:], "qs_tm")
    if cfg.stop == 2:
        P.finish()
        return nc, P, es, dbg_outs
    P.op("dve", lambda e: e.tensor_scalar(out=omvf[:], in0=vf, scalar1=-1.0, scalar2=1.0, op0=ALU.mult, op1=ALU.add), reads=["flg"], writes=["omvf"])
    with ExitStack() as s3:
        sc_tm = sb("sc_tm", [NS, 3 * D], stack=s3)
        P.dma("sp", sc_tm[:], sconv_d.rearrange("b j d -> b (j d)"), writes=["sc_tm"])
        h0_tm = sb("h0_tm", [NS, D], stack=s3)
        P.dma("sp", h0_tm[:], srg_d, writes=["h0_tm"])
        scT = sb("scT", [128, 3, KC, NS], stack=s3)
        h0T = sb("h0T", [128, KC, NS], stack=s3)

        def trs(e):
            ins = None
            for j in range(3):
                for kc in range(KC):
                    ins = e.transpose(ps[7][:, (j * KC + kc) * NS:(j * KC + kc + 1) * NS], sc_tm[:, j * D + kc * 128:j * D + (kc + 1) * 128], ident_f[0:NS, 0:NS])
            for kc in range(KC):
                ins = e.transpose(ps[7][:, (3 * KC + kc) * NS:(3 * KC + kc + 1) * NS], h0_tm[:, kc * 128:(kc + 1) * 128], ident_f[0:NS, 0:NS])
            return ins
        P.op("pe", trs, reads=["sc_tm", "h0_tm", "consts"], writes=[PK[7]])
        P.op("dve", lambda e: e.tensor_copy(out=scT[:].rearrange("p j k n -> p (j k n)"), in_=ps[7][:, 0:3 * KC * NS]), reads=[PK[7]], writes=["scT"])
        P.op("dve", lambda e: e.tensor_copy(out=h0T[:].rearrange("p k n -> p (k n)"), in_=ps[7][:, 3 * KC * NS:4 * KC * NS]), reads=[PK[7]], writes=["h0T"])
        P.dma("sp", convs_o[:, 0:2, :], sconv_d[:, 1:3, :], lane="convs_a", final=True)

        wxg = [sb("wxg%d" % i, [128, KC, 256], BF16, stack=s3) for i in range(2)]
        wrb = [sb("wrb%d" % i, [128, 2, 128], BF16, stack=s3) for i in range(2)]
        for i in range(2):
            P.op("dve", lambda e, i=i: e.memset(wrb[i][:], 0.0), writes=[("wrb", i)])
        xrp = [sb("xrp%d" % i, [128, 515], stack=s3) for i in range(2)]
        xc = [sb("xc%d" % i, [128, 512], stack=s3) for i in range(2)]
        xcb = [sb("xcb%d" % i, [128, 512], BF16, stack=s3) for i in range(2)]
        rg = [sb("rg%d" % i, [128, 512], stack=s3) for i in range(2)]
        ig = [sb("ig%d" % i, [128, 512], stack=s3) for i in range(2)]
        at = [sb("at%d" % i, [128, 512], stack=s3) for i in range(2)]
        mt = [sb("mt%d" % i, [128, 512], stack=s3) for i in range(2)]
        ut = [sb("ut%d" % i, [128, 512], stack=s3) for i in range(2)]
        hh = [sb("hh%d" % i, [128, 512], stack=s3) for i in range(2)]
        gl = [sb("gl%d" % i, [128, 512], stack=s3) for i in range(2)]
        ystg = [sb("ystg%d" % i, [128, 512], BF16, stack=s3) for i in range(2)]
        cnt2 = [0]

        def rg_tile(c, w, wk, wb, wbk, src_rhs, srckeys, n, prev, is_prefix, first_own, sample):
            sl = cnt2[0] % 2
            cnt2[0] += 1
            K_ = lambda nm: (nm, sl)
            cw = lambda j: vfm[:, V_CW + j * 8 + c:V_CW + j * 8 + c + 1]
            cb = vfm[:, V_CB + c:V_CB + c + 1]
            def mm(e):
                ins = None
                for kc in range(KC):
                    ins = e.matmul(ps[0][:, 0:n], lhsT=w[:, kc, 0:128], rhs=src_rhs(kc), start=(kc == 0), stop=(kc == KC - 1))
                return ins
            P.op("pe", mm, reads=srckeys + [wk], writes=[PK[0]])
            if sample:
                P.op("dve", lambda e: e.tensor_copy(out=xrsT[:, c, :], in_=ps[0][:, 0:n]), reads=[PK[0]], writes=["xrsT"])
                P.op("dve", lambda e: e.tensor_scalar(out=xc[sl][:, 0:n], in0=ps[0][:, 0:n], scalar1=cw(3), scalar2=cb, op0=ALU.mult, op1=ALU.add),
                     reads=[PK[0], "vfm"], writes=[K_("xc")])
                for j in range(3):
                    P.op("dve", lambda e, j=j: e.scalar_tensor_tensor(out=xc[sl][:, 0:n], in0=scT[:, j, c, :], scalar=cw(j), in1=xc[sl][:, 0:n],
                                                                 op0=ALU.mult, op1=ALU.add),
                         reads=["scT", K_("xc"), "vfm"], writes=[K_("xc")])
            else:
                if is_prefix:
                    P.op("dve", lambda e: e.tensor_scalar(out=xrp[sl][:, 3:515], in0=ps[0][:, :], scalar1=vf, scalar2=None, op0=ALU.mult),
                         reads=[PK[0], "flg"], writes=[K_("xrp")])
                else:
                    P.op("dve", lambda e: e.tensor_copy(out=xrp[sl][:, 3:515], in_=ps[0][:, :]), reads=[PK[0]], writes=[K_("xrp")])
                if prev is None:
                    P.op("dve", lambda e: e.memset(xrp[sl][:, 0:3], 0.0), writes=[K_("xrp")])
                else:
                    P.op("dve", lambda e: e.tensor_copy(out=xrp[sl][:, 0:3], in_=xrp[prev][:, 512:515]), reads=[("xrp", prev)], writes=[K_("xrp")])
                P.op("dve", lambda e: e.tensor_scalar(out=xc[sl][:], in0=xrp[sl][:, 3:515], scalar1=cw(3), scalar2=cb, op0=ALU.mult, op1=ALU.add),
                     reads=[K_("xrp"), "vfm"], writes=[K_("xc")])
                for j in (2, 1, 0):
                    P.op("dve", lambda e, j=j: e.scalar_tensor_tensor(out=xc[sl][:], in0=xrp[sl][:, j:j + 512], scalar=cw(j), in1=xc[sl][:],
                                                                 op0=ALU.mult, op1=ALU.add),
                         reads=[K_("xrp"), K_("xc"), "vfm"], writes=[K_("xc")])
            P.op("pool", lambda e: e.tensor_copy(out=xcb[sl][:, 0:n], in_=xc[sl][:, 0:n]), reads=[K_("xc")], writes=[K_("xcb")])
            P.op("pe", lambda e: e.matmul(ps[1][:, 0:n], lhsT=wb[:, 0, :], rhs=xcb[sl][:, 0:n], start=True, stop=True), reads=[K_("xcb"), wbk], writes=[PK[1]])
            P.op("pe", lambda e: e.matmul(ps[2][:, 0:n], lhsT=wb[:, 1, :], rhs=xcb[sl][:, 0:n], start=True, stop=True), reads=[K_("xcb"), wbk], writes=[PK[2]])
            P.op("act", lambda e: e.activation(out=rg[sl][:, 0:n], in_=ps[1][:, 0:n], func=AF.Sigmoid, bias=vfm[:, V_BR + c:V_BR + c + 1]),
                 reads=[PK[1], "vfm"], writes=[K_("rg")])
            P.op("act", lambda e: e.activation(out=ig[sl][:, 0:n], in_=ps[2][:, 0:n], func=AF.Sigmoid, bias=vfm[:, V_BI + c:V_BI + c + 1]),
                 reads=[PK[2], "vfm"], writes=[K_("ig")])
            P.op("act", lambda e: e.activation(out=at[sl][:, 0:n], in_=rg[sl][:, 0:n], func=AF.Exp, scale=cl[:, 0, c:c + 1]), reads=[K_("rg"), "cl"], writes=[K_("at")])
            P.op("act", lambda e: e.activation(out=mt[sl][:, 0:n], in_=rg[sl][:, 0:n], func=AF.Exp, scale=cl[:, 1, c:c + 1]), reads=[K_("rg"), "cl"], writes=[K_("mt")])
            P.op("dve", lambda e: e.tensor_scalar(out=mt[sl][:, 0:n], in0=mt[sl][:, 0:n], scalar1=-1.0, scalar2=1.0, op0=ALU.mult, op1=ALU.add),
                 reads=[K_("mt")], writes=[K_("mt")])
            P.op("act", lambda e: e.activation(out=mt[sl][:, 0:n], in_=mt[sl][:, 0:n], func=AF.Sqrt), reads=[K_("mt")], writes=[K_("mt")])
            if (not sample) and is_prefix and prev is None:
                P.op("dve", lambda e: e.memset(mt[sl][:, 0:1], 1.0), reads=[K_("mt")], writes=[K_("mt")])
            if first_own:
                P.op("dve", lambda e: e.tensor_scalar(out=mt[sl][:, 0:1], in0=mt[sl][:, 0:1], scalar1=vf, scalar2=omvf[:, 0:1], op0=ALU.mult, op1=ALU.add),
                     reads=[K_("mt"), "flg", "omvf"], writes=[K_("mt")])
            P.op("dve", lambda e: e.tensor_tensor(out=ut[sl][:, 0:n], in0=mt[sl][:, 0:n], in1=ig[sl][:, 0:n], op=ALU.mult), reads=[K_("mt"), K_("ig")], writes=[K_("ut")])
            P.op("dve", lambda e: e.tensor_tensor(out=ut[sl][:, 0:n], in0=ut[sl][:, 0:n], in1=xc[sl][:, 0:n], op=ALU.mult), reads=[K_("ut"), K_("xc")], writes=[K_("ut")])
            if is_prefix:
                P.op("dve", lambda e: e.tensor_scalar(out=ut[sl][:, 0:n], in0=ut[sl][:, 0:n], scalar1=vf, scalar2=None, op0=ALU.mult),
                     reads=[K_("ut"), "flg"], writes=[K_("ut")])
            if sample:
                P.op("dve", lambda e: e.tensor_tensor(out=hh[sl][:, 0:n], in0=at[sl][:, 0:n], in1=h0T[:, c, :], op=ALU.mult), reads=[K_("at"), "h0T"], writes=[K_("hh")])
                P.op("dve", lambda e: e.tensor_tensor(out=hh[sl][:, 0:n], in0=hh[sl][:, 0:n], in1=ut[sl][:, 0:n], op=ALU.add), reads=[K_("hh"), K_("ut")], writes=[K_("hh")])
                P.op("dve", lambda e: e.tensor_copy(out=hnT[:, c, :], in_=hh[sl][:, 0:n]), reads=[K_("hh")], writes=["hnT"])
            else:
                init = 0.0 if prev is None else hh[prev][:, 511:512]
                P.op("dve", lambda e: e.tensor_tensor_scan(out=hh[sl][:], data0=at[sl][:], data1=ut[sl][:], initial=init, op0=ALU.mult, op1=ALU.add),
                     reads=[K_("at"), K_("ut")] + ([("hh", prev)] if prev is not None else []), writes=[K_("hh")])
            return sl

        def gate_tile(c, w, wk, src_rhs, srckeys, n, sl, dst, dstkey):
            def mm(e):
                ins = None
                for kc in range(KC):
                    ins = e.matmul(ps[3][:, 0:n], lhsT=w[:, kc, 128:256], rhs=src_rhs(kc), start=(kc == 0), stop=(kc == KC - 1))
                return ins
            P.op("pe", mm, reads=srckeys + [wk], writes=[PK[3]])
            P.op("act", lambda e: e.activation(out=gl[sl][:, 0:n], in_=ps[3][:, 0:n], func=AF.Gelu_apprx_tanh), reads=[PK[3]], writes=[("gl", sl)])
            P.op("dve", lambda e: e.tensor_tensor(out=dst, in0=hh[sl][:, 0:n], in1=gl[sl][:, 0:n], op=ALU.mult), reads=[("hh", sl), ("gl", sl)], writes=[dstkey])

        for c in range(KC):
            w = wxg[c % 2]
            wk = ("wxg", c % 2)
            wdma(w[:, :, 0:128], w_in, XR + c * 128, 128, wk)
            wdma(w[:, :, 128:256], w_in, GR + c * 128, 128, wk)
            wb = wrb[c % 2]
            wbk = ("wrb", c % 2)
            for gi, wsrc in enumerate((rg_wr, rg_wi)):
                for q2 in range(2):
                    P.dma("pool", wb[q2 * 64:(q2 + 1) * 64, gi, q2 * 64:(q2 + 1) * 64], wsrc[2 * c + q2], writes=[wbk])
            prev = None
            tl = []
            for t in range(NT_ALL):
                P.rec = []
                sl = rg_tile(c, w, wk, wb, wbk, lambda kc, t=t: hT[:, kc, t * 512:(t + 1) * 512], [("hT", b) for b in range(4 * t, 4 * t + 4)],
                             512, prev, t < NT_PRE, t == NT_PRE, False)
                if t >= NT_PRE:
                    o0 = (t - NT_PRE) * 512
                    ysl = t % 2
                    gate_tile(c, w, wk, lambda kc, t=t: hT[:, kc, t * 512:(t + 1) * 512], [("hT", b) for b in range(4 * t, 4 * t + 4)], 512, sl,
                              ystg[ysl][:, :], ("ystg", ysl))
                    P.dma("sp", ya_d[:, c, o0:o0 + 512], ystg[ysl][:, :], reads=[("ystg", ysl)], writes=["ya_d"], lane=("ystg", ysl))
                prev = sl
                tl.append(P.rec)
                P.rec = None
                if len(tl) == 2:
                    P.replay_interleaved(tl, lag=8)
                    tl = []
            if tl:
                P.replay_interleaved(tl, lag=8)
            P.dma("sp", rg_o[0:1, c * 128:(c + 1) * 128].rearrange("o p -> p o"), hh[prev][:, 511:512], reads=[("hh", prev)], lane=("hh", prev), final=True)
            for j in range(3):
                P.dma("sp", conv_o[j:j + 1, c * 128:(c + 1) * 128].rearrange("o p -> p o"), xrp[prev][:, 512 + j:513 + j], reads=[("xrp", prev)],
                      lane=("xrp", prev), final=True)
            sl = rg_tile(c, w, wk, wb, wbk, lambda kc: hsT[:, kc, :], ["hsT"], NS, None, False, False, True)
            gate_tile(c, w, wk, lambda kc: hsT[:, kc, :], ["hsT"], NS, sl, yainsT[:, c, :], "yainsT")
        for (srcT, skey, dst_ap, lane) in ((hnT, "hnT", rgs_o, "rgs_out"), (xrsT, "xrsT", convs_o[:, 2, :], "convs_b")):
            def trb(e, srcT=srcT):
                ins = None
                for kc in range(KC):
                    ins = e.transpose(ps[4 + kc // 4][0:NS, (kc % 4) * 128:(kc % 4 + 1) * 128], srcT[:, kc, :], ident_f)
                return ins
            P.op("pe", trb, reads=[skey, "consts"], writes=[PK[4], PK[5]])
            stt = sb("stt_" + lane, [NS, D], stack=s3)
            P.op("dve", lambda e, stt=stt: e.tensor_copy(out=stt[:, 0:512], in_=ps[4][0:NS, :]), reads=[PK[4]], writes=["stt_" + lane])
            P.op("dve", lambda e, stt=stt: e.tensor_copy(out=stt[:, 512:1024], in_=ps[5][0:NS, :]), reads=[PK[5]], writes=["stt_" + lane])
            P.dma("sp", dst_ap, stt[:], reads=["stt_" + lane], lane=lane, final=True)
    P.fence()
    dbg("yainsT", yainsT[:], "yainsT")
    if cfg.stop == 3:
        P.finish()
        return nc, P, es, dbg_outs
    with ExitStack() as s4:
        ptb = sb("ptb", [128, NPGS], I32, stack=s4)
        P.dma("sp", ptb[:], pt_d.partition_broadcast(128), writes=["ptb"])
        iot = sb("iot", [128, 1], I32, stack=s4)
        P.op("pool", lambda e: e.iota(iot[:], pattern=[[0, 1]], base=0, channel_multiplier=1), writes=["iot"])
        idx = sb("idx", [128, NPGS], I32, stack=s4)
        P.op("pool", lambda e: e.tensor_scalar(out=idx[:], in0=ptb[:], scalar1=128, scalar2=None, op0=ALU.mult), reads=["ptb"], writes=["idx"])
        P.op("pool", lambda e: e.tensor_tensor(out=idx[:], in0=idx[:], in1=iot[:].to_broadcast([128, NPGS]), op=ALU.add), reads=["idx", "iot"], writes=["idx"])
        ptc = sb("ptc", [NPG, NS], I32, stack=s4)
        for b in range(NS):
            P.dma("sp", ptc[:, b:b + 1], pt_d[0:1, b * NPG:(b + 1) * NPG].rearrange("o j -> j o"), writes=["ptc"])
        kp = [sb("kp%d" % i, [128, D], stack=s4) for i in range(NRING)]
        vp = [sb("vp%d" % i, [128, D], BF16, stack=s4) for i in range(NRING)]
        qb = sb("qb", [128, D], stack=s4)
        scj = [sb("scj%d" % i, [128, H], stack=s4) for i in range(NRING)]
        Pj = [sb("Pj%d" % i, [128, H], BF16, stack=s4) for i in range(NRING)]
        Dt = sb("Dt", [128, H, NPG], stack=s4)
        lfp = sb("lfp", [NPG, 128, H], stack=s4)
        inc = sb("inc", [NPG, 128, H], stack=s4)
        Dp = sb("Dp", [NPG, 128, H], stack=s4)
        totp = sb("totp", [NPG, H], stack=s4)
        onesP = sb("onesP", [NPG, 128], stack=s4)
        P.op("dve", lambda e: e.memset(onesP[:], 1.0), writes=["onesP"])
        onesb = cbf[:, C_ONE:C_ONE + 1]
        msk = consts[0:H, C_BMK:C_BMK + D]
        omask = sb("omask", [H, D], stack=s4)
        ldiag = sb("ldiag", [H, H], stack=s4)
        lcol = sb("lcol", [H, 1], stack=s4)
        pgc = [0]
        for b in range(NS):
            Eb = consts[0:H, C_E + 4 * b:C_E + 4 * b + NS]
            for hf in range(2):
                P.op("pe", lambda e, hf=hf, b=b: e.matmul(ps[hf][:, :], lhsT=consts[0:NS, C_SELQ + 128 * b:C_SELQ + 128 * (b + 1)],
                                                          rhs=qs_tm[:, hf * 512:(hf + 1) * 512], start=True, stop=True),
                     reads=["qs_tm", "consts"], writes=[PK[hf]])
                P.op("dve", lambda e, hf=hf: e.tensor_copy(out=qb[:, hf * 512:(hf + 1) * 512], in_=ps[hf][:, :]), reads=[PK[hf]], writes=["qb"])
            P.dma("pool", lfp[:].rearrange("j s h -> j (s h)"), clf_d, reads=["ptc"], writes=["lfp"],
                  indirect=bass.IndirectOffsetOnAxis(ap=ptc[:, b:b + 1], axis=0))
            for h in range(H):
                P.op("dve", lambda e, h=h: e.tensor_tensor_scan(out=inc[:, :, h], data0=onesP[:], data1=lfp[:, :, h], initial=0.0, op0=ALU.mult, op1=ALU.add),
                     reads=["lfp", "onesP"], writes=["inc"])
            P.op("dve", lambda e: e.tensor_copy(out=totp[:], in_=inc[:, 127, :]), reads=["inc"], writes=["totp"])

            def ymm(e, b=b):
                e.matmul(ps[2][0:NPG, 0:H], lhsT=consts[0:NPG, C_SU:C_SU + NPG], rhs=totp[:], start=True, stop=False)
                return e.matmul(ps[2][0:NPG, 0:H], lhsT=consts[0:NS, C_SELQ + 128 * b:C_SELQ + 128 * b + NPG], rhs=lfs_tm[:], start=False, stop=True)
            P.op("pe", ymm, reads=["totp", "lfs_tm", "consts"], writes=[PK[2]])
            P.op("dve", lambda e: e.tensor_tensor(out=Dp[:], in0=totp[:].unsqueeze(1).to_broadcast([NPG, 128, H]), in1=inc[:], op=ALU.subtract),
                 reads=["totp", "inc"], writes=["Dp"])
            P.op("dve", lambda e: e.tensor_copy(out=totp[:], in_=ps[2][0:NPG, 0:H]), reads=[PK[2]], writes=["totp"])
            P.op("dve", lambda e: e.tensor_tensor(out=Dp[:], in0=Dp[:], in1=totp[:].unsqueeze(1).to_broadcast([NPG, 128, H]), op=ALU.add),
                 reads=["Dp", "totp"], writes=["Dp"])
            nb_ = (H * NPG + 511) // 512
            hpb = 512 // NPG

            def dtr(e):
                ins = None
                for h in range(H):
                    ins = e.transpose(ps[3 + h // hpb][:, (h % hpb) * NPG:(h % hpb + 1) * NPG], Dp[:, :, h], ident_f[0:NPG, 0:NPG])
                return ins
            P.op("pe", dtr, reads=["Dp", "consts"], writes=[PK[3 + i] for i in range(nb_)])
            for i in range(nb_):
                w_ = min(512, H * NPG - i * 512)
                P.op("dve", lambda e, i=i, w_=w_: e.tensor_copy(out=Dt[:].rearrange("p h j -> p (h j)")[:, i * 512:i * 512 + w_], in_=ps[3 + i][:, 0:w_]),
                     reads=[PK[3 + i]], writes=["Dt"])
            for j in range(NPG):
                sl = pgc[0] % NRING
                pgc[0] += 1
                col = b * NPG + j
                P.dma("pool", kp[sl][:], ck_d, reads=["idx"], writes=[("kp", sl)], indirect=bass.IndirectOffsetOnAxis(ap=idx[:, col:col + 1], axis=0))
                P.dma("pool", vp[sl][:], cv_d, reads=["idx"], writes=[("vp", sl)], indirect=bass.IndirectOffsetOnAxis(ap=idx[:, col:col + 1], axis=0))
                P.op("dve", lambda e, sl=sl: e.tensor_tensor(out=kp[sl][:], in0=kp[sl][:], in1=qb[:], op=ALU.mult), reads=[("kp", sl), "qb"], writes=[("kp", sl)])
                P.op("dve", lambda e, sl=sl: e.tensor_reduce(out=scj[sl][:], in_=kp[sl][:].rearrange("p (h d) -> p h d", d=DH), axis=AX.X, op=ALU.add),
                     reads=[("kp", sl)], writes=[("scj", sl)])
                P.op("dve", lambda e, sl=sl, j=j: e.tensor_tensor(out=scj[sl][:], in0=scj[sl][:], in1=Dt[:, :, j], op=ALU.add), reads=[("scj", sl), "Dt"], writes=[("scj", sl)])
                P.op("act", lambda e, sl=sl: e.activation(out=Pj[sl][:], in_=scj[sl][:], func=AF.Exp), reads=[("scj", sl)], writes=[("Pj", sl)])

                def pvm(e, sl=sl, j=j):
                    e.matmul(ps[0][0:H, :], lhsT=Pj[sl][:], rhs=vp[sl][:, 0:512], start=(j == 0), stop=(j == NPG - 1))
                    e.matmul(ps[1][0:H, :], lhsT=Pj[sl][:], rhs=vp[sl][:, 512:1024], start=(j == 0), stop=(j == NPG - 1))
                    return e.matmul(ps[2][0:H, 0:1], lhsT=Pj[sl][:], rhs=onesb, start=(j == 0), stop=(j == NPG - 1))
                P.op("pe", pvm, reads=[("Pj", sl), ("vp", sl), "cbf"], writes=[PK[0], PK[1], PK[2]])
            P.op("dve", lambda e: e.tensor_tensor(out=omask[:, 0:512], in0=ps[0][0:H, :], in1=msk[:, 0:512], op=ALU.mult), reads=[PK[0], "consts"], writes=["omask"])
            P.op("dve", lambda e: e.tensor_tensor(out=omask[:, 512:1024], in0=ps[1][0:H, :], in1=msk[:, 512:1024], op=ALU.mult), reads=[PK[1], "consts"], writes=["omask"])
            P.op("dve", lambda e: e.tensor_copy(out=lcol[:], in_=ps[2][0:H, 0:1]), reads=[PK[2]], writes=["lcol"])
            P.op("dve", lambda e: e.tensor_scalar(out=ldiag[:], in0=ident_f[0:H, 0:H], scalar1=lcol[:, 0:1], scalar2=None, op0=ALU.mult),
                 reads=["lcol", "consts"], writes=["ldiag"])

            def colm(e, b=b, Eb=Eb):
                e.matmul(ps[5][0:NS, :], lhsT=Eb, rhs=omask[:, 0:512], start=(b == 0), stop=(b == NS - 1))
                e.matmul(ps[6][0:NS, :], lhsT=Eb, rhs=omask[:, 512:1024], start=(b == 0), stop=(b == NS - 1))
                return e.matmul(ps[7][0:NS, 0:H], lhsT=Eb, rhs=ldiag[:], start=(b == 0), stop=(b == NS - 1))
            P.op("pe", colm, reads=["omask", "ldiag", "consts"], writes=[PK[5], PK[6], PK[7]])
        tq = sb("tq", [NS, D], stack=s4)
        scur = sb("scur", [NS, H], stack=s4)
        lt = sb("lt", [NS, H], stack=s4)
        P.op("dve", lambda e: e.tensor_tensor(out=tq[:], in0=qs_tm[:], in1=ks_tm[:], op=ALU.mult), reads=["qs_tm", "ks_tm"], writes=["tq"])
        P.op("dve", lambda e: e.tensor_reduce(out=scur[:], in_=tq[:].rearrange("p (h d) -> p h d", d=DH), axis=AX.X, op=ALU.add), reads=["tq"], writes=["scur"])
        P.op("act", lambda e: e.activation(out=scur[:], in_=scur[:], func=AF.Exp), reads=["scur"], writes=["scur"])
        P.op("dve", lambda e: e.tensor_tensor(out=lt[:], in0=ps[7][0:NS, 0:H], in1=scur[:], op=ALU.add), reads=[PK[7], "scur"], writes=["lt"])
        P.op("dve", lambda e: e.reciprocal(out=lt[:], in_=lt[:]), reads=["lt"], writes=["lt"])
        P.op("dve", lambda e: e.tensor_tensor(out=tq[:].rearrange("p (h d) -> p h d", d=DH), in0=vs_tm[:].rearrange("p (h d) -> p h d", d=DH),
                                              in1=scur[:].unsqueeze(2).to_broadcast([NS, H, DH]), op=ALU.mult), reads=["vs_tm", "scur"], writes=["tq"])
        P.op("dve", lambda e: e.tensor_tensor(out=tq[:, 0:512], in0=tq[:, 0:512], in1=ps[5][0:NS, :], op=ALU.add), reads=["tq", PK[5]], writes=["tq"])
        P.op("dve", lambda e: e.tensor_tensor(out=tq[:, 512:1024], in0=tq[:, 512:1024], in1=ps[6][0:NS, :], op=ALU.add), reads=["tq", PK[6]], writes=["tq"])
        P.op("dve", lambda e: e.tensor_tensor(out=os_tm[:].rearrange("p (h d) -> p h d", d=DH), in0=tq[:].rearrange("p (h d) -> p h d", d=DH),
                                              in1=lt[:].unsqueeze(2).to_broadcast([NS, H, DH]), op=ALU.mult), reads=["tq", "lt"], writes=["os_tm"])
        P.dma("sp", ks_o, ks_tm[:], reads=["ks_tm"], lane="ks_out", final=True)
        P.dma("sp", vs_o, vs_tm[:], reads=["vs_tm"], lane="vs_out", final=True)
        osb = sb("osb", [NS, D], BF16, stack=s4)
        P.op("dve", lambda e: e.tensor_copy(out=osb[:], in_=os_tm[:]), reads=["os_tm"], writes=["osb"])
        pvs = to_fm(osb, "osb", NS, 3, ident_b)
        P.op("dve", lambda e: e.tensor_copy(out=osT[:].rearrange("p k n -> p (k n)"), in_=pvs[:, 0:KC * NS]), reads=[PK[3]], writes=["osT"])
    P.fence()
    dbg("os_tm", os_tm[:], "os_tm")
    if cfg.stop == 4:
        P.finish()
        return nc, P, es, dbg_outs

    with ExitStack() as s5:
        w3 = [sb("w3a%d" % i, [128, KC, 512], BF16, stack=s5) for i in range(2)]
        sga = [sb("sga%d" % i, [128, 512], stack=s5) for i in range(2)]
        sgb = [sb("sgb%d" % i, [128, 512], stack=s5) for i in range(2)]
        oTt = sb("oTt", [128, KC, 512], BF16, stack=s5)
        yaTt = sb("yaTt", [128, KC, 512], BF16, stack=s5)
        mstg = [sb("mstg%d" % i, [128, 512], BF16, stack=s5) for i in range(2)]
        c3 = [0]
        for c in range(KC):
            w = w3[c % 2]
            wk = ("w3a", c % 2)
            wdma(w[:, :, 0:128], w_in, GA + c * 128, 128, wk)
            wdma(w[:, :, 128:256], w_in, GB + c * 128, 128, wk)
            wdma(w[:, :, 256:384], w_pa, c * 128, 128, wk)
            wdma(w[:, :, 384:512], w_pb, c * 128, 128, wk)
            for t in range(NT_OWN + 1):
                sample = (t == NT_OWN)
                n = NS if sample else 512
                sl = c3[0] % 2
                c3[0] += 1
                if sample:
                    srcs = [(lambda kc: hsT[:, kc, :], ["hsT"]), (lambda kc: hsT[:, kc, :], ["hsT"]),
                            (lambda kc: yainsT[:, kc, :], ["yainsT"]), (lambda kc: osT[:, kc, :], ["osT"])]
                    dst, dkey = mergedsT[:, c, :], "mergedsT"
                else:
                    tb = NT_PRE + t
                    hk = [("hT", b) for b in range(4 * tb, 4 * tb + 4)]
                    P.dma("sp", oTt[:], oT_d[:, :, t * 512:(t + 1) * 512], reads=["oT_d"], writes=["oTt"])
                    P.dma("sp", yaTt[:], ya_d[:, :, t * 512:(t + 1) * 512], reads=["ya_d"], writes=["yaTt"])
                    srcs = [(lambda kc, tb=tb: hT[:, kc, tb * 512:(tb + 1) * 512], hk), (lambda kc, tb=tb: hT[:, kc, tb * 512:(tb + 1) * 512], hk),
                            (lambda kc: yaTt[:, kc, :], ["yaTt"]),
                            (lambda kc: oTt[:, kc, :], ["oTt"])]
                    dst, dkey = mstg[sl][:, :], ("mstg", sl)
                for gi, (src, skeys) in enumerate(srcs):
                    def mm(e, gi=gi, src=src):
                        ins = None
                        for kc in range(KC):
                            ins = e.matmul(ps[gi][:, 0:n], lhsT=w[:, kc, gi * 128:(gi + 1) * 128], rhs=src(kc), start=(kc == 0), stop=(kc == KC - 1))
                        return ins
                    P.op("pe", mm, reads=skeys + [wk], writes=[PK[gi]])
                P.op("act", lambda e: e.activation(out=sga[sl][:, 0:n], in_=ps[0][:, 0:n], func=AF.Sigmoid), reads=[PK[0]], writes=[("sga", sl)])
                P.op("act", lambda e: e.activation(out=sgb[sl][:, 0:n], in_=ps[1][:, 0:n], func=AF.Sigmoid), reads=[PK[1]], writes=[("sgb", sl)])
                P.op("dve", lambda e: e.tensor_tensor(out=sga[sl][:, 0:n], in0=ps[2][:, 0:n], in1=sga[sl][:, 0:n], op=ALU.mult), reads=[PK[2], ("sga", sl)], writes=[("sga", sl)])
                P.op("dve", lambda e: e.tensor_tensor(out=sgb[sl][:, 0:n], in0=ps[3][:, 0:n], in1=sgb[sl][:, 0:n], op=ALU.mult), reads=[PK[3], ("sgb", sl)], writes=[("sgb", sl)])
                P.op("dve", lambda e, dst=dst: e.tensor_tensor(out=dst, in0=sga[sl][:, 0:n], in1=sgb[sl][:, 0:n], op=ALU.add), reads=[("sga", sl), ("sgb", sl)], writes=[dkey])
                if not sample:
                    P.dma("sp", mg_d[:, c, t * 512:(t + 1) * 512], mstg[sl][:, :], reads=[("mstg", sl)], writes=["mg_d"], lane=("mstg", sl))
    sH.close()
    P.fence()
    if cfg.stop == 5:
        P.finish()
        return nc, P, es, dbg_outs

    NCH = cfg.DFF // 128
    with ExitStack() as s6:
        wo_t = sb("wo_t", [128, KC, D], BF16, stack=s6)
        for hf in range(2):
            wdma(wo_t[:, :, hf * 512:(hf + 1) * 512], w_o, hf * 512, 512, "wo_t")
        wfo_t = sb("wfo_t", [128, NCH, D], BF16, stack=s6)
        for c0 in range(0, NCH, 6):
            c1 = min(NCH, c0 + 6)
            P.dma("pool", wfo_t[:, c0:c1, :], w_fo[c0 * 128:c1 * 128, :].rearrange("(c p) n -> p c n", p=128), writes=["wfo_t"])
        wf = [sb("wf%d" % i, [128, KC, 256], BF16, stack=s6) for i in range(4)]
        x3 = sb("x3", [128, 4, D], stack=s6)
        h2T = sb("h2T", [128, KC, 512], BF16, stack=s6)
        actT = sb("actT", [128, NCH, 512], BF16, stack=s6)
        sg = [sb("sg%d" % i, [128, 512], stack=s6) for i in range(2)]
        tmp3 = [sb("tmp3%d" % i, [128, 512], stack=s6) for i in range(2)]
        xn3 = [sb("xn3%d" % i, [128, D], BF16, stack=s6) for i in range(2)]
        junk3 = sb("junk3", [128, D], BF16, stack=s6)
        st3 = sb("st3", [128, 4], stack=s6)
        yst = [sb("yst%d" % i, [128, D], stack=s6) for i in range(1)]
        tmp_s3 = sb("tmp_s3", [128, KC, NS], stack=s6)
        mTt = sb("mTt", [128, KC, 512], BF16, stack=s6)
        cc = [0]
        for t in range(NT_OWN + 1):
            sample = (t == NT_OWN)
            nblk = 1 if sample else 4
            nt = NS if sample else 128
            ntok = NS if sample else 512
            mT = mergedsT if sample else mTt
            mkeys = ["mergedsT"] if sample else ["mTt"]
            if not sample:
                P.dma("sp", mTt[:], mg_d[:, :, t * 512:(t + 1) * 512], reads=["mg_d"], writes=["mTt"])
            g1t, g1k = (g1s, "g1s") if sample else (g1b, "g1b")
            g2t, g2k = (g2s, "g2s") if sample else (g2b, "g2b")
            if sample:
                P.dma("sp", x3[0:NS, 0, :], xs_d, writes=["x3"])
            else:
                P.dma("sp", x3[:], xa[S_PRE + t * 512:S_PRE + (t + 1) * 512, :].rearrange("(b p) d -> p b d", p=128), writes=["x3"])

            def tok_cols(blk):
                return slice(0, NS) if sample else slice(blk * 128, (blk + 1) * 128)

            def resid(blk, hf, lhs_of, nk, rhs_of, rkeys, gt, gk):
                bank = cc[0] % 2
                cc[0] += 1

                def mm(e):
                    ins = None
                    for kc in range(nk):
                        ins = e.matmul(ps[bank][0:nt, :], lhsT=lhs_of(kc), rhs=rhs_of(kc, hf), start=(kc == 0), stop=(kc == nk - 1))
                    return ins
                P.op("pe", mm, reads=rkeys, writes=[PK[bank]])
                P.op("dve", lambda e: e.tensor_tensor(out=tmp3[bank][0:nt, :], in0=ps[bank][0:nt, :], in1=gt[0:nt, hf * 512:(hf + 1) * 512], op=ALU.mult),
                     reads=[PK[bank], gk], writes=[("tmp3", bank)])
                P.op("dve", lambda e: e.tensor_tensor(out=x3[0:nt, blk, hf * 512:(hf + 1) * 512], in0=x3[0:nt, blk, hf * 512:(hf + 1) * 512], in1=tmp3[bank][0:nt, :], op=ALU.add),
                     reads=[("tmp3", bank), "x3"], writes=["x3"])
            for blk in range(nblk):
                for hf in range(2):
                    resid(blk, hf, lambda kc, blk=blk: mT[:, kc, tok_cols(blk)], KC, lambda kc, hf: wo_t[:, kc, hf * 512:(hf + 1) * 512],
                          mkeys + ["wo_t"], g1t, g1k)
            for blk in range(nblk):
                sl = blk % 2
                P.op("dve", lambda e: e.memset(st3[:, 0:2], 0.0), writes=["st3"])
                norm_block(s6, x3[0:nt, blk, :], "x3", nt, st3[:, 0:2], "st3", xn3[sl], ("xn3", sl), junk3, 999)
                pvv = to_fm(xn3[sl], ("xn3", sl), nt, 2 + sl, ident_b)
                if sample:
                    P.op("dve", lambda e, pvv=pvv: e.tensor_tensor(out=tmp_s3[:], in0=pvv[:, 0:KC * NS].rearrange("p (k n) -> p k n", n=NS), in1=ABs[:, 2, :, :], op=ALU.mult),
                         reads=[PK[2 + sl], "ABs"], writes=["tmp_s3"])
                    P.op("dve", lambda e: e.tensor_tensor(out=h2T[:, :, 0:NS], in0=tmp_s3[:], in1=ABs[:, 3, :, :], op=ALU.add), reads=["tmp_s3", "ABs"], writes=["h2T"])
                else:
                    for kc in range(KC):
                        P.op("dve", lambda e, kc=kc, pvv=pvv, blk=blk: e.tensor_scalar(
                            out=h2T[:, kc, blk * 128:(blk + 1) * 128], in0=pvv[:, kc * 128:(kc + 1) * 128],
                            scalar1=AB[:, 2, kc:kc + 1], scalar2=AB[:, 3, kc:kc + 1], op0=ALU.mult, op1=ALU.add),
                            reads=[PK[2 + sl], "AB"], writes=["h2T"])
            for c in range(NCH):
                w = wf[c % 4]
                wk = ("wf", c % 4)
                wdma(w[:, :, 0:128], w_fi, c * 128, 128, wk)
                wdma(w[:, :, 128:256], w_fi, cfg.DFF + c * 128, 128, wk)
                pb = 4 + 2 * (c % 2)
                for gi in range(2):
                    def mm(e, gi=gi, pb=pb):
                        ins = None
                        for kc in range(KC):
                            ins = e.matmul(ps[pb + gi][:, 0:ntok], lhsT=w[:, kc, gi * 128:(gi + 1) * 128], rhs=h2T[:, kc, 0:ntok], start=(kc == 0), stop=(kc == KC - 1))
                        return ins
                    P.op("pe", mm, reads=["h2T", wk], writes=[PK[pb + gi]])
                P.op("act", lambda e, pb=pb, c=c: e.activation(out=sg[c % 2][:, 0:ntok], in_=ps[pb][:, 0:ntok], func=AF.Silu), reads=[PK[pb]], writes=[("sg", c % 2)])
                P.op("dve", lambda e, pb=pb, c=c: e.tensor_tensor(out=actT[:, c, 0:ntok], in0=ps[pb + 1][:, 0:ntok], in1=sg[c % 2][:, 0:ntok], op=ALU.mult),
                     reads=[PK[pb + 1], ("sg", c % 2)], writes=["actT"])
            for blk in range(nblk):
                for hf in range(2):
                    resid(blk, hf, lambda kc, blk=blk: actT[:, kc, (slice(0, NS) if sample else slice(blk * 128, (blk + 1) * 128))], NCH,
                          lambda kc, hf: wfo_t[:, kc, hf * 512:(hf + 1) * 512], ["actT", "wfo_t"], g2t, g2k)
            for blk in range(nblk):
                sl = 0
                P.op("dve", lambda e: e.memset(st3[:, 2:4], 0.0), writes=["st3b"])
                P.op("act", lambda e, blk=blk: e.activation(out=junk3[0:nt, :], in_=x3[0:nt, blk, :], func=AF.Square, accum_out=st3[0:nt, 2:3]), reads=["x3", "st3b"], writes=["st3b"])
                P.op("act", lambda e: e.activation(out=st3[0:nt, 3:4], in_=st3[0:nt, 2:3], func=AF.Ln, scale=1.0 / D, bias=EPS), reads=["st3b"], writes=["st3b"])
                P.op("act", lambda e: e.activation(out=st3[0:nt, 3:4], in_=st3[0:nt, 3:4], func=AF.Exp, scale=-0.5), reads=["st3b"], writes=["st3b"])
                P.op("act", lambda e, blk=blk, sl=sl: e.activation(out=yst[sl][0:nt, :], in_=x3[0:nt, blk, :], func=AF.Copy, scale=st3[0:nt, 3:4]),
                     reads=["x3", "st3b"], writes=[("yst", sl)])
                P.op("dve", lambda e, sl=sl: e.tensor_tensor(out=yst[sl][0:nt, :], in0=yst[sl][0:nt, :], in1=normf_b[0:nt, :], op=ALU.mult),
                     reads=[("yst", sl), "normf_b"], writes=[("yst", sl)])
                if sample:
                    P.dma("sp", ys_o, yst[sl][0:NS, :], reads=[("yst", sl)], lane=("yst", sl), final=True)
                else:
                    r0 = t * 512 + blk * 128
                    P.dma("sp", y_o[r0:r0 + 128, :], yst[sl][:, :], reads=[("yst", sl)], lane=("yst", sl), final=True)
    P.fence()
    P.finish()
    return nc, P, es, dbg_outs
    return nc, P, es, dbg_outs


def prep_inputs(cfg, inp):
    f = lambda a: np.ascontiguousarray(np.asarray(a))
    D = cfg.D
    x_prompt = f(inp["x_prompt"])
    B, SEQ, _ = x_prompt.shape
    assert SEQ == 2 * cfg.S_OWN
    n_cores = 2 * B
    consts = make_consts()
    vecs = np.concatenate([
        f(inp["ada_b"]).reshape(48, 128), f(inp["norm1_g"]).reshape(8, 128), f(inp["norm2_g"]).reshape(8, 128),
        f(inp["conv_w"]).reshape(32, 128), f(inp["conv_b"]).reshape(8, 128), f(inp["rg_br"]).reshape(8, 128),
        f(inp["rg_bi"]).reshape(8, 128), f(inp["rg_lambda"]).reshape(8, 128)], axis=0).astype(np.float32)
    shared = {
        "ck": f(inp["cache_k"]).reshape(cfg.NPOOL * 128, D),
        "cv": f(inp["cache_v"]).reshape(cfg.NPOOL * 128, D),
        "clf": f(inp["cache_logf"]).reshape(cfg.NPOOL, 128 * cfg.H),
        "consts": consts,
        "ada_w": f(inp["ada_w"]).reshape(D, 6 * D),
        "ada_b": f(inp["ada_b"]).reshape(1, 6 * D),
        "vecs": vecs,
        "normf_g": f(inp["normf_g"]).reshape(1, D),
        "w_in": f(inp["w_in"]).reshape(D, cfg.NIN),
        "b_f": f(inp["b_f"]).reshape(1, cfg.H),
        "rg_wr": f(inp["rg_wr"]).reshape(16, 64, 64),
        "rg_wi": f(inp["rg_wi"]).reshape(16, 64, 64),
        "w_proj_a": f(inp["w_proj_a"]).reshape(D, D),
        "w_proj_b": f(inp["w_proj_b"]).reshape(D, D),
        "w_o": f(inp["w_o"]).reshape(D, D),
        "w_ffn_in": f(inp["w_ffn_in"]).reshape(D, 2 * cfg.DFF),
        "w_ffn_out": f(inp["w_ffn_out"]).reshape(cfg.DFF, D),
    }
    c_prompt = f(inp["c_prompt"]); c_sample = f(inp["c_sample"])
    x_sample = f(inp["x_sample"]).reshape(-1, D)
    sconv = f(inp["state_conv"])[0]; srg = f(inp["state_rglru"])[0]
    pt = f(inp["page_table"]).astype(np.int32)
    maps = []
    for c in range(n_cores):
        b, half = c // 2, c % 2
        xa = np.zeros((cfg.S_ALL, D), np.float32)
        if half == 0:
            xa[cfg.S_PRE:] = x_prompt[b, :cfg.S_OWN]
        else:
            xa[:] = x_prompt[b]
        flags = np.zeros((128, 2), np.float32)
        flags[:, 0] = 1.0 if half == 1 else 0.0
        flags[:, 1] = 0.0 if half == 1 else NEG
        crow = np.zeros((33, D), np.float32)
        crow[0:4] = c_sample[4 * c:4 * c + 4]
        crow[32] = c_prompt[b]
        m = dict(shared)
        m.update({
            "xa": xa, "flags": flags, "crow": crow,
            "xs": np.ascontiguousarray(x_sample[4 * c:4 * c + 4]),
            "sconv": np.ascontiguousarray(sconv[4 * c:4 * c + 4]),
            "srg": np.ascontiguousarray(srg[4 * c:4 * c + 4]),
            "pt": np.ascontiguousarray(pt[4 * c:4 * c + 4].reshape(1, -1)),
        })
        maps.append(m)
    return maps


def run(cfg, inp):
    nc, P, es, dbg_outs = build_program(cfg)
    maps = prep_inputs(cfg, inp)
    used = set()
    for alloc in nc.allocations:
        if isinstance(alloc, mybir.MemoryLocationSet) and alloc.kind == "ExternalInput":
            used.add(alloc.memorylocations[0].name)
    maps = [{k: v for k, v in m.items() if k in used} for m in maps]
    res = run_bass_kernel_spmd(nc, maps, core_ids=list(range(len(maps))))
    es.close()
    return res.results


def assemble(cfg, res):
    n_cores = len(res)
    B = n_cores // 2
    S = cfg.S_OWN
    D, H, DH = cfg.D, cfg.H, cfg.DH
    cat = lambda name: np.concatenate([res[c][name] for c in range(n_cores)], axis=0)
    y = cat("y").reshape(B, 2 * S, D)
    k = cat("ko").reshape(1, B, 2 * S, H, DH)
    v = cat("vo").reshape(1, B, 2 * S, H, DH)
    lf = cat("lfo").reshape(1, B, 2 * S, H)
    conv = np.stack([res[2 * b + 1]["convo"] for b in range(B)])[None]
    rg = np.stack([res[2 * b + 1]["rgo"][0] for b in range(B)])[None]
    ys = cat("yso").reshape(-1, 1, D)
    ks = cat("kso").reshape(1, -1, 1, H, DH)
    vs = cat("vso").reshape(1, -1, 1, H, DH)
    lfs = cat("lfso").reshape(1, -1, 1, H)
    convs = cat("convso").reshape(1, -1, 3, D)
    rgs = cat("rgso").reshape(1, -1, D)
    return (y, ys, k, v, lf, conv, rg, ks, vs, lfs, convs, rgs)


def kernel(**inputs):
    cfg = Cfg()
    res = run(cfg, inputs)
    return tuple(np.ascontiguousarray(a, dtype=np.float32) for a in assemble(cfg, res))
```

```python
import numpy as np
from contextlib import ExitStack
import concourse.bass as bass
import concourse.mybir as mybir
from concourse.bass_utils import run_bass_kernel_spmd

F32 = mybir.dt.float32
BF16 = mybir.dt.bfloat16
I32 = mybir.dt.int32
AF = mybir.ActivationFunctionType
ALU = mybir.AluOpType
AX = mybir.AxisListType

NEG = -30000.0
NRING = 5
SAME_SYNC = True


class Cfg:
    def __init__(self, s_own=2048, npg=64, npool=2560, debug=()):
        self.D = 1024
        self.KC = 8
        self.H = 16
        self.DH = 64
        self.DFF = 2816
        self.NIN = 7184
        self.S_OWN = s_own
        self.S_PRE = s_own
        self.S_ALL = 2 * s_own
        self.NS = 4
        self.NPG = npg
        self.NPOOL = npool
        self.NB_ALL = self.S_ALL // 128
        self.NB_PRE = self.S_PRE // 128
        self.NB_OWN = self.S_OWN // 128
        self.NT_ALL = self.S_ALL // 512
        self.NT_PRE = self.S_PRE // 512
        self.NT_OWN = self.S_OWN // 512
        self.debug = tuple(debug)
        self.stop = 99


XR, GR, QO, KO, VO, FLO, GA, GB = 0, 1024, 2048, 3072, 4096, 5120, 5136, 6160

C_ID = 0
C_U = 128
C_ONE = 256
C_BM = 384
C_SU = 384
C_SEL32 = C_SU + 128
C_ONESEL = C_SEL32 + 128
C_SELQ = C_ONESEL + 16
C_BMK = C_SELQ + 512
C_E = C_BMK + 1024
NCP = C_E + 16
C_BM_SRC = NCP
NCONST = NCP + 2048


def make_consts():
    c = np.zeros((128, NCONST), np.float32)
    c[:, C_ID:C_ID + 128] = np.eye(128)
    j = np.arange(128)[:, None]
    t = np.arange(128)[None, :]
    c[:, C_U:C_U + 128] = (j <= t)
    c[:, C_ONE:C_ONE + 128] = 1.0
    kt = np.arange(128)[:, None]
    qt = np.arange(512)[None, :]
    for jj in range(4):
        c[:, C_BM_SRC + 512 * jj:C_BM_SRC + 512 * (jj + 1)] = np.where(qt - 128 * jj - kt >= 0, 0.0, NEG)
    c[:, C_SU:C_SU + 128] = (j > t)
    c[32, C_SEL32:C_SEL32 + 128] = 1.0
    for b in range(4):
        c[:, C_ONESEL + 4 * b + b] = 1.0
        c[b, C_SELQ + 128 * b:C_SELQ + 128 * (b + 1)] = 1.0
        c[:16, C_E + 4 * b + b] = 1.0
    for h in range(16):
        c[h, C_BMK + 64 * h:C_BMK + 64 * (h + 1)] = 1.0
    return c


class Prog:
    def __init__(self, nc):
        self.nc = nc
        self.eng = {"pe": nc.tensor, "act": nc.scalar, "dve": nc.vector, "pool": nc.gpsimd, "sp": nc.sync}
        self.pool = [nc.alloc_semaphore("ks%d" % i) for i in range(96)]
        self.allsems = list(self.pool)
        for sm in self.pool:
            nc.gpsimd.sem_clear(sm)
        nc.all_engine_barrier()
        self.sem = {e: self.pool.pop() for e in ("pe", "act", "dve", "pool")}
        self.cnt = {e: 0 for e in ("pe", "act", "dve", "pool")}
        self.known = {e: {} for e in self.eng}
        self.last_w = {}
        self.readers = {}
        self.lanes = {}
        self.out_lanes = set()
        self.nops = 0
        self.fence_need = {}
        self.rec = None

    def _sem_of(self, key):
        if key in self.sem:
            return self.sem[key]
        return self.lanes[key][0]

    def fence(self):
        for k, v in self.cnt.items():
            if v:
                self.fence_need[k] = v
        for lk, (sm, v) in self.lanes.items():
            if v:
                self.fence_need[lk] = v

    def _deps(self, engine, reads, writes):
        need = {k: v for k, v in self.fence_need.items() if k != engine}

        def add(tok, kind):
            if tok is None:
                return
            sk, val = tok
            if sk == engine:
                if engine == "pe":
                    return
                if kind == "war" or not SAME_SYNC:
                    return
            if need.get(sk, 0) < val:
                need[sk] = val

        for r in reads:
            add(self.last_w.get(r), "raw")
        for w in writes:
            add(self.last_w.get(w), "waw")
            for tok in self.readers.get(w, ()):
                add(tok, "war")
        e = self.eng[engine]
        kn = self.known[engine]
        for sk, val in need.items():
            if kn.get(sk, 0) >= val:
                continue
            e.wait_ge(self._sem_of(sk), val)
            kn[sk] = val

    def _commit(self, tok, reads, writes):
        for w in writes:
            self.last_w[w] = tok
            self.readers[w] = []
        for r in reads:
            if r in writes:
                continue
            self.readers.setdefault(r, []).append(tok)

    def op(self, engine, fn, reads=(), writes=()):
        if self.rec is not None:
            self.rec.append(("op", (engine, fn, tuple(reads), tuple(writes)), {}))
            return
        self._deps(engine, reads, writes)
        ins = fn(self.eng[engine])
        self.cnt[engine] += 1
        ins.then_inc(self.sem[engine], 1)
        self._commit((engine, self.cnt[engine]), reads, writes)
        self.nops += 1

    def dma(self, q, out, in_, reads=(), writes=(), lane=None, final=False, indirect=None):
        if self.rec is not None:
            self.rec.append(("dma", (q, out, in_), dict(reads=tuple(reads), writes=tuple(writes), lane=lane, final=final, indirect=indirect)))
            return
        if lane is None:
            lane = writes[0] if writes else reads[0]
        lk = ("dma", lane)
        if lk not in self.lanes:
            self.lanes[lk] = [self.pool.pop(), 0]
        self._deps(q, reads, writes)
        e = self.eng[q]
        if indirect is not None:
            ins = e.indirect_dma_start(out=out, out_offset=None, in_=in_, in_offset=indirect)
        else:
            ins = e.dma_start(out=out, in_=in_)
        self.lanes[lk][1] += 16
        ins.then_inc(self.lanes[lk][0], 16)
        self._commit((lk, self.lanes[lk][1]), reads, writes)
        if final:
            self.out_lanes.add(lk)
        self.nops += 1

    def replay_interleaved(self, lists, lag=0):
        pos = [0] * len(lists)
        live = True
        step = 0
        while live:
            live = False
            step += 1
            for i, l in enumerate(lists):
                if i > 0 and step <= lag * i and pos[0] < len(lists[0]):
                    live = True
                    continue
                if pos[i] < len(l):
                    kind, a, kw = l[pos[i]]
                    pos[i] += 1
                    live = True
                    if kind == "op":
                        self.op(*a)
                    else:
                        self.dma(*a, **kw)

    def finish(self):
        e = self.eng["sp"]
        for lk in sorted(self.out_lanes, key=str):
            e.wait_ge(self.lanes[lk][0], self.lanes[lk][1])
        for k in ("pe", "act", "dve", "pool"):
            if self.cnt[k]:
                e.wait_ge(self.sem[k], self.cnt[k])
        self.nc.all_engine_barrier()
        for sm in self.allsems:
            self.nc.gpsimd.sem_clear(sm)
        self.nc.all_engine_barrier()


def build_program(cfg):
    nc = bass.Bass("TRN2", target_bir_lowering=False)
    P = Prog(nc)
    D, KC, H, DH = cfg.D, cfg.KC, cfg.H, cfg.DH
    S_ALL, S_OWN, S_PRE = cfg.S_ALL, cfg.S_OWN, cfg.S_PRE
    NB_ALL, NB_PRE, NB_OWN = cfg.NB_ALL, cfg.NB_PRE, cfg.NB_OWN
    NT_ALL, NT_PRE, NT_OWN = cfg.NT_ALL, cfg.NT_PRE, cfg.NT_OWN
    NS, NPG = cfg.NS, cfg.NPG
    NPGS = NS * NPG

    def din(name, shape, dt=F32):
        return nc.dram_tensor(name, list(shape), dt, kind="ExternalInput").ap()

    def dout(name, shape, dt=F32):
        return nc.dram_tensor(name, list(shape), dt, kind="ExternalOutput").ap()

    xa = din("xa", [S_ALL, D])
    flags = din("flags", [128, 2])
    crow = din("crow", [33, D])
    xs_d = din("xs", [NS, D])
    sconv_d = din("sconv", [NS, 3, D])
    srg_d = din("srg", [NS, D])
    pt_d = din("pt", [1, NPGS], I32)
    if cfg.stop >= 5:
        ck_d = din("ck", [cfg.NPOOL * 128, D])
        cv_d = din("cv", [cfg.NPOOL * 128, D])
        clf_d = din("clf", [cfg.NPOOL, 128 * H])
    consts_d = din("consts", [128, NCONST])
    ada_w = din("ada_w", [D, 6 * D])
    ada_b = din("ada_b", [1, 6 * D])
    vecs_d = din("vecs", [128, 128])
    normf_d = din("normf_g", [1, D])
    w_in = din("w_in", [D, cfg.NIN])
    b_f = din("b_f", [1, H])
    rg_wr = din("rg_wr", [16, 64, 64])
    rg_wi = din("rg_wi", [16, 64, 64])
    w_pa = din("w_proj_a", [D, D])
    w_pb = din("w_proj_b", [D, D])
    w_o = din("w_o", [D, D])
    w_fi = din("w_ffn_in", [D, 2 * cfg.DFF])
    w_fo = din("w_ffn_out", [cfg.DFF, D])

    y_o = dout("y", [S_OWN, D])
    k_o = dout("ko", [S_OWN, D])
    v_o = dout("vo", [S_OWN, D])
    lf_o = dout("lfo", [S_OWN, H])
    conv_o = dout("convo", [3, D])
    rg_o = dout("rgo", [1, D])
    ys_o = dout("yso", [NS, D])
    ks_o = dout("kso", [NS, D])
    vs_o = dout("vso", [NS, D])
    lfs_o = dout("lfso", [NS, H])
    convs_o = dout("convso", [NS, 3, D])
    rgs_o = dout("rgso", [NS, D])
    dbg_outs = {}

    es = ExitStack()

    def sb(name, shape, dt=F32, stack=None):
        return (stack or es).enter_context(nc.sbuf_tensor("s_" + name, list(shape), dt))

    ps = [es.enter_context(nc.psum_tensor("ps%d" % i, [128, 512], F32)) for i in range(8)]
    PK = [("ps", i) for i in range(8)]

    def wview(w, c0, ncols):
        return w[:, c0:c0 + ncols].rearrange("(k p) n -> p k n", p=128)

    def dbg(name, ap, key):
        if name not in cfg.debug:
            return
        shape = list(ap.shape)
        t = nc.dram_tensor("dbg_" + name, shape, ap.dtype, kind="ExternalOutput").ap()
        dbg_outs[name] = t
        P.dma("sp", t, ap, reads=list(key) if isinstance(key, list) else [key], lane=("dbg", name), final=True)

    consts = sb("consts", [128, NCP])
    P.dma("sp", consts[:], consts_d[:, 0:NCP], writes=["consts"])
    ident_f = consts[:, C_ID:C_ID + 128]
    ones_f = consts[:, C_ONE:C_ONE + 128]
    cbf = sb("cbf", [128, 384 + 2048], BF16)
    P.op("dve", lambda e: e.tensor_copy(out=cbf[:, 0:384], in_=consts[:, 0:384]), reads=["consts"], writes=["cbf"])
    ident_b = cbf[:, C_ID:C_ID + 128]
    flg = sb("flg", [128, 2])
    P.dma("sp", flg[:], flags, writes=["flg"])
    vf = flg[:, 0:1]
    pm = flg[:, 1:2]
    vecs_tm = sb("vecs_tm", [128, 128])
    P.dma("sp", vecs_tm[:], vecs_d, writes=["vecs_tm"])
    vfm = sb("vfm", [128, 128])
    P.op("pe", lambda e: e.transpose(ps[7][:, 0:128], vecs_tm[:], ident_f), reads=["vecs_tm", "consts"], writes=[PK[7]])
    P.op("dve", lambda e: e.tensor_copy(out=vfm[:], in_=ps[7][:, 0:128]), reads=[PK[7]], writes=["vfm"])
    V_N1, V_N2, V_CW, V_CB, V_BR, V_BI, V_LAM = 48, 56, 64, 96, 104, 112, 120
    normf_b = sb("normf_b", [128, D])
    P.dma("sp", normf_b[:], normf_d.partition_broadcast(128), writes=["normf_b"])
    bfb = sb("bfb", [128, H])
    P.dma("sp", bfb[:], b_f.partition_broadcast(128), writes=["bfb"])
    g1b = sb("g1b", [128, D])
    g2b = sb("g2b", [128, D])
    g1s = sb("g1s", [NS, D])
    g2s = sb("g2s", [NS, D])
    AB = sb("AB", [128, 4, KC])
    ABs = sb("ABs", [128, 4, KC, NS])
    cl = sb("cl", [128, 2, KC])
    hsT = sb("hsT", [128, KC, NS], BF16)

    with ExitStack() as s0:
        m_fm = sb("m_fm", [128, 48, 33], stack=s0)
        mtmp = sb("mtmp", [128, 2048], stack=s0)
        P.dma("sp", mtmp[:], consts_d[:, C_BM_SRC:C_BM_SRC + 2048], writes=["mtmp"])
        P.op("dve", lambda e: e.tensor_copy(out=cbf[:, C_BM:C_BM + 2048], in_=mtmp[:]), reads=["mtmp"], writes=["cbf"])
        c_tm = sb("c_tm", [33, D], stack=s0)
        P.dma("sp", c_tm[:], crow, writes=["c_tm"])
        P.op("act", lambda e: e.activation(out=c_tm[:], in_=c_tm[:], func=AF.Silu), reads=["c_tm"], writes=["c_tm"])
        silT = sb("silT", [128, KC, 33], BF16, stack=s0)
        for kc in range(KC):
            P.op("pe", lambda e, kc=kc: e.transpose(ps[7][:, kc * 33:(kc + 1) * 33], c_tm[:, kc * 128:(kc + 1) * 128], ident_f[0:33, 0:33]),
                 reads=["c_tm", "consts"], writes=[PK[7]])
        P.op("dve", lambda e: e.tensor_copy(out=silT[:].rearrange("p k n -> p (k n)"), in_=ps[7][:, 0:KC * 33]), reads=[PK[7]], writes=["silT"])
        adab = sb("adab", [1, 6 * D], stack=s0)
        P.dma("sp", adab[:], ada_b, writes=["adab"])
        m_tm = sb("m_tm", [33, 6 * D], stack=s0)
        awb = [sb("awb%d" % i, [128, KC, 512], BF16, stack=s0) for i in range(2)]
        for j in range(12):
            wk = ("awb", j % 2)
            P.dma("pool", awb[j % 2][:], wview(ada_w, j * 512, 512), writes=[wk])
            pk = PK[j % 2]

            def mm(e, j=j):
                for kc in range(KC):
                    e.matmul(ps[j % 2][0:33, :], lhsT=silT[:, kc, :], rhs=awb[j % 2][:, kc, :], start=(kc == 0), stop=False)
                return e.matmul(ps[j % 2][0:33, :], lhsT=ones_f[0:1, 0:33], rhs=adab[0:1, j * 512:(j + 1) * 512], start=False, stop=True)
            P.op("pe", mm, reads=["silT", wk, "adab", "consts"], writes=[pk])
            P.op("act", lambda e, j=j: e.activation(out=m_tm[:, j * 512:(j + 1) * 512], in_=ps[j % 2][0:33, :], func=AF.Copy),
                 reads=[pk], writes=["m_tm"])
        for g in range(4):
            for jj in range(12):
                j = g * 12 + jj
                P.op("pe", lambda e, j=j, jj=jj, g=g: e.transpose(ps[g][:, jj * 33:(jj + 1) * 33], m_tm[:, j * 128:(j + 1) * 128], ident_f[0:33, 0:33]),
                     reads=["m_tm", "consts"], writes=[PK[g]])
            P.op("dve", lambda e, g=g: e.tensor_copy(out=m_fm[:, g * 12:(g + 1) * 12, :].rearrange("p a b -> p (a b)"), in_=ps[g][:, 0:12 * 33]),
                 reads=[PK[g]], writes=["m_fm"])
        for gi, (gt, c0) in enumerate(((g1b, 2 * D), (g2b, 5 * D))):
            for hf in range(2):
                pk = PK[4 + hf]
                P.op("pe", lambda e, c0=c0, hf=hf: e.matmul(ps[4 + hf][:, :], lhsT=consts[0:33, C_SEL32:C_SEL32 + 128],
                                                            rhs=m_tm[:, c0 + hf * 512:c0 + (hf + 1) * 512], start=True, stop=True),
                     reads=["m_tm", "consts"], writes=[pk])
                P.op("dve", lambda e, gt=gt, hf=hf: e.tensor_copy(out=gt[:, hf * 512:(hf + 1) * 512], in_=ps[4 + hf][:, :]),
                     reads=[pk], writes=["g%db" % (gi + 1)])
        P.op("dve", lambda e: e.tensor_copy(out=g1s[:], in_=m_tm[0:NS, 2 * D:3 * D]), reads=["m_tm"], writes=["g1s"])
        P.op("dve", lambda e: e.tensor_copy(out=g2s[:], in_=m_tm[0:NS, 5 * D:6 * D]), reads=["m_tm"], writes=["g2s"])
        for i, (sc_c, sh_c, vn) in enumerate(((8, 0, V_N1), (32, 24, V_N2))):
            P.op("dve", lambda e, i=i, sc_c=sc_c, vn=vn: e.scalar_tensor_tensor(
                out=AB[:, 2 * i, :], in0=m_fm[:, sc_c:sc_c + KC, 32], scalar=1.0, in1=vfm[:, vn:vn + KC], op0=ALU.add, op1=ALU.mult),
                reads=["m_fm", "vfm"], writes=["AB"])
            P.op("dve", lambda e, i=i, sh_c=sh_c: e.tensor_copy(out=AB[:, 2 * i + 1, :], in_=m_fm[:, sh_c:sh_c + KC, 32]),
                 reads=["m_fm"], writes=["AB"])
            P.op("dve", lambda e, i=i, sc_c=sc_c, vn=vn: e.scalar_tensor_tensor(
                out=ABs[:, 2 * i, :, :], in0=m_fm[:, sc_c:sc_c + KC, 0:NS], scalar=1.0,
                in1=vfm[:, vn:vn + KC].unsqueeze(2).to_broadcast([128, KC, NS]), op0=ALU.add, op1=ALU.mult),
                reads=["m_fm", "vfm"], writes=["ABs"])
            P.op("dve", lambda e, i=i, sh_c=sh_c: e.tensor_copy(out=ABs[:, 2 * i + 1, :, :], in_=m_fm[:, sh_c:sh_c + KC, 0:NS]),
                 reads=["m_fm"], writes=["ABs"])
    P.fence()
    P.op("act", lambda e: e.activation(out=cl[:, 0, :], in_=vfm[:, V_LAM:V_LAM + KC], func=AF.Exp, scale=-1.0), reads=["vfm"], writes=["cl"])
    P.op("act", lambda e: e.activation(out=cl[:, 0, :], in_=cl[:, 0, :], func=AF.Ln, bias=1.0), reads=["cl"], writes=["cl"])
    P.op("dve", lambda e: e.tensor_scalar(out=cl[:, 1, :], in0=cl[:, 0, :], scalar1=-16.0, scalar2=None, op0=ALU.mult), reads=["cl"], writes=["cl"])
    P.op("dve", lambda e: e.tensor_scalar(out=cl[:, 0, :], in0=cl[:, 0, :], scalar1=-8.0, scalar2=None, op0=ALU.mult), reads=["cl"], writes=["cl"])

    dbg("AB", AB[:], "AB")
    dbg("cl", cl[:], "cl")
    dbg("g1b", g1b[:], "g1b")
    if cfg.stop == 0:
        P.finish()
        return nc, P, es, dbg_outs

    lfs_tm = sb("lfs_tm", [NS, H])
    hnT = sb("hnT", [128, KC, NS])
    xrsT = sb("xrsT", [128, KC, NS])
    omvf = sb("omvf", [128, 1])
    mergedsT = sb("mergedsT", [128, KC, NS], BF16)
    oT_d = nc.dram_tensor("oT_scr", [128, KC, S_OWN], BF16, kind="Internal").ap()
    ya_d = nc.dram_tensor("ya_scr", [128, KC, S_OWN], BF16, kind="Internal").ap()
    mg_d = nc.dram_tensor("mg_scr", [128, KC, S_OWN], BF16, kind="Internal").ap()
    sH = ExitStack()
    es.callback(sH.close)
    hT = sb("hT", [128, KC, S_ALL], BF16, stack=sH)
    osT = sb("osT", [128, KC, NS], BF16, stack=sH)
    yainsT = sb("yainsT", [128, KC, NS], BF16, stack=sH)
    qs_tm = sb("qs_tm", [NS, D], stack=sH)
    ks_tm = sb("ks_tm", [NS, D], stack=sH)
    vs_tm = sb("vs_tm", [NS, D], stack=sH)
    os_tm = sb("os_tm", [NS, D], stack=sH)
    EPS = 1e-6

    def norm_block(stk, x_tile, xkey, nt, stat, statkey, xn, xnkey, junk, idx):
        P.op("act", lambda e: e.activation(out=junk[0:nt, :], in_=x_tile, func=AF.Square, accum_out=stat[0:nt, 0:1]),
             reads=[xkey], writes=[statkey, ("junk", idx)])
        if idx == 0 and nt == 128:
            dbg("junk0", junk[:, :], ("junk", 0))
        P.op("act", lambda e: e.activation(out=stat[0:nt, 1:2], in_=stat[0:nt, 0:1], func=AF.Ln, scale=1.0 / D, bias=EPS),
             reads=[statkey], writes=[statkey])
        P.op("act", lambda e: e.activation(out=stat[0:nt, 1:2], in_=stat[0:nt, 1:2], func=AF.Exp, scale=-0.5),
             reads=[statkey], writes=[statkey])
        P.op("act", lambda e: e.activation(out=xn[0:nt, :], in_=x_tile, func=AF.Copy, scale=stat[0:nt, 1:2]),
             reads=[xkey, statkey], writes=[xnkey])

    def to_fm(xn, xnkey, nt, bank, ident):
        if xn.dtype == BF16:
            pv = ps[bank][:].bitcast(BF16)
        else:
            pv = ps[bank][:]

        def tr(e):
            ins = None
            for kc in range(KC):
                ins = e.transpose(pv[:, kc * nt:(kc + 1) * nt], xn[0:nt, kc * 128:(kc + 1) * 128], ident[0:nt, 0:nt])
            return ins
        P.op("pe", tr, reads=[xnkey, "consts", "cbf"], writes=[PK[bank]])
        return pv

    with ExitStack() as s1:
        xt = [sb("xt%d" % i, [128, D], stack=s1) for i in range(3)]
        xn = [sb("xn%d" % i, [128, D], BF16, stack=s1) for i in range(2)]
        junk = sb("junk", [128, D], F32, stack=s1)
        st1 = sb("st1", [128, 2 * (NB_ALL + 1)], stack=s1)
        for b in range(NB_ALL + 1):
            P.op("dve", lambda e, b=b: e.memset(st1[:, 2 * b:2 * b + 2], 0.0), writes=[("st1", b)])
        P.op("act", lambda e: e.activation(out=junk[:, 0:128], in_=consts[:, 0:128], func=AF.Square, accum_out=st1[:, 0:1]),
             reads=["consts"], writes=[("st1", 0)])
        P.op("dve", lambda e: e.memset(st1[:, 0:2], 0.0), writes=[("st1", 0)])
        for b in range(NB_ALL):
            xk = ("xt", b % 3)
            P.dma("sp", xt[b % 3][:], xa[b * 128:(b + 1) * 128, :], writes=[xk])
            nk = ("xn", b % 2)
            norm_block(s1, xt[b % 3][:], xk, 128, st1[:, 2 * b:2 * b + 2], ("st1", b), xn[b % 2], nk, junk, b)
            if b == 0:
                dbg("xt0", xt[0][:], xk)
                dbg("xn0", xn[0][:], nk)
                dbg("st0", st1[:, 0:2], ("st1", 0))
            bank = b % 2
            pv = to_fm(xn[b % 2], nk, 128, bank, ident_b)
            for kc in range(KC):
                P.op("dve", lambda e, kc=kc, pv=pv, b=b: e.tensor_scalar(
                    out=hT[:, kc, b * 128:(b + 1) * 128], in0=pv[:, kc * 128:(kc + 1) * 128],
                    scalar1=AB[:, 0, kc:kc + 1], scalar2=AB[:, 1, kc:kc + 1], op0=ALU.mult, op1=ALU.add),
                    reads=[PK[bank], "AB"], writes=[("hT", b)])
        xs_t = sb("xs_t", [NS, D], stack=s1)
        P.dma("sp", xs_t[:], xs_d, writes=["xs_t"])
        xns = sb("xns", [NS, D], BF16, stack=s1)
        norm_block(s1, xs_t[:], "xs_t", NS, st1[:, 2 * NB_ALL:2 * NB_ALL + 2], ("st1", NB_ALL), xns, "xns", junk, 999)
        pv = to_fm(xns, "xns", NS, 2, ident_b)
        tmp_s = sb("tmp_s", [128, KC, NS], stack=s1)
        P.op("dve", lambda e: e.tensor_tensor(out=tmp_s[:], in0=pv[:, 0:KC * NS].rearrange("p (k n) -> p k n", n=NS), in1=ABs[:, 0, :, :], op=ALU.mult),
             reads=[PK[2], "ABs"], writes=["tmp_s"])
        P.op("dve", lambda e: e.tensor_tensor(out=hsT[:], in0=tmp_s[:], in1=ABs[:, 1, :, :], op=ALU.add),
             reads=["tmp_s", "ABs"], writes=["hsT"])
    P.fence()
    dbg("hT", hT[:, :, :], [("hT", b) for b in range(NB_ALL)])
    dbg("hsT", hsT[:], "hsT")
    if cfg.stop == 1:
        P.finish()
        return nc, P, es, dbg_outs

    def wdma(tile_ap, w, c0, ncols, key):
        P.dma("pool", tile_ap, wview(w, c0, ncols), writes=[key])

    with ExitStack() as s2:
        NBH = NB_ALL * H
        GTk = sb("GTk", [48, S_ALL], BF16, stack=s2)
        GTq = sb("GTq", [48, S_OWN], BF16, stack=s2)
        s2f = ExitStack()
        wfl = sb("wfl", [128, KC, H], BF16, stack=s2f)
        wdma(wfl[:], w_in, FLO, H, "wfl")
        sp_t = sb("sp_t", [128, NB_ALL, H], stack=s2f)
        G = sb("G", [128, NB_ALL, H], stack=s2f)
        def flmm(e):
            ins = None
            for b in range(NB_ALL):
                for kc in range(KC):
                    ins = e.matmul(ps[7][:, b * H:(b + 1) * H], lhsT=hT[:, kc, b * 128:(b + 1) * 128], rhs=wfl[:, kc, :],
                                   start=(kc == 0), stop=(kc == KC - 1))
            return ins
        P.op("pe", flmm, reads=[("hT", b) for b in range(NB_ALL)] + ["wfl"], writes=[PK[7]])
        P.op("dve", lambda e: e.tensor_tensor(out=sp_t[:], in0=ps[7][:, 0:NBH].rearrange("p (b h) -> p b h", h=H),
                                              in1=bfb[:].unsqueeze(1).to_broadcast([128, NB_ALL, H]), op=ALU.add),
             reads=[PK[7], "bfb"], writes=["sp_t"])
        P.op("act", lambda e: e.activation(out=sp_t[:], in_=sp_t[:], func=AF.Exp, scale=-1.0), reads=["sp_t"], writes=["sp_t"])
        P.op("act", lambda e: e.activation(out=sp_t[:], in_=sp_t[:], func=AF.Ln, bias=1.0), reads=["sp_t"], writes=["sp_t"])
        def flmm_s(e):
            ins = None
            for kc in range(KC):
                ins = e.matmul(ps[6][0:NS, 0:H], lhsT=hsT[:, kc, :], rhs=wfl[:, kc, :], start=(kc == 0), stop=(kc == KC - 1))
            return ins
        P.op("pe", flmm_s, reads=["hsT", "wfl"], writes=[PK[6]])
        P.op("dve", lambda e: e.tensor_tensor(out=lfs_tm[:], in0=ps[6][0:NS, 0:H], in1=bfb[0:NS, :], op=ALU.add),
             reads=[PK[6], "bfb"], writes=["lfs_tm"])
        P.op("act", lambda e: e.activation(out=lfs_tm[:], in_=lfs_tm[:], func=AF.Exp, scale=-1.0), reads=["lfs_tm"], writes=["lfs_tm"])
        P.op("act", lambda e: e.activation(out=lfs_tm[:], in_=lfs_tm[:], func=AF.Ln, bias=1.0), reads=["lfs_tm"], writes=["lfs_tm"])
        P.op("dve", lambda e: e.tensor_scalar(out=lfs_tm[:], in0=lfs_tm[:], scalar1=-1.0, scalar2=None, op0=ALU.mult),
             reads=["lfs_tm"], writes=["lfs_tm"])
        P.dma("sp", lfs_o, lfs_tm[:], reads=["lfs_tm"], lane="lfs_out", final=True)
        lfst = sb("lfst", [128, NB_OWN, H], stack=s2f)
        P.op("dve", lambda e: e.tensor_scalar(out=lfst[:], in0=sp_t[:, NB_PRE:, :], scalar1=-1.0, scalar2=None, op0=ALU.mult),
             reads=["sp_t"], writes=["lfst"])
        P.dma("sp", lf_o.rearrange("(b p) h -> p b h", p=128), lfst[:], reads=["lfst"], lane="lf_out", final=True)
        spf = sp_t[:].rearrange("p b h -> p (b h)")
        P.op("pe", lambda e: e.matmul(ps[6][:, 0:NBH], lhsT=consts[:, C_U:C_U + 128], rhs=spf, start=True, stop=True),
             reads=["sp_t", "consts"], writes=[PK[6]])
        P.op("pe", lambda e: e.matmul(ps[7][:, 0:NBH], lhsT=ones_f, rhs=spf, start=True, stop=True),
             reads=["sp_t", "consts"], writes=[PK[7]])
        TA = sb("TA", [128, NB_ALL, H], stack=s2f)
        TB = sb("TB", [128, NB_ALL, H], stack=s2f)
        P.op("dve", lambda e: e.tensor_copy(out=TA[:].rearrange("p b h -> p (b h)"), in_=ps[7][:, 0:NBH]), reads=[PK[7]], writes=["TA"])
        cur, oth, ck_, ok_ = TA, TB, "TA", "TB"
        st = 1
        while st < NB_ALL:
            P.op("dve", lambda e, cur=cur, oth=oth, st=st: e.tensor_copy(out=oth[:, 0:st, :], in_=cur[:, 0:st, :]), reads=[ck_], writes=[ok_])
            P.op("dve", lambda e, cur=cur, oth=oth, st=st: e.tensor_tensor(out=oth[:, st:, :], in0=cur[:, st:, :], in1=cur[:, 0:NB_ALL - st, :], op=ALU.add),
                 reads=[ck_], writes=[ok_])
            cur, oth, ck_, ok_ = oth, cur, ok_, ck_
            st *= 2
        P.op("dve", lambda e, cur=cur: e.tensor_tensor(out=G[:].rearrange("p b h -> p (b h)"), in0=ps[6][:, 0:NBH],
                                                      in1=cur[:].rearrange("p b h -> p (b h)"), op=ALU.add),
             reads=[PK[6], ck_], writes=["G"])
        P.op("dve", lambda e: e.tensor_tensor(out=G[:].rearrange("p b h -> p (b h)"), in0=G[:].rearrange("p b h -> p (b h)"),
                                              in1=ps[7][:, 0:NBH], op=ALU.subtract),
             reads=[PK[7], "G"], writes=["G"])
        Gk = sb("Gk", [128, NB_ALL, H], stack=s2f)
        P.op("dve", lambda e: e.tensor_scalar(out=Gk[:, 0:NB_PRE, :], in0=G[:, 0:NB_PRE, :], scalar1=pm, scalar2=None, op0=ALU.add),
             reads=["G", "flg"], writes=["Gk"])
        P.op("dve", lambda e: e.tensor_copy(out=Gk[:, NB_PRE:, :], in_=G[:, NB_PRE:, :]), reads=["G"], writes=["Gk"])
        Gq = sb("Gq", [128, NB_OWN, H], stack=s2f)
        P.op("dve", lambda e: e.tensor_scalar(out=Gq[:], in0=G[:, NB_PRE:, :], scalar1=-1.0, scalar2=None, op0=ALU.mult),
             reads=["G"], writes=["Gq"])
        pk_ = sb("pk_", [128, NB_ALL, H, 3], BF16, stack=s2f)
        pq_ = sb("pq_", [128, NB_OWN, H, 3], BF16, stack=s2f)
        rr = sb("rr", [128, NB_ALL, H], stack=s2f)

        def split3(src, srckey, dst, dstkey, nb):
            P.op("dve", lambda e: e.tensor_copy(out=dst[:, :, :, 0], in_=src[:, 0:nb, :]), reads=[srckey], writes=[dstkey])
            P.op("dve", lambda e: e.tensor_tensor(out=rr[:, 0:nb, :], in0=src[:, 0:nb, :], in1=dst[:, :, :, 0], op=ALU.subtract),
                 reads=[srckey, dstkey], writes=["rr"])
            P.op("dve", lambda e: e.tensor_copy(out=dst[:, :, :, 1], in_=rr[:, 0:nb, :]), reads=["rr"], writes=[dstkey])
            P.op("dve", lambda e: e.tensor_tensor(out=rr[:, 0:nb, :], in0=rr[:, 0:nb, :], in1=dst[:, :, :, 1], op=ALU.subtract),
                 reads=["rr", dstkey], writes=["rr"])
            P.op("dve", lambda e: e.tensor_copy(out=dst[:, :, :, 2], in_=rr[:, 0:nb, :]), reads=["rr"], writes=[dstkey])
        split3(Gk, "Gk", pk_, "pk_", NB_ALL)
        split3(Gq, "Gq", pq_, "pq_", NB_OWN)
        for (src, srckey, dstT, dkey, nb) in ((pk_, "pk_", GTk, "GTk", NB_ALL), (pq_, "pq_", GTq, "GTq", NB_OWN)):
            gs = min(8, nb)
            for g0 in range(0, nb, gs):
                bank = 4 + (g0 // 8) % 2
                pvb = ps[bank][:].bitcast(BF16)

                def trg(e, src=src, g0=g0, pvb=pvb, gs=gs):
                    ins = None
                    for b in range(g0, g0 + gs):
                        ins = e.transpose(pvb[0:48, (b - g0) * 128:(b - g0 + 1) * 128],
                                          src[:, b, :, :].rearrange("p h c -> p (h c)"), ident_b)
                    return ins
                P.op("pe", trg, reads=[srckey, "cbf"], writes=[PK[bank]])
                P.op("dve", lambda e, dstT=dstT, g0=g0, pvb=pvb, gs=gs: e.tensor_copy(out=dstT[:, g0 * 128:(g0 + gs) * 128], in_=pvb[0:48, 0:gs * 128]),
                     reads=[PK[bank]], writes=[dkey])

        s2f.close()
        P.fence()
        qa = [sb("qa%d" % i, [128, S_OWN], BF16, stack=s2) for i in range(2)]
        ka = [sb("ka%d" % i, [128, S_ALL], BF16, stack=s2) for i in range(2)]
        for i in range(2):
            P.op("dve", lambda e, i=i: e.memset(qa[i][64:128, :], 1.0), writes=[("qa", i)])
            P.op("dve", lambda e, i=i: e.memset(ka[i][64:128, :], 1.0), writes=[("ka", i)])
        vaug = [sb("vaug%d" % i, [128, NB_ALL, 128], BF16, stack=s2) for i in range(2)]
        P.op("pool", lambda e: e.memset(vaug[0][:, :, 64:128], 1.0), writes=[("vaug", 0)])
        P.op("pool", lambda e: e.memset(vaug[1][:, :, 0:64], 1.0), writes=[("vaug", 1)])
        wqk = [sb("wqk%d" % i, [128, KC, 128], BF16, stack=s2) for i in range(2)]
        wkv = [sb("wkv%d" % i, [128, KC, 256], BF16, stack=s2) for i in range(2)]
        kvst = [sb("kvst%d" % i, [128, 2, 256], stack=s2) for i in range(2)]
        PT = [sb("PT%d" % i, [128, 512], BF16, stack=s2) for i in range(3)]
        rl = sb("rl", [128, 512], stack=s2)
        ostg = [sb("ostg%d" % i, [128, 512], BF16, stack=s2) for i in range(2)]
        kvs_st = sb("kvs_st", [NS, 256], stack=s2)
        bmb = [cbf[:, C_BM + 512 * j:C_BM + 512 * (j + 1)] for j in range(4)]
        kvcnt = [0]

        def pair_kv(pr):
            w = wkv[pr % 2]
            wk = ("wkv", pr % 2)
            wdma(w[:, :, 0:128], w_in, KO + pr * 128, 128, wk)
            wdma(w[:, :, 128:256], w_in, VO + pr * 128, 128, wk)
            import os
            for g0 in range(0, 0 if os.environ.get("ONLYOWN") else NB_PRE, 4):
                bank = 5 + (kvcnt[0] % 2)
                kvcnt[0] += 1

                def mm(e, g0=g0, bank=bank):
                    ins = None
                    for b in range(g0, g0 + 4):
                        for kc in range(KC):
                            ins = e.matmul(ps[bank][:, (b - g0) * 128:(b - g0 + 1) * 128], lhsT=hT[:, kc, b * 128:(b + 1) * 128],
                                           rhs=w[:, kc, 128:256], start=(kc == 0), stop=(kc == KC - 1))
                    return ins
                P.op("pe", mm, reads=[("hT", b) for b in range(g0, g0 + 4)] + [wk], writes=[PK[bank]])
                pv4 = ps[bank][:, :].rearrange("p (b c) -> p b c", c=128)
                P.op("dve", lambda e, g0=g0, pv4=pv4: e.tensor_copy(out=vaug[0][:, g0:g0 + 4, 0:64], in_=pv4[:, :, 0:64]),
                     reads=[PK[bank]], writes=[("vaug", 0)])
                P.op("dve", lambda e, g0=g0, pv4=pv4: e.tensor_copy(out=vaug[1][:, g0:g0 + 4, 64:128], in_=pv4[:, :, 64:128]),
                     reads=[PK[bank]], writes=[("vaug", 1)])
            import os
            BIS = int(os.environ.get("BIS", "99"))
            if BIS <= 1:
                return
            P.fence()
            for g0 in range(NB_PRE, NB_ALL, 2):
                bank = 5 + (kvcnt[0] % 2)
                if os.environ.get("BANK5"):
                    bank = 5
                sl = kvcnt[0] % 2
                kvcnt[0] += 1

                def mm2(e, g0=g0, bank=bank):
                    ins = None
                    for b in range(g0, g0 + 2):
                        for kc in range(KC):
                            if os.environ.get("VONLY"):
                                ins = e.matmul(ps[bank][:, (b - g0) * 256 + 128:(b - g0 + 1) * 256], lhsT=hT[:, kc, b * 128:(b + 1) * 128],
                                               rhs=w[:, kc, 128:256], start=(kc == 0), stop=(kc == KC - 1))
                            else:
                                ins = e.matmul(ps[bank][:, (b - g0) * 256:(b - g0 + 1) * 256], lhsT=hT[:, kc, b * 128:(b + 1) * 128],
                                               rhs=w[:, kc, :], start=(kc == 0), stop=(kc == KC - 1))
                    return ins
                P.op("pe", mm2, reads=[("hT", b) for b in range(g0, g0 + 2)] + [wk], writes=[PK[bank]])
                if os.environ.get("NOEVAC"):
                    continue
                P.op("act", lambda e, sl=sl, bank=bank: e.activation(out=kvst[sl][:].rearrange("p b c -> p (b c)"), in_=ps[bank][:, :], func=AF.Copy),
                     reads=[PK[bank]], writes=[("kvst", sl)])
                P.op("dve", lambda e, g0=g0, sl=sl: e.tensor_copy(out=vaug[0][:, g0:g0 + 2, 0:64], in_=kvst[sl][:, :, 128:192]),
                     reads=[("kvst", sl)], writes=[("vaug", 0)])
                P.op("dve", lambda e, g0=g0, sl=sl: e.tensor_copy(out=vaug[1][:, g0:g0 + 2, 64:128], in_=kvst[sl][:, :, 192:256]),
                     reads=[("kvst", sl)], writes=[("vaug", 1)])
                ob = g0 - NB_PRE
                if BIS <= 2:
                    continue
                P.dma("sp", k_o[ob * 128:(ob + 2) * 128, pr * 128:(pr + 1) * 128].rearrange("(b p) c -> p b c", p=128),
                      kvst[sl][:, :, 0:128], reads=[("kvst", sl)], lane=("kvst", sl), final=True)
                P.dma("sp", v_o[ob * 128:(ob + 2) * 128, pr * 128:(pr + 1) * 128].rearrange("(b p) c -> p b c", p=128),
                      kvst[sl][:, :, 128:256], reads=[("kvst", sl)], lane=("kvst", sl), final=True)
            if BIS <= 3:
                return
            def mms(e):
                ins = None
                for kc in range(KC):
                    ins = e.matmul(ps[7][0:NS, 0:256], lhsT=hsT[:, kc, :], rhs=w[:, kc, :], start=(kc == 0), stop=(kc == KC - 1))
                return ins
            P.op("pe", mms, reads=["hsT", wk], writes=[PK[7]])
            P.op("dve", lambda e: e.tensor_copy(out=ks_tm[:, pr * 128:(pr + 1) * 128], in_=ps[7][0:NS, 0:128]), reads=[PK[7]], writes=["ks_tm"])
            P.op("dve", lambda e: e.tensor_copy(out=vs_tm[:, pr * 128:(pr + 1) * 128], in_=ps[7][0:NS, 128:256]), reads=[PK[7]], writes=["vs_tm"])

        qkcnt = [0]

        def head_qk(h):
            import os
            BIS = int(os.environ.get("BIS", "99"))
            if BIS <= 4:
                return
            w = wqk[h % 2]
            wk = ("wqk", h % 2)
            wdma(w[:, :, 0:64], w_in, QO + h * 64, 64, wk)
            wdma(w[:, :, 64:128], w_in, KO + h * 64, 64, wk)
            qk_, kk_ = ("qa", h % 2), ("ka", h % 2)
            for t in range(NT_ALL):
                bank = 5 + (qkcnt[0] % 2)
                qkcnt[0] += 1

                def mm(e, t=t, bank=bank):
                    ins = None
                    for kc in range(KC):
                        ins = e.matmul(ps[bank][:, :], lhsT=w[:, kc, :], rhs=hT[:, kc, t * 512:(t + 1) * 512], start=(kc == 0), stop=(kc == KC - 1))
                    return ins
                P.op("pe", mm, reads=[("hT", b) for b in range(4 * t, 4 * t + 4)] + [wk], writes=[PK[bank]])
                P.op("dve", lambda e, t=t, bank=bank: e.tensor_copy(out=ka[h % 2][0:64, t * 512:(t + 1) * 512], in_=ps[bank][64:128, :]),
                     reads=[PK[bank]], writes=[kk_])
                if t >= NT_PRE:
                    P.op("dve", lambda e, t=t, bank=bank: e.tensor_scalar(out=qa[h % 2][0:64, (t - NT_PRE) * 512:(t - NT_PRE + 1) * 512], in0=ps[bank][0:64, :],
                                                                      scalar1=0.125, scalar2=None, op0=ALU.mult),
                         reads=[PK[bank]], writes=[qk_])
            import os
            if not os.environ.get("SKIP_AUG"):
                P.dma("sp", qa[h % 2][64:67, :], GTq[3 * h:3 * h + 3, :], reads=["GTq"], writes=[qk_], lane=qk_)
                P.dma("sp", ka[h % 2][67:70, :], GTk[3 * h:3 * h + 3, :], reads=["GTk"], writes=[kk_], lane=kk_)
            def mms(e):
                ins = None
                for kc in range(KC):
                    ins = e.matmul(ps[7][0:NS, 0:128], lhsT=hsT[:, kc, :], rhs=w[:, kc, :], start=(kc == 0), stop=(kc == KC - 1))
                return ins
            P.op("pe", mms, reads=["hsT", wk], writes=[PK[7]])
            P.op("dve", lambda e: e.tensor_scalar(out=qs_tm[:, h * 64:(h + 1) * 64], in0=ps[7][0:NS, 0:64], scalar1=0.125, scalar2=None, op0=ALU.mult),
                 reads=[PK[7]], writes=["qs_tm"])

        scnt = [0]

        def head_attn(h):
            qk_, kk_ = ("qa", h % 2), ("ka", h % 2)
            va = vaug[h % 2]
            vk = ("vaug", h % 2)
            for i in range(NT_OWN):
                nkb = NB_PRE + 4 * i + 4
                obank = 3 + (i % 2)
                ok = PK[obank]
                slots = []

                def S_step(kb):
                    sl = scnt[0] % 3
                    scnt[0] += 1
                    j = kb - (NB_PRE + 4 * i)

                    def mm(e):
                        ins = e.matmul(ps[sl][:, :], lhsT=ka[h % 2][0:70, kb * 128:(kb + 1) * 128], rhs=qa[h % 2][0:70, i * 512:(i + 1) * 512],
                                       start=True, stop=(j < 0))
                        if j >= 0:
                            ins = e.matmul(ps[sl][:, :], lhsT=ident_b, rhs=bmb[j], start=False, stop=True)
                        return ins
                    P.op("pe", mm, reads=[qk_, kk_, "cbf"], writes=[PK[sl]])
                    P.op("act", lambda e: e.activation(out=PT[sl][:, :], in_=ps[sl][:, :], func=AF.Exp), reads=[PK[sl]], writes=[("PT", sl)])
                    return sl

                def PV_step(kb, sl):
                    P.op("pe", lambda e: e.matmul(ps[obank][:, :], lhsT=va[:, kb, :], rhs=PT[sl][:, :], start=(kb == 0), stop=(kb == nkb - 1)),
                         reads=[("PT", sl), vk], writes=[ok])
                pend = []
                for kb in range(nkb):
                    pend.append((kb, S_step(kb)))
                    if len(pend) > 2:
                        PV_step(*pend.pop(0))
                while pend:
                    PV_step(*pend.pop(0))
                olo, llo = (0, 64) if h % 2 == 0 else (64, 0)
                P.op("act", lambda e: e.activation(out=rl[llo:llo + 64, :], in_=ps[obank][llo:llo + 64, :], func=AF.Ln), reads=[ok], writes=["rl"])
                P.op("act", lambda e: e.activation(out=rl[llo:llo + 64, :], in_=rl[llo:llo + 64, :], func=AF.Exp, scale=-1.0), reads=["rl"], writes=["rl"])
                osl = (h * NT_OWN + i) % 2
                P.op("dve", lambda e: e.tensor_tensor(out=ostg[osl][olo:olo + 64, :], in0=ps[obank][olo:olo + 64, :],
                                                      in1=rl[llo:llo + 64, :], op=ALU.mult),
                     reads=[ok, "rl"], writes=[("ostg", osl)])
                P.dma("sp", oT_d[olo:olo + 64, h // 2, i * 512:(i + 1) * 512], ostg[osl][olo:olo + 64, :], reads=[("ostg", osl)], writes=["oT_d"], lane=("ostg", osl))

        for h in range(H):
            if h % 2 == 0:
                pair_kv(h // 2)
            head_qk(h)
            if cfg.stop == 1.5:
                dbg("qa0", qa[0][0:70, :], ("qa", 0))
                dbg("ka0", ka[0][0:70, :], ("ka", 0))
                dbg("vaug0", vaug[0][:], ("vaug", 0))
                P.fence()
                P.finish()
                return nc, P, es, dbg_outs
            head_attn(h)
            if cfg.stop == 1.7:
                break
    P.fence()
    dbg("qs_tm", qs_tm[:], "qs_tm")
    if cfg.stop == 2:
        P.finish()
        return nc, P, es, dbg_outs
    P.op("dve", lambda e: e.tensor_scalar(out=omvf[:], in0=vf, scalar1=-1.0, scalar2=1.0, op0=ALU.mult, op1=ALU.add), reads=["flg"], writes=["omvf"])
    with ExitStack() as s3:
        sc_tm = sb("sc_tm", [NS, 3 * D], stack=s3)
        P.dma("sp", sc_tm[:], sconv_d.rearrange("b j d -> b (j d)"), writes=["sc_tm"])
        h0_tm = sb("h0_tm", [NS, D], stack=s3)
        P.dma("sp", h0_tm[:], srg_d, writes=["h0_tm"])
        scT = sb("scT", [128, 3, KC, NS], stack=s3)
        h0T = sb("h0T", [128, KC, NS], stack=s3)

        def trs(e):
            ins = None
            for j in range(3):
                for kc in range(KC):
                    ins = e.transpose(ps[7][:, (j * KC + kc) * NS:(j * KC + kc + 1) * NS], sc_tm[:, j * D + kc * 128:j * D + (kc + 1) * 128], ident_f[0:NS, 0:NS])
            for kc in range(KC):
                ins = e.transpose(ps[7][:, (3 * KC + kc) * NS:(3 * KC + kc + 1) * NS], h0_tm[:, kc * 128:(kc + 1) * 128], ident_f[0:NS, 0:NS])
            return ins
        P.op("pe", trs, reads=["sc_tm", "h0_tm", "consts"], writes=[PK[7]])
        P.op("dve", lambda e: e.tensor_copy(out=scT[:].rearrange("p j k n -> p (j k n)"), in_=ps[7][:, 0:3 * KC * NS]), reads=[PK[7]], writes=["scT"])
        P.op("dve", lambda e: e.tensor_copy(out=h0T[:].rearrange("p k n -> p (k n)"), in_=ps[7][:, 3 * KC * NS:4 * KC * NS]), reads=[PK[7]], writes=["h0T"])
        P.dma("sp", convs_o[:, 0:2, :], sconv_d[:, 1:3, :], lane="convs_a", final=True)

        wxg = [sb("wxg%d" % i, [128, KC, 256], BF16, stack=s3) for i in range(2)]
        wrb = [sb("wrb%d" % i, [128, 2, 128], BF16, stack=s3) for i in range(2)]
        for i in range(2):
            P.op("dve", lambda e, i=i: e.memset(wrb[i][:], 0.0), writes=[("wrb", i)])
        xrp = [sb("xrp%d" % i, [128, 515], stack=s3) for i in range(2)]
        xc = [sb("xc%d" % i, [128, 512], stack=s3) for i in range(2)]
        xcb = [sb("xcb%d" % i, [128, 512], BF16, stack=s3) for i in range(2)]
        rg = [sb("rg%d" % i, [128, 512], stack=s3) for i in range(2)]
        ig = [sb("ig%d" % i, [128, 512], stack=s3) for i in range(2)]
        at = [sb("at%d" % i, [128, 512], stack=s3) for i in range(2)]
        mt = [sb("mt%d" % i, [128, 512], stack=s3) for i in range(2)]
        ut = [sb("ut%d" % i, [128, 512], stack=s3) for i in range(2)]
        hh = [sb("hh%d" % i, [128, 512], stack=s3) for i in range(2)]
        gl = [sb("gl%d" % i, [128, 512], stack=s3) for i in range(2)]
        ystg = [sb("ystg%d" % i, [128, 512], BF16, stack=s3) for i in range(2)]
        cnt2 = [0]

        def rg_tile(c, w, wk, wb, wbk, src_rhs, srckeys, n, prev, is_prefix, first_own, sample):
            sl = cnt2[0] % 2
            cnt2[0] += 1
            K_ = lambda nm: (nm, sl)
            cw = lambda j: vfm[:, V_CW + j * 8 + c:V_CW + j * 8 + c + 1]
            cb = vfm[:, V_CB + c:V_CB + c + 1]
            def mm(e):
                ins = None
                for kc in range(KC):
                    ins = e.matmul(ps[0][:, 0:n], lhsT=w[:, kc, 0:128], rhs=src_rhs(kc), start=(kc == 0), stop=(kc == KC - 1))
                return ins
            P.op("pe", mm, reads=srckeys + [wk], writes=[PK[0]])
            if sample:
                P.op("dve", lambda e: e.tensor_copy(out=xrsT[:, c, :], in_=ps[0][:, 0:n]), reads=[PK[0]], writes=["xrsT"])
                P.op("dve", lambda e: e.tensor_scalar(out=xc[sl][:, 0:n], in0=ps[0][:, 0:n], scalar1=cw(3), scalar2=cb, op0=ALU.mult, op1=ALU.add),
                     reads=[PK[0], "vfm"], writes=[K_("xc")])
                for j in range(3):
                    P.op("dve", lambda e, j=j: e.scalar_tensor_tensor(out=xc[sl][:, 0:n], in0=scT[:, j, c, :], scalar=cw(j), in1=xc[sl][:, 0:n],
                                                                 op0=ALU.mult, op1=ALU.add),
                         reads=["scT", K_("xc"), "vfm"], writes=[K_("xc")])
            else:
                if is_prefix:
                    P.op("dve", lambda e: e.tensor_scalar(out=xrp[sl][:, 3:515], in0=ps[0][:, :], scalar1=vf, scalar2=None, op0=ALU.mult),
                         reads=[PK[0], "flg"], writes=[K_("xrp")])
                else:
                    P.op("dve", lambda e: e.tensor_copy(out=xrp[sl][:, 3:515], in_=ps[0][:, :]), reads=[PK[0]], writes=[K_("xrp")])
                if prev is None:
                    P.op("dve", lambda e: e.memset(xrp[sl][:, 0:3], 0.0), writes=[K_("xrp")])
                else:
                    P.op("dve", lambda e: e.tensor_copy(out=xrp[sl][:, 0:3], in_=xrp[prev][:, 512:515]), reads=[("xrp", prev)], writes=[K_("xrp")])
                P.op("dve", lambda e: e.tensor_scalar(out=xc[sl][:], in0=xrp[sl][:, 3:515], scalar1=cw(3), scalar2=cb, op0=ALU.mult, op1=ALU.add),
                     reads=[K_("xrp"), "vfm"], writes=[K_("xc")])
                for j in (2, 1, 0):
                    P.op("dve", lambda e, j=j: e.scalar_tensor_tensor(out=xc[sl][:], in0=xrp[sl][:, j:j + 512], scalar=cw(j), in1=xc[sl][:],
                                                                 op0=ALU.mult, op1=ALU.add),
                         reads=[K_("xrp"), K_("xc"), "vfm"], writes=[K_("xc")])
            P.op("pool", lambda e: e.tensor_copy(out=xcb[sl][:, 0:n], in_=xc[sl][:, 0:n]), reads=[K_("xc")], writes=[K_("xcb")])
            P.op("pe", lambda e: e.matmul(ps[1][:, 0:n], lhsT=wb[:, 0, :], rhs=xcb[sl][:, 0:n], start=True, stop=True), reads=[K_("xcb"), wbk], writes=[PK[1]])
            P.op("pe", lambda e: e.matmul(ps[2][:, 0:n], lhsT=wb[:, 1, :], rhs=xcb[sl][:, 0:n], start=True, stop=True), reads=[K_("xcb"), wbk], writes=[PK[2]])
            P.op("act", lambda e: e.activation(out=rg[sl][:, 0:n], in_=ps[1][:, 0:n], func=AF.Sigmoid, bias=vfm[:, V_BR + c:V_BR + c + 1]),
                 reads=[PK[1], "vfm"], writes=[K_("rg")])
            P.op("act", lambda e: e.activation(out=ig[sl][:, 0:n], in_=ps[2][:, 0:n], func=AF.Sigmoid, bias=vfm[:, V_BI + c:V_BI + c + 1]),
                 reads=[PK[2], "vfm"], writes=[K_("ig")])
            P.op("act", lambda e: e.activation(out=at[sl][:, 0:n], in_=rg[sl][:, 0:n], func=AF.Exp, scale=cl[:, 0, c:c + 1]), reads=[K_("rg"), "cl"], writes=[K_("at")])
            P.op("act", lambda e: e.activation(out=mt[sl][:, 0:n], in_=rg[sl][:, 0:n], func=AF.Exp, scale=cl[:, 1, c:c + 1]), reads=[K_("rg"), "cl"], writes=[K_("mt")])
            P.op("dve", lambda e: e.tensor_scalar(out=mt[sl][:, 0:n], in0=mt[sl][:, 0:n], scalar1=-1.0, scalar2=1.0, op0=ALU.mult, op1=ALU.add),
                 reads=[K_("mt")], writes=[K_("mt")])
            P.op("act", lambda e: e.activation(out=mt[sl][:, 0:n], in_=mt[sl][:, 0:n], func=AF.Ln), reads=[K_("mt")], writes=[K_("mt")])
            P.op("act", lambda e: e.activation(out=mt[sl][:, 0:n], in_=mt[sl][:, 0:n], func=AF.Exp, scale=0.5), reads=[K_("mt")], writes=[K_("mt")])
            if (not sample) and is_prefix and prev is None:
                P.op("dve", lambda e: e.memset(mt[sl][:, 0:1], 1.0), reads=[K_("mt")], writes=[K_("mt")])
            if first_own:
                P.op("dve", lambda e: e.tensor_scalar(out=mt[sl][:, 0:1], in0=mt[sl][:, 0:1], scalar1=vf, scalar2=omvf[:, 0:1], op0=ALU.mult, op1=ALU.add),
                     reads=[K_("mt"), "flg", "omvf"], writes=[K_("mt")])
            P.op("dve", lambda e: e.tensor_tensor(out=ut[sl][:, 0:n], in0=mt[sl][:, 0:n], in1=ig[sl][:, 0:n], op=ALU.mult), reads=[K_("mt"), K_("ig")], writes=[K_("ut")])
            P.op("dve", lambda e: e.tensor_tensor(out=ut[sl][:, 0:n], in0=ut[sl][:, 0:n], in1=xc[sl][:, 0:n], op=ALU.mult), reads=[K_("ut"), K_("xc")], writes=[K_("ut")])
            if is_prefix:
                P.op("dve", lambda e: e.tensor_scalar(out=ut[sl][:, 0:n], in0=ut[sl][:, 0:n], scalar1=vf, scalar2=None, op0=ALU.mult),
                     reads=[K_("ut"), "flg"], writes=[K_("ut")])
            if sample:
                P.op("dve", lambda e: e.tensor_tensor(out=hh[sl][:, 0:n], in0=at[sl][:, 0:n], in1=h0T[:, c, :], op=ALU.mult), reads=[K_("at"), "h0T"], writes=[K_("hh")])
                P.op("dve", lambda e: e.tensor_tensor(out=hh[sl][:, 0:n], in0=hh[sl][:, 0:n], in1=ut[sl][:, 0:n], op=ALU.add), reads=[K_("hh"), K_("ut")], writes=[K_("hh")])
                P.op("dve", lambda e: e.tensor_copy(out=hnT[:, c, :], in_=hh[sl][:, 0:n]), reads=[K_("hh")], writes=["hnT"])
            else:
                init = 0.0 if prev is None else hh[prev][:, 511:512]
                P.op("dve", lambda e: e.tensor_tensor_scan(out=hh[sl][:], data0=at[sl][:], data1=ut[sl][:], initial=init, op0=ALU.mult, op1=ALU.add),
                     reads=[K_("at"), K_("ut")] + ([("hh", prev)] if prev is not None else []), writes=[K_("hh")])
            return sl

        def gate_tile(c, w, wk, src_rhs, srckeys, n, sl, dst, dstkey):
            def mm(e):
                ins = None
                for kc in range(KC):
                    ins = e.matmul(ps[3][:, 0:n], lhsT=w[:, kc, 128:256], rhs=src_rhs(kc), start=(kc == 0), stop=(kc == KC - 1))
                return ins
            P.op("pe", mm, reads=srckeys + [wk], writes=[PK[3]])
            P.op("act", lambda e: e.activation(out=gl[sl][:, 0:n], in_=ps[3][:, 0:n], func=AF.Gelu_apprx_tanh), reads=[PK[3]], writes=[("gl", sl)])
            P.op("dve", lambda e: e.tensor_tensor(out=dst, in0=hh[sl][:, 0:n], in1=gl[sl][:, 0:n], op=ALU.mult), reads=[("hh", sl), ("gl", sl)], writes=[dstkey])

        for c in range(KC):
            w = wxg[c % 2]
            wk = ("wxg", c % 2)
            wdma(w[:, :, 0:128], w_in, XR + c * 128, 128, wk)
            wdma(w[:, :, 128:256], w_in, GR + c * 128, 128, wk)
            wb = wrb[c % 2]
            wbk = ("wrb", c % 2)
            for gi, wsrc in enumerate((rg_wr, rg_wi)):
                for q2 in range(2):
                    P.dma("pool", wb[q2 * 64:(q2 + 1) * 64, gi, q2 * 64:(q2 + 1) * 64], wsrc[2 * c + q2], writes=[wbk])
            prev = None
            tl = []
            for t in range(NT_ALL):
                P.rec = []
                sl = rg_tile(c, w, wk, wb, wbk, lambda kc, t=t: hT[:, kc, t * 512:(t + 1) * 512], [("hT", b) for b in range(4 * t, 4 * t + 4)],
                             512, prev, t < NT_PRE, t == NT_PRE, False)
                if t >= NT_PRE:
                    o0 = (t - NT_PRE) * 512
                    ysl = t % 2
                    gate_tile(c, w, wk, lambda kc, t=t: hT[:, kc, t * 512:(t + 1) * 512], [("hT", b) for b in range(4 * t, 4 * t + 4)], 512, sl,
                              ystg[ysl][:, :], ("ystg", ysl))
                    P.dma("sp", ya_d[:, c, o0:o0 + 512], ystg[ysl][:, :], reads=[("ystg", ysl)], writes=["ya_d"], lane=("ystg", ysl))
                prev = sl
                tl.append(P.rec)
                P.rec = None
                if len(tl) == 2:
                    P.replay_interleaved(tl, lag=8)
                    tl = []
            if tl:
                P.replay_interleaved(tl, lag=8)
            P.dma("sp", rg_o[0:1, c * 128:(c + 1) * 128].rearrange("o p -> p o"), hh[prev][:, 511:512], reads=[("hh", prev)], lane=("hh", prev), final=True)
            for j in range(3):
                P.dma("sp", conv_o[j:j + 1, c * 128:(c + 1) * 128].rearrange("o p -> p o"), xrp[prev][:, 512 + j:513 + j], reads=[("xrp", prev)],
                      lane=("xrp", prev), final=True)
            sl = rg_tile(c, w, wk, wb, wbk, lambda kc: hsT[:, kc, :], ["hsT"], NS, None, False, False, True)
            gate_tile(c, w, wk, lambda kc: hsT[:, kc, :], ["hsT"], NS, sl, yainsT[:, c, :], "yainsT")
        for (srcT, skey, dst_ap, lane) in ((hnT, "hnT", rgs_o, "rgs_out"), (xrsT, "xrsT", convs_o[:, 2, :], "convs_b")):
            def trb(e, srcT=srcT):
                ins = None
                for kc in range(KC):
                    ins = e.transpose(ps[4 + kc // 4][0:NS, (kc % 4) * 128:(kc % 4 + 1) * 128], srcT[:, kc, :], ident_f)
                return ins
            P.op("pe", trb, reads=[skey, "consts"], writes=[PK[4], PK[5]])
            stt = sb("stt_" + lane, [NS, D], stack=s3)
            P.op("dve", lambda e, stt=stt: e.tensor_copy(out=stt[:, 0:512], in_=ps[4][0:NS, :]), reads=[PK[4]], writes=["stt_" + lane])
            P.op("dve", lambda e, stt=stt: e.tensor_copy(out=stt[:, 512:1024], in_=ps[5][0:NS, :]), reads=[PK[5]], writes=["stt_" + lane])
            P.dma("sp", dst_ap, stt[:], reads=["stt_" + lane], lane=lane, final=True)
    P.fence()
    dbg("yainsT", yainsT[:], "yainsT")
    if cfg.stop == 3:
        P.finish()
        return nc, P, es, dbg_outs
    with ExitStack() as s4:
        ptb = sb("ptb", [128, NPGS], I32, stack=s4)
        P.dma("sp", ptb[:], pt_d.partition_broadcast(128), writes=["ptb"])
        iot = sb("iot", [128, 1], I32, stack=s4)
        P.op("pool", lambda e: e.iota(iot[:], pattern=[[0, 1]], base=0, channel_multiplier=1), writes=["iot"])
        idx = sb("idx", [128, NPGS], I32, stack=s4)
        P.op("pool", lambda e: e.tensor_scalar(out=idx[:], in0=ptb[:], scalar1=128, scalar2=None, op0=ALU.mult), reads=["ptb"], writes=["idx"])
        P.op("pool", lambda e: e.tensor_tensor(out=idx[:], in0=idx[:], in1=iot[:].to_broadcast([128, NPGS]), op=ALU.add), reads=["idx", "iot"], writes=["idx"])
        ptc = sb("ptc", [NPG, NS], I32, stack=s4)
        for b in range(NS):
            P.dma("sp", ptc[:, b:b + 1], pt_d[0:1, b * NPG:(b + 1) * NPG].rearrange("o j -> j o"), writes=["ptc"])
        kp = [sb("kp%d" % i, [128, D], stack=s4) for i in range(NRING)]
        vp = [sb("vp%d" % i, [128, D], BF16, stack=s4) for i in range(NRING)]
        qb = sb("qb", [128, D], stack=s4)
        scj = [sb("scj%d" % i, [128, H], stack=s4) for i in range(NRING)]
        Pj = [sb("Pj%d" % i, [128, H], BF16, stack=s4) for i in range(NRING)]
        Dt = sb("Dt", [128, H, NPG], stack=s4)
        lfp = sb("lfp", [NPG, 128, H], stack=s4)
        inc = sb("inc", [NPG, 128, H], stack=s4)
        Dp = sb("Dp", [NPG, 128, H], stack=s4)
        totp = sb("totp", [NPG, H], stack=s4)
        onesP = sb("onesP", [NPG, 128], stack=s4)
        P.op("dve", lambda e: e.memset(onesP[:], 1.0), writes=["onesP"])
        onesb = cbf[:, C_ONE:C_ONE + 1]
        msk = consts[0:H, C_BMK:C_BMK + D]
        omask = sb("omask", [H, D], stack=s4)
        ldiag = sb("ldiag", [H, H], stack=s4)
        lcol = sb("lcol", [H, 1], stack=s4)
        pgc = [0]
        for b in range(NS):
            Eb = consts[0:H, C_E + 4 * b:C_E + 4 * b + NS]
            for hf in range(2):
                P.op("pe", lambda e, hf=hf, b=b: e.matmul(ps[hf][:, :], lhsT=consts[0:NS, C_SELQ + 128 * b:C_SELQ + 128 * (b + 1)],
                                                          rhs=qs_tm[:, hf * 512:(hf + 1) * 512], start=True, stop=True),
                     reads=["qs_tm", "consts"], writes=[PK[hf]])
                P.op("dve", lambda e, hf=hf: e.tensor_copy(out=qb[:, hf * 512:(hf + 1) * 512], in_=ps[hf][:, :]), reads=[PK[hf]], writes=["qb"])
            P.dma("pool", lfp[:].rearrange("j s h -> j (s h)"), clf_d, reads=["ptc"], writes=["lfp"],
                  indirect=bass.IndirectOffsetOnAxis(ap=ptc[:, b:b + 1], axis=0))
            for h in range(H):
                P.op("dve", lambda e, h=h: e.tensor_tensor_scan(out=inc[:, :, h], data0=onesP[:], data1=lfp[:, :, h], initial=0.0, op0=ALU.mult, op1=ALU.add),
                     reads=["lfp", "onesP"], writes=["inc"])
            P.op("dve", lambda e: e.tensor_copy(out=totp[:], in_=inc[:, 127, :]), reads=["inc"], writes=["totp"])

            def ymm(e, b=b):
                e.matmul(ps[2][0:NPG, 0:H], lhsT=consts[0:NPG, C_SU:C_SU + NPG], rhs=totp[:], start=True, stop=False)
                return e.matmul(ps[2][0:NPG, 0:H], lhsT=consts[0:NS, C_SELQ + 128 * b:C_SELQ + 128 * b + NPG], rhs=lfs_tm[:], start=False, stop=True)
            P.op("pe", ymm, reads=["totp", "lfs_tm", "consts"], writes=[PK[2]])
            P.op("dve", lambda e: e.tensor_tensor(out=Dp[:], in0=totp[:].unsqueeze(1).to_broadcast([NPG, 128, H]), in1=inc[:], op=ALU.subtract),
                 reads=["totp", "inc"], writes=["Dp"])
            P.op("dve", lambda e: e.tensor_copy(out=totp[:], in_=ps[2][0:NPG, 0:H]), reads=[PK[2]], writes=["totp"])
            P.op("dve", lambda e: e.tensor_tensor(out=Dp[:], in0=Dp[:], in1=totp[:].unsqueeze(1).to_broadcast([NPG, 128, H]), op=ALU.add),
                 reads=["Dp", "totp"], writes=["Dp"])
            nb_ = (H * NPG + 511) // 512
            hpb = 512 // NPG

            def dtr(e):
                ins = None
                for h in range(H):
                    ins = e.transpose(ps[3 + h // hpb][:, (h % hpb) * NPG:(h % hpb + 1) * NPG], Dp[:, :, h], ident_f[0:NPG, 0:NPG])
                return ins
            P.op("pe", dtr, reads=["Dp", "consts"], writes=[PK[3 + i] for i in range(nb_)])
            for i in range(nb_):
                w_ = min(512, H * NPG - i * 512)
                P.op("dve", lambda e, i=i, w_=w_: e.tensor_copy(out=Dt[:].rearrange("p h j -> p (h j)")[:, i * 512:i * 512 + w_], in_=ps[3 + i][:, 0:w_]),
                     reads=[PK[3 + i]], writes=["Dt"])
            for j in range(NPG):
                sl = pgc[0] % NRING
                pgc[0] += 1
                col = b * NPG + j
                P.dma("pool", kp[sl][:], ck_d, reads=["idx"], writes=[("kp", sl)], indirect=bass.IndirectOffsetOnAxis(ap=idx[:, col:col + 1], axis=0))
                P.dma("pool", vp[sl][:], cv_d, reads=["idx"], writes=[("vp", sl)], indirect=bass.IndirectOffsetOnAxis(ap=idx[:, col:col + 1], axis=0))
                P.op("dve", lambda e, sl=sl: e.tensor_tensor(out=kp[sl][:], in0=kp[sl][:], in1=qb[:], op=ALU.mult), reads=[("kp", sl), "qb"], writes=[("kp", sl)])
                P.op("dve", lambda e, sl=sl: e.tensor_reduce(out=scj[sl][:], in_=kp[sl][:].rearrange("p (h d) -> p h d", d=DH), axis=AX.X, op=ALU.add),
                     reads=[("kp", sl)], writes=[("scj", sl)])
                P.op("dve", lambda e, sl=sl, j=j: e.tensor_tensor(out=scj[sl][:], in0=scj[sl][:], in1=Dt[:, :, j], op=ALU.add), reads=[("scj", sl), "Dt"], writes=[("scj", sl)])
                P.op("act", lambda e, sl=sl: e.activation(out=Pj[sl][:], in_=scj[sl][:], func=AF.Exp), reads=[("scj", sl)], writes=[("Pj", sl)])

                def pvm(e, sl=sl, j=j):
                    e.matmul(ps[0][0:H, :], lhsT=Pj[sl][:], rhs=vp[sl][:, 0:512], start=(j == 0), stop=(j == NPG - 1))
                    e.matmul(ps[1][0:H, :], lhsT=Pj[sl][:], rhs=vp[sl][:, 512:1024], start=(j == 0), stop=(j == NPG - 1))
                    return e.matmul(ps[2][0:H, 0:1], lhsT=Pj[sl][:], rhs=onesb, start=(j == 0), stop=(j == NPG - 1))
                P.op("pe", pvm, reads=[("Pj", sl), ("vp", sl), "cbf"], writes=[PK[0], PK[1], PK[2]])
            P.op("dve", lambda e: e.tensor_tensor(out=omask[:, 0:512], in0=ps[0][0:H, :], in1=msk[:, 0:512], op=ALU.mult), reads=[PK[0], "consts"], writes=["omask"])
            P.op("dve", lambda e: e.tensor_tensor(out=omask[:, 512:1024], in0=ps[1][0:H, :], in1=msk[:, 512:1024], op=ALU.mult), reads=[PK[1], "consts"], writes=["omask"])
            P.op("dve", lambda e: e.tensor_copy(out=lcol[:], in_=ps[2][0:H, 0:1]), reads=[PK[2]], writes=["lcol"])
            P.op("dve", lambda e: e.tensor_scalar(out=ldiag[:], in0=ident_f[0:H, 0:H], scalar1=lcol[:, 0:1], scalar2=None, op0=ALU.mult),
                 reads=["lcol", "consts"], writes=["ldiag"])

            def colm(e, b=b, Eb=Eb):
                e.matmul(ps[5][0:NS, :], lhsT=Eb, rhs=omask[:, 0:512], start=(b == 0), stop=(b == NS - 1))
                e.matmul(ps[6][0:NS, :], lhsT=Eb, rhs=omask[:, 512:1024], start=(b == 0), stop=(b == NS - 1))
                return e.matmul(ps[7][0:NS, 0:H], lhsT=Eb, rhs=ldiag[:], start=(b == 0), stop=(b == NS - 1))
            P.op("pe", colm, reads=["omask", "ldiag", "consts"], writes=[PK[5], PK[6], PK[7]])
        tq = sb("tq", [NS, D], stack=s4)
        scur = sb("scur", [NS, H], stack=s4)
        lt = sb("lt", [NS, H], stack=s4)
        P.op("dve", lambda e: e.tensor_tensor(out=tq[:], in0=qs_tm[:], in1=ks_tm[:], op=ALU.mult), reads=["qs_tm", "ks_tm"], writes=["tq"])
        P.op("dve", lambda e: e.tensor_reduce(out=scur[:], in_=tq[:].rearrange("p (h d) -> p h d", d=DH), axis=AX.X, op=ALU.add), reads=["tq"], writes=["scur"])
        P.op("act", lambda e: e.activation(out=scur[:], in_=scur[:], func=AF.Exp), reads=["scur"], writes=["scur"])
        P.op("dve", lambda e: e.tensor_tensor(out=lt[:], in0=ps[7][0:NS, 0:H], in1=scur[:], op=ALU.add), reads=[PK[7], "scur"], writes=["lt"])
        P.op("dve", lambda e: e.reciprocal(out=lt[:], in_=lt[:]), reads=["lt"], writes=["lt"])
        P.op("dve", lambda e: e.tensor_tensor(out=tq[:].rearrange("p (h d) -> p h d", d=DH), in0=vs_tm[:].rearrange("p (h d) -> p h d", d=DH),
                                              in1=scur[:].unsqueeze(2).to_broadcast([NS, H, DH]), op=ALU.mult), reads=["vs_tm", "scur"], writes=["tq"])
        P.op("dve", lambda e: e.tensor_tensor(out=tq[:, 0:512], in0=tq[:, 0:512], in1=ps[5][0:NS, :], op=ALU.add), reads=["tq", PK[5]], writes=["tq"])
        P.op("dve", lambda e: e.tensor_tensor(out=tq[:, 512:1024], in0=tq[:, 512:1024], in1=ps[6][0:NS, :], op=ALU.add), reads=["tq", PK[6]], writes=["tq"])
        P.op("dve", lambda e: e.tensor_tensor(out=os_tm[:].rearrange("p (h d) -> p h d", d=DH), in0=tq[:].rearrange("p (h d) -> p h d", d=DH),
                                              in1=lt[:].unsqueeze(2).to_broadcast([NS, H, DH]), op=ALU.mult), reads=["tq", "lt"], writes=["os_tm"])
        P.dma("sp", ks_o, ks_tm[:], reads=["ks_tm"], lane="ks_out", final=True)
        P.dma("sp", vs_o, vs_tm[:], reads=["vs_tm"], lane="vs_out", final=True)
        osb = sb("osb", [NS, D], BF16, stack=s4)
        P.op("dve", lambda e: e.tensor_copy(out=osb[:], in_=os_tm[:]), reads=["os_tm"], writes=["osb"])
        pvs = to_fm(osb, "osb", NS, 3, ident_b)
        P.op("dve", lambda e: e.tensor_copy(out=osT[:].rearrange("p k n -> p (k n)"), in_=pvs[:, 0:KC * NS]), reads=[PK[3]], writes=["osT"])
    P.fence()
    dbg("os_tm", os_tm[:], "os_tm")
    if cfg.stop == 4:
        P.finish()
        return nc, P, es, dbg_outs

    with ExitStack() as s5:
        w3 = [sb("w3a%d" % i, [128, KC, 512], BF16, stack=s5) for i in range(3)]
        sga = [sb("sga%d" % i, [128, 512], stack=s5) for i in range(2)]
        sgb = [sb("sgb%d" % i, [128, 512], stack=s5) for i in range(2)]
        oTt = sb("oTt", [128, KC, 512], BF16, stack=s5)
        yaTt = sb("yaTt", [128, KC, 512], BF16, stack=s5)
        mstg = [sb("mstg%d" % i, [128, 512], BF16, stack=s5) for i in range(2)]
        c3 = [0]
        for c in range(KC):
            w = w3[c % 3]
            wk = ("w3a", c % 3)
            wdma(w[:, :, 0:128], w_in, GA + c * 128, 128, wk)
            wdma(w[:, :, 128:256], w_in, GB + c * 128, 128, wk)
            wdma(w[:, :, 256:384], w_pa, c * 128, 128, wk)
            wdma(w[:, :, 384:512], w_pb, c * 128, 128, wk)
            for t in range(NT_OWN + 1):
                sample = (t == NT_OWN)
                n = NS if sample else 512
                sl = c3[0] % 2
                c3[0] += 1
                if sample:
                    srcs = [(lambda kc: hsT[:, kc, :], ["hsT"]), (lambda kc: hsT[:, kc, :], ["hsT"]),
                            (lambda kc: yainsT[:, kc, :], ["yainsT"]), (lambda kc: osT[:, kc, :], ["osT"])]
                    dst, dkey = mergedsT[:, c, :], "mergedsT"
                else:
                    tb = NT_PRE + t
                    hk = [("hT", b) for b in range(4 * tb, 4 * tb + 4)]
                    P.dma("sp", oTt[:], oT_d[:, :, t * 512:(t + 1) * 512], reads=["oT_d"], writes=["oTt"])
                    P.dma("sp", yaTt[:], ya_d[:, :, t * 512:(t + 1) * 512], reads=["ya_d"], writes=["yaTt"])
                    srcs = [(lambda kc, tb=tb: hT[:, kc, tb * 512:(tb + 1) * 512], hk), (lambda kc, tb=tb: hT[:, kc, tb * 512:(tb + 1) * 512], hk),
                            (lambda kc: yaTt[:, kc, :], ["yaTt"]),
                            (lambda kc: oTt[:, kc, :], ["oTt"])]
                    dst, dkey = mstg[sl][:, :], ("mstg", sl)
                for gi, (src, skeys) in enumerate(srcs):
                    def mm(e, gi=gi, src=src):
                        ins = None
                        for kc in range(KC):
                            ins = e.matmul(ps[gi][:, 0:n], lhsT=w[:, kc, gi * 128:(gi + 1) * 128], rhs=src(kc), start=(kc == 0), stop=(kc == KC - 1))
                        return ins
                    P.op("pe", mm, reads=skeys + [wk], writes=[PK[gi]])
                P.op("act", lambda e: e.activation(out=sga[sl][:, 0:n], in_=ps[0][:, 0:n], func=AF.Sigmoid), reads=[PK[0]], writes=[("sga", sl)])
                P.op("act", lambda e: e.activation(out=sgb[sl][:, 0:n], in_=ps[1][:, 0:n], func=AF.Sigmoid), reads=[PK[1]], writes=[("sgb", sl)])
                P.op("dve", lambda e: e.tensor_tensor(out=sga[sl][:, 0:n], in0=ps[2][:, 0:n], in1=sga[sl][:, 0:n], op=ALU.mult), reads=[PK[2], ("sga", sl)], writes=[("sga", sl)])
                P.op("dve", lambda e: e.tensor_tensor(out=sgb[sl][:, 0:n], in0=ps[3][:, 0:n], in1=sgb[sl][:, 0:n], op=ALU.mult), reads=[PK[3], ("sgb", sl)], writes=[("sgb", sl)])
                P.op("dve", lambda e, dst=dst: e.tensor_tensor(out=dst, in0=sga[sl][:, 0:n], in1=sgb[sl][:, 0:n], op=ALU.add), reads=[("sga", sl), ("sgb", sl)], writes=[dkey])
                if not sample:
                    P.dma("sp", mg_d[:, c, t * 512:(t + 1) * 512], mstg[sl][:, :], reads=[("mstg", sl)], writes=["mg_d"], lane=("mstg", sl))
    sH.close()
    P.fence()
    if cfg.stop == 5:
        P.finish()
        return nc, P, es, dbg_outs

    NCH = cfg.DFF // 128
    with ExitStack() as s6:
        wo_t = sb("wo_t", [128, KC, D], BF16, stack=s6)
        for hf in range(2):
            wdma(wo_t[:, :, hf * 512:(hf + 1) * 512], w_o, hf * 512, 512, "wo_t")
        wfo_t = sb("wfo_t", [128, NCH, D], BF16, stack=s6)
        for c0 in range(0, NCH, 6):
            c1 = min(NCH, c0 + 6)
            P.dma("pool", wfo_t[:, c0:c1, :], w_fo[c0 * 128:c1 * 128, :].rearrange("(c p) n -> p c n", p=128), writes=["wfo_t"])
        wf = [sb("wf%d" % i, [128, KC, 256], BF16, stack=s6) for i in range(4)]
        x3 = sb("x3", [128, 4, D], stack=s6)
        h2T = sb("h2T", [128, KC, 512], BF16, stack=s6)
        actT = sb("actT", [128, NCH, 512], BF16, stack=s6)
        sg = [sb("sg%d" % i, [128, 512], stack=s6) for i in range(2)]
        tmp3 = [sb("tmp3%d" % i, [128, 512], stack=s6) for i in range(2)]
        xn3 = [sb("xn3%d" % i, [128, D], BF16, stack=s6) for i in range(2)]
        junk3 = sb("junk3", [128, D], BF16, stack=s6)
        st3 = sb("st3", [128, 4], stack=s6)
        yst = [sb("yst%d" % i, [128, D], stack=s6) for i in range(1)]
        tmp_s3 = sb("tmp_s3", [128, KC, NS], stack=s6)
        mTt = sb("mTt", [128, KC, 512], BF16, stack=s6)
        cc = [0]
        for t in range(NT_OWN + 1):
            sample = (t == NT_OWN)
            nblk = 1 if sample else 4
            nt = NS if sample else 128
            ntok = NS if sample else 512
            mT = mergedsT if sample else mTt
            mkeys = ["mergedsT"] if sample else ["mTt"]
            if not sample:
                P.dma("sp", mTt[:], mg_d[:, :, t * 512:(t + 1) * 512], reads=["mg_d"], writes=["mTt"])
            g1t, g1k = (g1s, "g1s") if sample else (g1b, "g1b")
            g2t, g2k = (g2s, "g2s") if sample else (g2b, "g2b")
            if sample:
                P.dma("sp", x3[0:NS, 0, :], xs_d, writes=["x3"])
            else:
                P.dma("sp", x3[:], xa[S_PRE + t * 512:S_PRE + (t + 1) * 512, :].rearrange("(b p) d -> p b d", p=128), writes=["x3"])

            def tok_cols(blk):
                return slice(0, NS) if sample else slice(blk * 128, (blk + 1) * 128)

            def resid(blk, hf, lhs_of, nk, rhs_of, rkeys, gt, gk):
                bank = cc[0] % 2
                cc[0] += 1

                def mm(e):
                    ins = None
                    for kc in range(nk):
                        ins = e.matmul(ps[bank][0:nt, :], lhsT=lhs_of(kc), rhs=rhs_of(kc, hf), start=(kc == 0), stop=(kc == nk - 1))
                    return ins
                P.op("pe", mm, reads=rkeys, writes=[PK[bank]])
                P.op("dve", lambda e: e.tensor_tensor(out=tmp3[bank][0:nt, :], in0=ps[bank][0:nt, :], in1=gt[0:nt, hf * 512:(hf + 1) * 512], op=ALU.mult),
                     reads=[PK[bank], gk], writes=[("tmp3", bank)])
                P.op("dve", lambda e: e.tensor_tensor(out=x3[0:nt, blk, hf * 512:(hf + 1) * 512], in0=x3[0:nt, blk, hf * 512:(hf + 1) * 512], in1=tmp3[bank][0:nt, :], op=ALU.add),
                     reads=[("tmp3", bank), "x3"], writes=["x3"])
            for blk in range(nblk):
                for hf in range(2):
                    resid(blk, hf, lambda kc, blk=blk: mT[:, kc, tok_cols(blk)], KC, lambda kc, hf: wo_t[:, kc, hf * 512:(hf + 1) * 512],
                          mkeys + ["wo_t"], g1t, g1k)
            for blk in range(nblk):
                sl = blk % 2
                P.op("dve", lambda e: e.memset(st3[:, 0:2], 0.0), writes=["st3"])
                norm_block(s6, x3[0:nt, blk, :], "x3", nt, st3[:, 0:2], "st3", xn3[sl], ("xn3", sl), junk3, 999)
                pvv = to_fm(xn3[sl], ("xn3", sl), nt, 2 + sl, ident_b)
                if sample:
                    P.op("dve", lambda e, pvv=pvv: e.tensor_tensor(out=tmp_s3[:], in0=pvv[:, 0:KC * NS].rearrange("p (k n) -> p k n", n=NS), in1=ABs[:, 2, :, :], op=ALU.mult),
                         reads=[PK[2 + sl], "ABs"], writes=["tmp_s3"])
                    P.op("dve", lambda e: e.tensor_tensor(out=h2T[:, :, 0:NS], in0=tmp_s3[:], in1=ABs[:, 3, :, :], op=ALU.add), reads=["tmp_s3", "ABs"], writes=["h2T"])
                else:
                    for kc in range(KC):
                        P.op("dve", lambda e, kc=kc, pvv=pvv, blk=blk: e.tensor_scalar(
                            out=h2T[:, kc, blk * 128:(blk + 1) * 128], in0=pvv[:, kc * 128:(kc + 1) * 128],
                            scalar1=AB[:, 2, kc:kc + 1], scalar2=AB[:, 3, kc:kc + 1], op0=ALU.mult, op1=ALU.add),
                            reads=[PK[2 + sl], "AB"], writes=["h2T"])
            for c in range(NCH):
                w = wf[c % 4]
                wk = ("wf", c % 4)
                wdma(w[:, :, 0:128], w_fi, c * 128, 128, wk)
                wdma(w[:, :, 128:256], w_fi, cfg.DFF + c * 128, 128, wk)
                pb = 4 + 2 * (c % 2)
                for gi in range(2):
                    def mm(e, gi=gi, pb=pb):
                        ins = None
                        for kc in range(KC):
                            ins = e.matmul(ps[pb + gi][:, 0:ntok], lhsT=w[:, kc, gi * 128:(gi + 1) * 128], rhs=h2T[:, kc, 0:ntok], start=(kc == 0), stop=(kc == KC - 1))
                        return ins
                    P.op("pe", mm, reads=["h2T", wk], writes=[PK[pb + gi]])
                P.op("act", lambda e, pb=pb, c=c: e.activation(out=sg[c % 2][:, 0:ntok], in_=ps[pb][:, 0:ntok], func=AF.Silu), reads=[PK[pb]], writes=[("sg", c % 2)])
                P.op("dve", lambda e, pb=pb, c=c: e.tensor_tensor(out=actT[:, c, 0:ntok], in0=ps[pb + 1][:, 0:ntok], in1=sg[c % 2][:, 0:ntok], op=ALU.mult),
                     reads=[PK[pb + 1], ("sg", c % 2)], writes=["actT"])
            for blk in range(nblk):
                for hf in range(2):
                    resid(blk, hf, lambda kc, blk=blk: actT[:, kc, (slice(0, NS) if sample else slice(blk * 128, (blk + 1) * 128))], NCH,
                          lambda kc, hf: wfo_t[:, kc, hf * 512:(hf + 1) * 512], ["actT", "wfo_t"], g2t, g2k)
            for blk in range(nblk):
                sl = 0
                P.op("dve", lambda e: e.memset(st3[:, 2:4], 0.0), writes=["st3b"])
                P.op("act", lambda e, blk=blk: e.activation(out=junk3[0:nt, :], in_=x3[0:nt, blk, :], func=AF.Square, accum_out=st3[0:nt, 2:3]), reads=["x3", "st3b"], writes=["st3b"])
                P.op("act", lambda e: e.activation(out=st3[0:nt, 3:4], in_=st3[0:nt, 2:3], func=AF.Ln, scale=1.0 / D, bias=EPS), reads=["st3b"], writes=["st3b"])
                P.op("act", lambda e: e.activation(out=st3[0:nt, 3:4], in_=st3[0:nt, 3:4], func=AF.Exp, scale=-0.5), reads=["st3b"], writes=["st3b"])
                P.op("act", lambda e, blk=blk, sl=sl: e.activation(out=yst[sl][0:nt, :], in_=x3[0:nt, blk, :], func=AF.Copy, scale=st3[0:nt, 3:4]),
                     reads=["x3", "st3b"], writes=[("yst", sl)])
                P.op("dve", lambda e, sl=sl: e.tensor_tensor(out=yst[sl][0:nt, :], in0=yst[sl][0:nt, :], in1=normf_b[0:nt, :], op=ALU.mult),
                     reads=[("yst", sl), "normf_b"], writes=[("yst", sl)])
                if sample:
                    P.dma("sp", ys_o, yst[sl][0:NS, :], reads=[("yst", sl)], lane=("yst", sl), final=True)
                else:
                    r0 = t * 512 + blk * 128
                    P.dma("sp", y_o[r0:r0 + 128, :], yst[sl][:, :], reads=[("yst", sl)], lane=("yst", sl), final=True)
    P.fence()
    P.finish()
    return nc, P, es, dbg_outs
    return nc, P, es, dbg_outs


def prep_inputs(cfg, inp):
    f = lambda a: np.ascontiguousarray(np.asarray(a))
    D = cfg.D
    x_prompt = f(inp["x_prompt"])
    B, SEQ, _ = x_prompt.shape
    assert SEQ == 2 * cfg.S_OWN
    n_cores = 2 * B
    consts = make_consts()
    vecs = np.concatenate([
        f(inp["ada_b"]).reshape(48, 128), f(inp["norm1_g"]).reshape(8, 128), f(inp["norm2_g"]).reshape(8, 128),
        f(inp["conv_w"]).reshape(32, 128), f(inp["conv_b"]).reshape(8, 128), f(inp["rg_br"]).reshape(8, 128),
        f(inp["rg_bi"]).reshape(8, 128), f(inp["rg_lambda"]).reshape(8, 128)], axis=0).astype(np.float32)
    shared = {
        "ck": f(inp["cache_k"]).reshape(cfg.NPOOL * 128, D),
        "cv": f(inp["cache_v"]).reshape(cfg.NPOOL * 128, D),
        "clf": f(inp["cache_logf"]).reshape(cfg.NPOOL, 128 * cfg.H),
        "consts": consts,
        "ada_w": f(inp["ada_w"]).reshape(D, 6 * D),
        "ada_b": f(inp["ada_b"]).reshape(1, 6 * D),
        "vecs": vecs,
        "normf_g": f(inp["normf_g"]).reshape(1, D),
        "w_in": f(inp["w_in"]).reshape(D, cfg.NIN),
        "b_f": f(inp["b_f"]).reshape(1, cfg.H),
        "rg_wr": f(inp["rg_wr"]).reshape(16, 64, 64),
        "rg_wi": f(inp["rg_wi"]).reshape(16, 64, 64),
        "w_proj_a": f(inp["w_proj_a"]).reshape(D, D),
        "w_proj_b": f(inp["w_proj_b"]).reshape(D, D),
        "w_o": f(inp["w_o"]).reshape(D, D),
        "w_ffn_in": f(inp["w_ffn_in"]).reshape(D, 2 * cfg.DFF),
        "w_ffn_out": f(inp["w_ffn_out"]).reshape(cfg.DFF, D),
    }
    c_prompt = f(inp["c_prompt"]); c_sample = f(inp["c_sample"])
    x_sample = f(inp["x_sample"]).reshape(-1, D)
    sconv = f(inp["state_conv"])[0]; srg = f(inp["state_rglru"])[0]
    pt = f(inp["page_table"]).astype(np.int32)
    maps = []
    for c in range(n_cores):
        b, half = c // 2, c % 2
        xa = np.zeros((cfg.S_ALL, D), np.float32)
        if half == 0:
            xa[cfg.S_PRE:] = x_prompt[b, :cfg.S_OWN]
        else:
            xa[:] = x_prompt[b]
        flags = np.zeros((128, 2), np.float32)
        flags[:, 0] = 1.0 if half == 1 else 0.0
        flags[:, 1] = 0.0 if half == 1 else NEG
        crow = np.zeros((33, D), np.float32)
        crow[0:4] = c_sample[4 * c:4 * c + 4]
        crow[32] = c_prompt[b]
        m = dict(shared)
        m.update({
            "xa": xa, "flags": flags, "crow": crow,
            "xs": np.ascontiguousarray(x_sample[4 * c:4 * c + 4]),
            "sconv": np.ascontiguousarray(sconv[4 * c:4 * c + 4]),
            "srg": np.ascontiguousarray(srg[4 * c:4 * c + 4]),
            "pt": np.ascontiguousarray(pt[4 * c:4 * c + 4].reshape(1, -1)),
        })
        maps.append(m)
    return maps


def run(cfg, inp):
    nc, P, es, dbg_outs = build_program(cfg)
    maps = prep_inputs(cfg, inp)
    used = set()
    for alloc in nc.allocations:
        if isinstance(alloc, mybir.MemoryLocationSet) and alloc.kind == "ExternalInput":
            used.add(alloc.memorylocations[0].name)
    maps = [{k: v for k, v in m.items() if k in used} for m in maps]
    res = run_bass_kernel_spmd(nc, maps, core_ids=list(range(len(maps))))
    es.close()
    return res.results


def assemble(cfg, res):
    n_cores = len(res)
    B = n_cores // 2
    S = cfg.S_OWN
    D, H, DH = cfg.D, cfg.H, cfg.DH
    cat = lambda name: np.concatenate([res[c][name] for c in range(n_cores)], axis=0)
    y = cat("y").reshape(B, 2 * S, D)
    k = cat("ko").reshape(1, B, 2 * S, H, DH)
    v = cat("vo").reshape(1, B, 2 * S, H, DH)
    lf = cat("lfo").reshape(1, B, 2 * S, H)
    conv = np.stack([res[2 * b + 1]["convo"] for b in range(B)])[None]
    rg = np.stack([res[2 * b + 1]["rgo"][0] for b in range(B)])[None]
    ys = cat("yso").reshape(-1, 1, D)
    ks = cat("kso").reshape(1, -1, 1, H, DH)
    vs = cat("vso").reshape(1, -1, 1, H, DH)
    lfs = cat("lfso").reshape(1, -1, 1, H)
    convs = cat("convso").reshape(1, -1, 3, D)
    rgs = cat("rgso").reshape(1, -1, D)
    return (y, ys, k, v, lf, conv, rg, ks, vs, lfs, convs, rgs)


def kernel(**inputs):
    cfg = Cfg()
    res = run(cfg, inputs)
    return tuple(np.ascontiguousarray(a, dtype=np.float32) for a in assemble(cfg, res))
```
